# Optimizing a Trainium2 kernel written in Bass

```python
import jax, jax.numpy as jnp
from jax import lax
import numpy as np

D_MODEL = 1024
BATCH = 32
SEQ = 256
DEPTH = 4
DEC_BATCH = 2
DEC_SEQ = 4096
PAST_LEN = 512

GRID_W = 64
N_MIXERS = 4
N_FNET = (DEPTH + 3) // 4
N_ATTN = (DEPTH + 2) // 4
N_POOL = (DEPTH + 1) // 4
N_GMLP = DEPTH // 4
N_GROUPS = 4
GROUP_W = D_MODEL // N_GROUPS
HEAD_DIM = 128
N_Q_HEADS = D_MODEL // HEAD_DIM
N_KV_HEADS = 2
Q_PER_KV = N_Q_HEADS // N_KV_HEADS
Q_W = N_Q_HEADS * HEAD_DIM
KV_W = N_KV_HEADS * HEAD_DIM
QKV_W = Q_W + 2 * KV_W
Q_BLOCK = 128
ROPE_THETA = 10000.0
POOL_HALF = (1, 2, 4, 8)
CHUNK = 128
GMLP_W = 2 * D_MODEL
GMLP_GROUPS = 4
GMLP_GW = GMLP_W // GMLP_GROUPS
D_FF = ((8 * D_MODEL // 3 + 127) // 128) * 128
ALPHA = (2 * DEPTH) ** 0.25
BETA = (8 * DEPTH) ** -0.25
LN_EPS = 1e-6

kernel_name = "hybrid_diffusion_prefix_trunk_step"

F32 = jnp.float32


def _layer_norm(x, g=None, b=None):
    xf = x.astype(F32)
    mu = jnp.mean(xf, axis=-1, keepdims=True)
    var = jnp.mean(jnp.square(xf - mu), axis=-1, keepdims=True)
    y = (xf - mu) * lax.rsqrt(var + LN_EPS)
    if g is not None:
        y = y * g.astype(F32) + b.astype(F32)
    return y.astype(x.dtype)


def _rms_norm(x, g):
    xf = x.astype(F32)
    y = xf * lax.rsqrt(jnp.mean(jnp.square(xf), axis=-1, keepdims=True) + LN_EPS)
    return (y * g.astype(F32)).astype(x.dtype)


def _axial_rope(x):
    T = x.shape[1]
    rows = T // GRID_W
    row = jnp.repeat(jnp.arange(rows), GRID_W).astype(F32)
    col = jnp.tile(jnp.arange(GRID_W), rows).astype(F32)
    half = HEAD_DIM // 2
    quarter = half // 2
    inv = ROPE_THETA ** (-jnp.arange(quarter, dtype=F32) / quarter)

    def rot(xa, pos):
        ang = pos[:, None] * inv[None, :]
        cos = jnp.cos(ang)[None, :, None, :].astype(x.dtype)
        sin = jnp.sin(ang)[None, :, None, :].astype(x.dtype)
        x1, x2 = xa[..., :quarter], xa[..., quarter:]
        return jnp.concatenate([x1 * cos - x2 * sin, x2 * cos + x1 * sin], axis=-1)

    return jnp.concatenate([rot(x[..., :half], row), rot(x[..., half:], col)], axis=-1)


def _qkv(h, w_qkv, q_g, k_g):
    B, T, _ = h.shape
    qkv = h @ w_qkv
    q = qkv[..., :Q_W].reshape(B, T, N_Q_HEADS, HEAD_DIM)
    k = qkv[..., Q_W:Q_W + KV_W].reshape(B, T, N_KV_HEADS, HEAD_DIM)
    v = qkv[..., Q_W + KV_W:].reshape(B, T, N_KV_HEADS, HEAD_DIM)
    return _rms_norm(q, q_g), _rms_norm(k, k_g), v


def _attend(q, k, v):
    B, T = q.shape[:2]
    nb = T // Q_BLOCK
    qb = q.reshape(B, nb, Q_BLOCK, N_KV_HEADS, Q_PER_KV, HEAD_DIM).transpose(1, 0, 2, 3, 4, 5)
    scale = HEAD_DIM ** -0.5

    def block(qblk):
        s = jnp.einsum('bqkgd,bskd->bkgqs', qblk, k).astype(F32) * scale
        p = jax.nn.softmax(s, axis=-1).astype(v.dtype)
        return jnp.einsum('bkgqs,bskd->bqkgd', p, v)

    o = lax.map(block, qb)
    return o.transpose(1, 0, 2, 3, 4, 5).reshape(B, T, Q_W)


def _fourier_mix(h, w_o):
    B, T, D = h.shape
    hg = h.astype(F32).reshape(B, T, N_GROUPS, GROUP_W)
    f = jnp.fft.fft2(hg, axes=(1, 3), norm='ortho').real
    return f.astype(h.dtype).reshape(B, T, D) @ w_o


def _pool_mix(h, w_grp, scale):
    B, T, D = h.shape
    hg = h.reshape(B, T, N_GROUPS, GROUP_W)
    csum = jnp.concatenate([jnp.zeros((B, 1, N_GROUPS, GROUP_W), F32),
                            jnp.cumsum(hg.astype(F32), axis=1)], axis=1)
    half = jnp.array(POOL_HALF, jnp.int32)[None, :]
    t = jnp.arange(T, dtype=jnp.int32)[:, None]
    lo = jnp.clip(t - half, 0, T)
    hi = jnp.clip(t + half, 0, T)
    g = jnp.arange(N_GROUPS)[None, :]
    wsum = csum[:, hi, g] - csum[:, lo, g]
    mean = wsum / (hi - lo).astype(F32)[None, :, :, None]
    p = (mean - hg.astype(F32)).astype(h.dtype)
    y = jnp.einsum('btgc,gcd->btgd', p, w_grp).reshape(B, T, D)
    return y * scale


def _gmlp_mix(h, w_in, b_in, ln_g, ln_b, w_s, b_s, w_o):
    B, T, _ = h.shape
    z = jax.nn.gelu(h @ w_in + b_in)
    u, v = jnp.split(z, 2, axis=-1)
    v = _layer_norm(v, ln_g, ln_b)
    vc = v.reshape(B, T // CHUNK, CHUNK, GMLP_GROUPS, GMLP_GW)
    s = jnp.einsum('bnpgc,gqp->bnqgc', vc, w_s) + b_s.T[None, None, :, :, None]
    return (u * s.reshape(B, T, GMLP_W)) @ w_o


def _conv_ffn(h, w_up, conv_w, conv_b, w_down):
    a, b = jnp.split(h @ w_up, 2, axis=-1)
    ap = jnp.pad(a, ((0, 0), (1, 1), (0, 0)))
    a = ap[:, :-2] * conv_w[0] + ap[:, 1:-1] * conv_w[1] + ap[:, 2:] * conv_w[2] + conv_b
    return (jax.nn.gelu(a) * b) @ w_down


def setup_inputs(seed: int = 0) -> dict:
    key = jax.random.key(seed)
    ks = jax.random.split(key, 32)
    D = D_MODEL
    nrm = lambda k, s, sc: jax.random.normal(k, s, F32) * sc
    return {
        'x_prompt': nrm(ks[0], (BATCH, SEQ, D), 1.0),
        'x_sample': nrm(ks[1], (DEC_BATCH, DEC_SEQ, D), 1.0),
        'cache_k': nrm(ks[2], (DEC_BATCH, N_ATTN, PAST_LEN, N_KV_HEADS, HEAD_DIM), 1.0),
        'cache_v': nrm(ks[3], (DEC_BATCH, N_ATTN, PAST_LEN, N_KV_HEADS, HEAD_DIM), 1.0),
        'c': nrm(ks[4], (DEC_BATCH, D), 1.0),
        'c_ctx': nrm(ks[5], (D,), 1.0),
        'w_mod': nrm(ks[6], (DEPTH, D, 6 * D), 0.5 * D ** -0.5),
        'b_mod': nrm(ks[7], (DEPTH, 6 * D), 0.02),
        'ln_g': 1.0 + nrm(ks[8], (DEPTH, 2, D), 0.02),
        'ln_b': nrm(ks[9], (DEPTH, 2, D), 0.02),
        'ffn_w_up': nrm(ks[10], (DEPTH, D, 2 * D_FF), D ** -0.5),
        'ffn_conv_w': nrm(ks[11], (DEPTH, 3, D_FF), 3 ** -0.5),
        'ffn_conv_b': nrm(ks[12], (DEPTH, D_FF), 0.02),
        'ffn_w_down': nrm(ks[13], (DEPTH, D_FF, D), BETA * D_FF ** -0.5),
        'fnet_w_o': nrm(ks[14], (N_FNET, D, D), BETA * D ** -0.5),
        'attn_w_qkv': nrm(ks[15], (N_ATTN, D, QKV_W), D ** -0.5),
        'attn_q_norm': 1.0 + nrm(ks[16], (N_ATTN, HEAD_DIM), 0.02),
        'attn_k_norm': 1.0 + nrm(ks[17], (N_ATTN, HEAD_DIM), 0.02),
        'attn_w_o': nrm(ks[18], (N_ATTN, Q_W, D), BETA * Q_W ** -0.5),
        'pool_w': nrm(ks[19], (N_POOL, N_GROUPS, GROUP_W, GROUP_W), BETA * GROUP_W ** -0.5),
        'pool_scale': 1.0 + nrm(ks[20], (N_POOL, D), 0.1),
        'gmlp_w_in': nrm(ks[21], (N_GMLP, D, 2 * GMLP_W), D ** -0.5),
        'gmlp_b_in': nrm(ks[22], (N_GMLP, 2 * GMLP_W), 0.02),
        'gmlp_ln_g': 1.0 + nrm(ks[23], (N_GMLP, GMLP_W), 0.02),
        'gmlp_ln_b': nrm(ks[24], (N_GMLP, GMLP_W), 0.02),
        'gmlp_w_s': nrm(ks[25], (N_GMLP, GMLP_GROUPS, CHUNK, CHUNK), 0.5 * CHUNK ** -0.5),
        'gmlp_b_s': 1.0 + nrm(ks[26], (N_GMLP, GMLP_GROUPS, CHUNK), 0.02),
        'gmlp_w_o': nrm(ks[27], (N_GMLP, GMLP_W, D), BETA * GMLP_W ** -0.5),
    }


def reference(x_prompt, x_sample, cache_k, cache_v, c, c_ctx, w_mod, b_mod, ln_g, ln_b,
              ffn_w_up, ffn_conv_w, ffn_conv_b, ffn_w_down, fnet_w_o, attn_w_qkv,
              attn_q_norm, attn_k_norm, attn_w_o, pool_w, pool_scale, gmlp_w_in, gmlp_b_in,
              gmlp_ln_g, gmlp_ln_b, gmlp_w_s, gmlp_b_s, gmlp_w_o):
    x_p, x_s = x_prompt, x_sample
    cond_p = jax.nn.silu(c_ctx)[None, :]
    cond_s = jax.nn.silu(c)
    new_ks, new_vs = [], []

    for i in range(DEPTH):
        kind, j = i % N_MIXERS, i // N_MIXERS
        m_p = jnp.split((cond_p @ w_mod[i] + b_mod[i])[:, None, :], 6, axis=-1)
        m_s = jnp.split((cond_s @ w_mod[i] + b_mod[i])[:, None, :], 6, axis=-1)

        h_p = _layer_norm(x_p) * (1 + m_p[1]) + m_p[0]
        h_s = _layer_norm(x_s) * (1 + m_s[1]) + m_s[0]
        if kind == 0:
            y_p = _fourier_mix(h_p, fnet_w_o[j])
            y_s = _fourier_mix(h_s, fnet_w_o[j])
        elif kind == 1:
            q_p, k_p, v_p = _qkv(h_p, attn_w_qkv[j], attn_q_norm[j], attn_k_norm[j])
            y_p = _attend(q_p, k_p, v_p) @ attn_w_o[j]
            new_ks.append(k_p)
            new_vs.append(v_p)
            q_s, k_s, v_s = _qkv(h_s, attn_w_qkv[j], attn_q_norm[j], attn_k_norm[j])
            q_s, k_s = _axial_rope(q_s), _axial_rope(k_s)
            k_all = jnp.concatenate([k_s, cache_k[:, j]], axis=1)
            v_all = jnp.concatenate([v_s, cache_v[:, j]], axis=1)
            y_s = _attend(q_s, k_all, v_all) @ attn_w_o[j]
        elif kind == 2:
            y_p = _pool_mix(h_p, pool_w[j], pool_scale[j])
            y_s = _pool_mix(h_s, pool_w[j], pool_scale[j])
        else:
            gp = (gmlp_w_in[j], gmlp_b_in[j], gmlp_ln_g[j], gmlp_ln_b[j], gmlp_w_s[j],
                  gmlp_b_s[j], gmlp_w_o[j])
            y_p = _gmlp_mix(h_p, *gp)
            y_s = _gmlp_mix(h_s, *gp)
        x_p = _layer_norm(ALPHA * x_p + m_p[2] * y_p, ln_g[i, 0], ln_b[i, 0])
        x_s = _layer_norm(ALPHA * x_s + m_s[2] * y_s, ln_g[i, 0], ln_b[i, 0])

        h_p = _layer_norm(x_p) * (1 + m_p[4]) + m_p[3]
        h_s = _layer_norm(x_s) * (1 + m_s[4]) + m_s[3]
        f = (ffn_w_up[i], ffn_conv_w[i], ffn_conv_b[i], ffn_w_down[i])
        x_p = _layer_norm(ALPHA * x_p + m_p[5] * _conv_ffn(h_p, *f), ln_g[i, 1], ln_b[i, 1])
        x_s = _layer_norm(ALPHA * x_s + m_s[5] * _conv_ffn(h_s, *f), ln_g[i, 1], ln_b[i, 1])

    new_k = jnp.stack(new_ks, axis=1)
    new_v = jnp.stack(new_vs, axis=1)
    return (x_p, x_s, new_k, new_v)
```

```python
import contextlib
import numpy as np
import ml_dtypes
import concourse.bass as bass
import concourse.mybir as mybir
from concourse.bass_utils import run_bass_kernel_spmd

F32 = mybir.dt.float32
BF16 = mybir.dt.bfloat16
AF = mybir.ActivationFunctionType
ALU = mybir.AluOpType

D = 1024
DFF = 2816
NFF = 22
DEPTH = 4
ALPHA = (2 * DEPTH) ** 0.25
EPS = 1e-6
GROUPS = [[0, 1, 2, 3], [4, 5, 6, 7]]
POOL_HALF = (1, 2, 4, 8)
DEBUG_STOP = None
DEBUG_STAGE = None
DEBUG_LAYERS = None
DEBUG_BLKS = None


class Sched:
    ENG = ['pe', 'act', 'dve', 'pool', 'sp']

    def __init__(self):
        self.ops = {e: [] for e in self.ENG}
        self.clock = {e: {} for e in self.ENG}
        self.cnt = {}
        self.res_w = {}
        self.res_r = {}
        self.alias = {}

    def expand(self, names):
        out = []
        for n in names:
            for m in self.alias.get(n, (n,)):
                if m not in out:
                    out.append(m)
        return out

    def op(self, eng, emit, reads=(), writes=(), dma=None, ndma=1):
        reads = self.expand(reads)
        writes = self.expand(writes)
        clk = self.clock[eng]
        own = 'E_' + eng
        need = {}

        def req(ev):
            if ev is None:
                return
            sem, val, eclk = ev
            if sem == own and eng == 'pe':
                return
            if clk.get(sem, 0) >= val:
                return
            if need.get(sem, (0, None))[0] < val:
                need[sem] = (val, eclk)

        for r in reads:
            req(self.res_w.get(r))
        for w in writes:
            req(self.res_w.get(w))
            for ev in self.res_r.get(w, ()):
                req(ev)
        if dma is not None:
            sem = 'D_' + dma + '_' + eng
            prev = self.cnt.get(sem, 0)
            if prev and clk.get(sem, 0) < prev:
                if need.get(sem, (0, None))[0] < prev:
                    need[sem] = (prev, {})
            inc = 16 * ndma
        else:
            sem = own
            inc = 1
        if emit is None:
            sem = None
        waits = []
        for s, (v, eclk) in need.items():
            if clk.get(s, 0) >= v:
                continue
            waits.append((s, v))
            for k2, v2 in eclk.items():
                if clk.get(k2, 0) < v2:
                    clk[k2] = v2
            if clk.get(s, 0) < v:
                clk[s] = v
        if emit is None:
            self.ops[eng].append((waits, None, None, 0))
            return None
        val = self.cnt.get(sem, 0) + inc
        self.cnt[sem] = val
        self.ops[eng].append((waits, emit, sem, inc, dma is not None and ndma > 1))
        eclk = dict(clk)
        eclk[sem] = val
        ev = (sem, val, eclk)
        for w in writes:
            self.res_w[w] = ev
            self.res_r[w] = []
        for r in reads:
            self.res_r.setdefault(r, []).append(ev)
        return ev

    def sem_names(self):
        return sorted(self.cnt.keys())

    def replay(self, eng, handle, sems):
        for rec in self.ops[eng]:
            waits, emit, sem, inc = rec[:4]
            for s, v in waits:
                handle.wait_ge(sems[s], v)
            if emit is None:
                continue
            if len(rec) > 4 and rec[4]:
                emit(handle, sems[sem])
            else:
                ins = emit(handle)
                ins.then_inc(sems[sem], inc)


def I(name, *a, **kw):
    return lambda e: getattr(e, name)(*a, **kw)


def G(lst):
    def f(e):
        ins = None
        for g in lst:
            ins = g(e)
        return ins
    return f


def build_program():
    nc = bass.Bass("TRN2", target_bir_lowering=False)
    S = Sched()
    op = S.op

    def din(name, shape, dt=F32):
        return nc.dram_tensor(name, list(shape), dt, kind="ExternalInput").ap()

    def dout(name, shape, dt=F32):
        return nc.dram_tensor(name, list(shape), dt, kind="ExternalOutput").ap()

    xin = din("xin", [2048, D])
    condT = din("condT", [128, 8, 2])
    w_mod = din("w_mod", [DEPTH, D, 6 * D])
    bmodT = din("bmodT", [128, DEPTH, 48])
    b_mod = din("b_mod", [DEPTH, 6 * D])
    ln_g = din("ln_g", [DEPTH, 2, D])
    ln_b = din("ln_b", [DEPTH, 2, D])
    ffn_w_up = din("ffn_w_up", [DEPTH, D, 2 * DFF])
    convT = din("convT", [128, DEPTH, 4, NFF])
    ffn_w_down = din("ffn_w_down", [DEPTH, DFF, D])
    fnet_w_o = din("fnet_w_o", [1, D, D])
    attn_w_qkv = din("attn_w_qkv", [1, D, 1536])
    qkT = din("qkT", [128, 2])
    attn_w_o = din("attn_w_o", [1, D, D])
    pool_w = din("pool_w", [1, 4, 256, 256])
    pool_scale = din("pool_scale", [1, D])
    gmlp_w_in = din("gmlp_w_in", [1, D, 4096])
    gbinT = din("gbinT", [128, 16])
    gmlp_b_in = din("gmlp_b_in", [1, 4096])
    gmlp_ln_g = din("gmlp_ln_g", [1, 2048])
    glnbT = din("glnbT", [128, 16])
    wsT = din("wsT", [128, 4, 128])
    gmlp_b_s = din("gmlp_b_s", [1, 512])
    gmlp_w_o = din("gmlp_w_o", [1, 2048, D])
    cache_k = din("cache_k", [512, 256])
    cache_v = din("cache_v", [512, 256])
    c_ident = din("c_ident", [128, 128], BF16)
    c_identf = din("c_identf", [128, 128])
    c_ones = din("c_ones", [128, 128], BF16)
    c_onesf = din("c_onesf", [128, 128])
    c_rot = din("c_rot", [128, 128])
    c_cs = din("c_cs", [128, 2, 512], BF16)
    c_ctp = din("c_ctp", [128, 2, 512], BF16)
    c_cts = din("c_cts", [4096, 1024], BF16)
    c_nsts = din("c_nsts", [4096, 1024], BF16)
    c_cos = din("c_cos", [128, 1024])
    c_sin = din("c_sin", [128, 1024])
    c_mask = din("c_mask", [128, 8])
    c_rc = din("c_rc", [128, 2, 4, 2, 16])

    yout = dout("yout", [2048, D])
    nk_out = dout("nk_out", [1024, 256])
    nv_out = dout("nv_out", [1024, 256])

    halo_w = [1, 1, 1, 1, 8]
    halo_in = [nc.dram_tensor(f"halo_in{i}", [128, 16 * w_], BF16) for i, w_ in enumerate(halo_w)]
    halo_out = [nc.dram_tensor(f"halo_out{i}", [512, 16 * w_], BF16) for i, w_ in enumerate(halo_w)]
    ab_in = [nc.dram_tensor(f"ab_in{j}", [256, 2048], BF16) for j in range(4)]
    ab_out = [nc.dram_tensor(f"ab_out{j}", [1024, 2048], BF16) for j in range(4)]
    k_in = nc.dram_tensor("k_in", [128, 2048], BF16)
    k_out = nc.dram_tensor("k_out", [512, 2048], BF16)
    v_in = nc.dram_tensor("v_in", [128, 2048], BF16)
    v_out = nc.dram_tensor("v_out", [512, 2048], BF16)

    st = contextlib.ExitStack()
    with st:
        def sb(name, shape, dt=F32):
            return st.enter_context(nc.sbuf_tensor(name, list(shape), dt))

        X = sb("X", [128, 16, D])
        HT = sb("HT", [128, 8, 1024], BF16)
        BIG = sb("BIG", [128, 26624], BF16)
        WR = [sb(f"WR{i}", [128, 4096], BF16) for i in range(3)]
        GATE = [sb(f"GATE{g}", [128, D]) for g in range(2)]
        LNG = sb("LNG", [128, D])
        LNB = sb("LNB", [128, D])
        A1 = [sb(f"A1_{i}", [128, 1088]) for i in range(2)]
        U = sb("U", [128, 2, 1024])
        XNB = [sb(f"XNB{i}", [128, 1024], BF16) for i in range(2)]
        MODT = sb("MODT", [128, 4, 8, 2])
        BMT = sb("BMT", [128, DEPTH, 48])
        CONV = sb("CONV", [128, DEPTH, 4, NFF])
        CST = sb("CST", [128, 8, 2])
        CSB = sb("CSB", [128, 8, 2], BF16)
        IDENT = sb("IDENT", [128, 128], BF16)
        IDENTF = sb("IDENTF", [128, 128])
        ONES = sb("ONES", [128, 128], BF16)
        ONESF = sb("ONESF", [128, 128])
        ROT = sb("ROT", [128, 128])
        MASK = sb("MASK", [128, 8])
        STATS = [sb(f"STATS{i}", [128, 4, 6]) for i in range(8)]
        MV = [sb(f"MV{i}", [128, 2]) for i in range(8)]
        RS = [sb(f"RS{i}", [128, 1]) for i in range(8)]
        NMR = [sb(f"NMR{i}", [128, 1]) for i in range(8)]
        EPSC = sb("EPSC", [128, 1])
        HALL = sb("HALL", [128, 4, 128], BF16)
        HSEL = sb("HSEL", [128, 2, 8, 8])
        HTH = sb("HTH", [128, 8, 2], BF16)
        QK = sb("QK", [128, 2])
        SMALL = sb("SMALL", [128, 512])
        GB = sb("GB", [128, 48])
        PS = st.enter_context(nc.psum_tensor("PS", [128, 8, 512], F32))

        al = S.alias
        al['U0'] = ['U0a', 'U0b']; al['U1'] = ['U1a', 'U1b']
        al['U0q'] = ['U0a']; al['U0s'] = ['U0b']; al['U1q'] = ['U1a']; al['U1s'] = ['U1b']
        al['VL'] = ['U0a', 'U0b', 'U1a', 'U1b']
        for jj in range(8):
            al[f'VS{jj}'] = [['U0a', 'U0b', 'U1a', 'U1b'][jj // 2]]
        for i in range(2):
            al[f'A1_{i}'] = [f'A1_{i}a', f'A1_{i}b', f'A1_{i}x']
            for h in range(2):
                al[f'NKS{i}_{h}'] = al[f'A1_{i}']
        al['COS'] = ['A1_0a', 'A1_0b']; al['SIN'] = ['A1_1a', 'A1_1b']
        al['P0'] = ['A1_0a']; al['P1'] = ['A1_0b']; al['P2'] = ['A1_1a']; al['P3'] = ['A1_1b']
        al['WST'] = ['HALL']
        al['BROW'] = ['A1_0a', 'A1_0b']

        def bu(off, ln):
            return [f'B{u}' for u in range(off // 1024, (off + ln + 1023) // 1024)]
        for c in range(NFF):
            al[f'GT{c}'] = bu(c * 1024, 1024)
        for jj in range(8):
            al[f'AB{jj}'] = bu(jj * 2048, 2048)
            al[f'FT{jj}'] = bu(16384 + jj * 1024, 1024)
            al[f'PT{jj}'] = bu(jj * 1024, 1024)
            al[f'V{jj}'] = bu(17408 + jj * 256, 256)
        for h in range(8):
            for hf in range(2):
                al[f'QT{h}_{hf}'] = bu(h * 1024 + hf * 512, 512)
            for q0 in range(0, 1024, 256):
                al[f'OT{h}_{q0}'] = bu(h * 1024 + q0, 256)
        for h in range(2):
            for hf in range(2):
                al[f'KT{h}_{hf}'] = bu(8192 + h * 4608 + hf * 512, 512)
            al[f'KTall{h}'] = bu(8192 + h * 4608, 4096)
            al[f'KTc{h}'] = bu(8192 + h * 4608 + 4096, 512)
        al['Vall'] = bu(17408, 32 * 256)
        al['Vc'] = bu(17408 + 32 * 256, 1024)
        for c in range(16):
            al[f'UT{c}'] = bu(c * 1024, 1024)
            al[f'CT{c}'] = bu(20480 + c * 256, 256)
        for jj in range(8):
            for gg in range(4):
                al[f'US{jj}_{gg}'] = bu(gg * 4096, 4096)
        al['GBC'] = bu(16384, 4096)
        al['VB'] = bu(24576, 2048)

        def bank(b):
            return PS[:, b, :]

        def bank2(b):
            return PS[:, b:b + 2, :].rearrange("p a n -> p (a n)")

        def psr(b, n=1):
            return [f'PS{i}' for i in range(b, b + n)]

        ring_i = [0]

        def ring():
            i = ring_i[0] % 3
            ring_i[0] += 1
            return WR[i], f'WR{i}'

        def dma(eng, out, in_, reads, writes, key):
            op(eng, I('dma_start', out=out, in_=in_), reads=reads, writes=writes, dma=key)

        def dmas(eng, pairs, reads, writes, key, slow=False):
            def emit(e, semh, pairs=tuple(pairs)):
                for (o, i_) in pairs:
                    e.dma_start(out=o, in_=i_, allow_slow_non_contiguous=slow).then_inc(semh, 16)
            op(eng, emit, reads=reads, writes=writes, dma=key, ndma=len(pairs))

        def wrows(src2d, c0, ncols):
            return src2d[:, c0:c0 + ncols].rearrange("(k p) n -> p k n", p=128)

        xin_v = xin.rearrange("(j p) d -> p j d", p=128)
        dmas('sp', [(X[:, j, :], xin_v[:, j, :]) for j in range(16)], [], [f'X{j}' for j in range(16)], 'XIN')
        for (t, src, nm) in [(CST, condT, 'CST'), (BMT, bmodT, 'BMT'), (CONV, convT, 'CONV'), (IDENT, c_ident, 'IDENT'),
                             (IDENTF, c_identf, 'IDENTF'), (ONES, c_ones, 'ONES'), (ONESF, c_onesf, 'ONESF'),
                             (ROT, c_rot, 'ROT'), (MASK, c_mask, 'MASK'), (QK, qkT, 'QK')]:
            dma('sp', t[:], src, [], [nm], nm)
        op('dve', I('memset', EPSC[:], EPS), writes=['EPSC'])
        op('act', I('activation', out=CST[:], in_=CST[:], func=AF.Silu), reads=['CST'], writes=['CST'])
        op('dve', I('tensor_copy', out=CSB[:], in_=CST[:]), reads=['CST'], writes=['CSB'])

        def mod_phase(L):
            for vi, v in enumerate((0, 1, 3, 4)):
                for half in range(2):
                    W, wn = ring()
                    Wv = W[:].rearrange("p (k n) -> p k n", k=8)
                    dma('pool', Wv, wrows(w_mod[L], v * 1024 + half * 512, 512), [], [wn], wn)
                    lst = [I('matmul', PS[0:2, 0, :], lhsT=CSB[:, k, :], rhs=Wv[:, k, :], start=(k == 0), stop=(k == 7)) for k in range(8)]
                    op('pe', G(lst), reads=[wn, 'CSB'], writes=['PS0'])
                    op('act', I('activation', out=SMALL[0:2, :], in_=PS[0:2, 0, :], func=AF.Identity), reads=['PS0'], writes=['SMALL'])
                    lst = [I('transpose', out=PS[:, 1, cc * 2:cc * 2 + 2], in_=SMALL[0:2, cc * 128:(cc + 1) * 128], identity=IDENTF[0:2, 0:2]) for cc in range(4)]
                    op('pe', G(lst), reads=['SMALL', 'IDENTF'], writes=['PS1'])
                    bias_ap = BMT[:, L, v * 8 + half * 4:v * 8 + half * 4 + 4].unsqueeze(2).to_broadcast([128, 4, 2])
                    src = PS[:, 1, 0:8].rearrange("p (c g) -> p c g", g=2)
                    dst = MODT[:, vi, half * 4:half * 4 + 4, :]
                    if v in (1, 4):
                        op('dve', I('scalar_tensor_tensor', out=dst, in0=src, scalar=1.0, in1=bias_ap, op0=ALU.add, op1=ALU.add), reads=['PS1', 'BMT'], writes=['MODT'])
                    else:
                        op('dve', I('tensor_tensor', out=dst, in0=src, in1=bias_ap, op=ALU.add), reads=['PS1', 'BMT'], writes=['MODT'])

        def gates_phase(L, s_):
            v = (2, 5)[s_]
            CSREP = [XNB[g][:].rearrange("p (k n) -> p k n", k=8) for g in range(2)]
            for g in range(2):
                op('dve', I('tensor_copy', out=CSREP[g], in_=CST[:, :, g:g + 1].to_broadcast([128, 8, 128])), reads=['CST'], writes=[f'XNB{g}'])
            for half in range(2):
                W, wn = ring()
                Wv = W[:].rearrange("p (k n) -> p k n", k=8)
                c0 = v * 1024 + half * 512
                dma('pool', Wv, wrows(w_mod[L], c0, 512), [], [wn], wn)
                dma('sp', SMALL[:], b_mod[L:L + 1, c0:c0 + 512].partition_broadcast(128), [], ['SMALL'], 'SMALL')
                for g in range(2):
                    lst = [I('matmul', bank(1 + g), lhsT=CSREP[g][:, k, :], rhs=Wv[:, k, :], start=(k == 0), stop=(k == 7)) for k in range(8)]
                    op('pe', G(lst), reads=[wn, f'XNB{g}'], writes=psr(1 + g))
                    op('dve', I('tensor_tensor', out=GATE[g][:, half * 512:(half + 1) * 512], in0=bank(1 + g), in1=SMALL[:], op=ALU.add),
                       reads=psr(1 + g) + ['SMALL'], writes=[f'GATE{g}'])

        def load_ln(L, sub):
            dma('sp', LNG[:], ln_g[L, sub:sub + 1, :].partition_broadcast(128), [], ['LNG'], 'LNG')
            dma('sp', LNB[:], ln_b[L, sub:sub + 1, :].partition_broadcast(128), [], ['LNB'], 'LNB')

        def ln_stats(src_ap, res, i, width=1024):
            nch = width // 512
            lst = [I('bn_stats', out=STATS[i][:, c, :], in_=src_ap[:, c * 512:(c + 1) * 512]) for c in range(nch)]
            op('dve', G(lst), reads=res, writes=[f'STATS{i}'])
            op('dve', I('bn_aggr', out=MV[i][:], in_=STATS[i][:, 0:nch, :]), reads=[f'STATS{i}'], writes=[f'MV{i}'])
            op('act', I('activation', out=RS[i][:], in_=MV[i][:, 1:2], func=AF.Sqrt, bias=EPSC[:], scale=1.0),
               reads=[f'MV{i}', 'EPSC'], writes=[f'RS{i}'])
            op('dve', I('reciprocal', out=RS[i][:], in_=RS[i][:]), reads=[f'RS{i}'], writes=[f'RS{i}'])

        def ht_res(tiles, ks=range(8)):
            return [f'HT{jj}_{k}' for jj in tiles for k in ks]

        def prenorm(blk, sub):
            g = blk
            for j0 in range(0, 8, 2):
                tiles = [(jj, blk * 8 + jj, jj % 2) for jj in (j0, j0 + 1)]
                for (jj, j, i) in tiles:
                    lst = [I('bn_stats', out=STATS[i][:, c, :], in_=X[:, j, c * 512:(c + 1) * 512]) for c in range(2)]
                    op('dve', G(lst), reads=[f'X{j}'], writes=[f'STATS{i}'])
                for (jj, j, i) in tiles:
                    op('dve', I('bn_aggr', out=MV[i][:], in_=STATS[i][:, 0:2, :]), reads=[f'STATS{i}'], writes=[f'MV{i}'])
                for (jj, j, i) in tiles:
                    op('act', I('activation', out=RS[i][:], in_=MV[i][:, 1:2], func=AF.Sqrt, bias=EPSC[:], scale=1.0), reads=[f'MV{i}', 'EPSC'], writes=[f'RS{i}'])
                for (jj, j, i) in tiles:
                    op('dve', I('reciprocal', out=RS[i][:], in_=RS[i][:]), reads=[f'RS{i}'], writes=[f'RS{i}'])
                for (jj, j, i) in tiles:
                    op('dve', I('tensor_scalar', out=XNB[i][:], in0=X[:, j, :], scalar1=MV[i][:, 0:1], scalar2=RS[i][:], op0=ALU.subtract, op1=ALU.mult),
                       reads=[f'X{j}', f'MV{i}', f'RS{i}'], writes=[f'XNB{i}'])
                for (jj, j, i) in tiles:
                    pb = 6 + i
                    pt = PS[:, pb, :].bitcast(BF16).rearrange("p (k n) -> p k n", k=8)
                    lst = [I('transpose', out=pt[:, k, :], in_=XNB[i][:, k * 128:(k + 1) * 128], identity=IDENT[:]) for k in range(8)]
                    op('pe', G(lst), reads=[f'XNB{i}', 'IDENT'], writes=psr(pb))
                for (jj, j, i) in tiles:
                    pb = 6 + i
                    pt = PS[:, pb, :].bitcast(BF16).rearrange("p (k n) -> p k n", k=8)
                    for k in range(8):
                        sc = MODT[:, 2 * sub + 1, k, g:g + 1]
                        sh = MODT[:, 2 * sub, k, g:g + 1]
                        dst = HT[:, k, jj * 128:(jj + 1) * 128]
                        op('dve', I('tensor_scalar', out=dst, in0=pt[:, k, :], scalar1=sc, scalar2=sh, op0=ALU.mult, op1=ALU.add),
                           reads=psr(pb) + ['MODT'], writes=[f'HT{jj}_{k}'])

        def epilogue4(blk, tg, sub, tiles=None):
            g = blk
            Tb = [U[:, 0, :], U[:, 1, :], A1[0][:, 0:1024], A1[1][:, 0:1024]]
            Tn = ['U0', 'U1', 'A1_0', 'A1_1']
            if tiles is None:
                tiles = [tg * 4 + t4 for t4 in range(4)]
            tl = [(4 + t4, blk * 8 + jj, Tb[t4], Tn[t4], t4 * 2) for t4, jj in enumerate(tiles)]
            for (i, j, T, tn, pb) in tl:
                op('dve', I('tensor_tensor', out=T, in0=bank2(pb), in1=GATE[g][:], op=ALU.mult), reads=psr(pb, 2) + [f'GATE{g}'], writes=[tn])
            for (i, j, T, tn, pb) in tl:
                op('dve', I('scalar_tensor_tensor', out=T, in0=X[:, j, :], scalar=ALPHA, in1=T, op0=ALU.mult, op1=ALU.add), reads=[tn, f'X{j}'], writes=[tn])
            for (i, j, T, tn, pb) in tl:
                lst = [I('bn_stats', out=STATS[i][:, c, :], in_=T[:, c * 512:(c + 1) * 512]) for c in range(2)]
                op('dve', G(lst), reads=[tn], writes=[f'STATS{i}'])
            for (i, j, T, tn, pb) in tl:
                op('dve', I('bn_aggr', out=MV[i][:], in_=STATS[i][:, 0:2, :]), reads=[f'STATS{i}'], writes=[f'MV{i}'])
            for (i, j, T, tn, pb) in tl:
                op('act', I('activation', out=RS[i][:], in_=MV[i][:, 1:2], func=AF.Sqrt, bias=EPSC[:], scale=1.0), reads=[f'MV{i}', 'EPSC'], writes=[f'RS{i}'])
            for (i, j, T, tn, pb) in tl:
                op('dve', I('reciprocal', out=RS[i][:], in_=RS[i][:]), reads=[f'RS{i}'], writes=[f'RS{i}'])
            for (i, j, T, tn, pb) in tl:
                op('dve', I('scalar_tensor_tensor', out=NMR[i][:], in0=MV[i][:, 0:1], scalar=-1.0, in1=RS[i][:], op0=ALU.mult, op1=ALU.mult),
                   reads=[f'MV{i}', f'RS{i}'], writes=[f'NMR{i}'])
            for (i, j, T, tn, pb) in tl:
                op('act', I('activation', out=T, in_=T, func=AF.Identity, scale=RS[i][:], bias=NMR[i][:]), reads=[tn, f'RS{i}', f'NMR{i}'], writes=[tn])
            for (i, j, T, tn, pb) in tl:
                op('dve', I('tensor_tensor', out=T, in0=T, in1=LNG[:], op=ALU.mult), reads=[tn, 'LNG'], writes=[tn])
            for (i, j, T, tn, pb) in tl:
                op('dve', I('tensor_tensor', out=X[:, j, :], in0=T, in1=LNB[:], op=ALU.add), reads=[tn, 'LNB'], writes=[f'X{j}'])

        def wd_load(Wd, k0, kn):
            W, wn = ring()
            Wv = W[:].rearrange("p (k n) -> p k n", k=4)
            dma('pool', Wv[:, 0:kn, :], Wd[k0 * 128:(k0 + kn) * 128, :].rearrange("(k p) n -> p k n", p=128), [], [wn], wn)
            return Wv, wn

        def proj_out(blk, sub, FT, ft_res, KC, Wd, gsz=4, preloaded=()):
            g = blk
            tmps = [(U[:, 0, 0:512], 'U0q'), (U[:, 0, 512:1024], 'U0s'), (U[:, 1, 0:512], 'U1q'), (U[:, 1, 512:1024], 'U1s')]
            kgroups = [(k0, min(8, KC - k0)) for k0 in range(0, KC, 8)]
            for hf in range(2):
                cs = slice(hf * 512, (hf + 1) * 512)
                for (k0, kn) in kgroups:
                    W, wn = ring()
                    Wv = W[:].rearrange("p (k n) -> p k n", k=8)
                    dma('pool', Wv[:, 0:kn, :], Wd[k0 * 128:(k0 + kn) * 128, cs].rearrange("(k p) n -> p k n", p=128), [], [wn], wn)
                    for jj in range(8):
                        lst = [I('matmul', bank(jj), lhsT=FT[:, k0 + kk, jj * 128:(jj + 1) * 128], rhs=Wv[:, kk, :],
                                 start=(k0 + kk == 0), stop=(k0 + kk == KC - 1)) for kk in range(kn)]
                        op('pe', G(lst), reads=[wn] + ft_res, writes=psr(jj))
                for jj in range(8):
                    j = blk * 8 + jj
                    T, tn = tmps[jj % 4]
                    op('dve', I('tensor_tensor', out=T, in0=bank(jj), in1=GATE[g][:, cs], op=ALU.mult), reads=psr(jj) + [f'GATE{g}'], writes=[tn])
                    op('dve', I('scalar_tensor_tensor', out=X[:, j, cs], in0=X[:, j, cs], scalar=ALPHA, in1=T, op0=ALU.mult, op1=ALU.add),
                       reads=[tn, f'X{j}'], writes=[f'X{j}'])
            for tg in range(2):
                tl = [(4 + t4, blk * 8 + tg * 4 + t4) for t4 in range(4)]
                for (i, j) in tl:
                    lst = [I('bn_stats', out=STATS[i][:, c, :], in_=X[:, j, c * 512:(c + 1) * 512]) for c in range(2)]
                    op('dve', G(lst), reads=[f'X{j}'], writes=[f'STATS{i}'])
                for (i, j) in tl:
                    op('dve', I('bn_aggr', out=MV[i][:], in_=STATS[i][:, 0:2, :]), reads=[f'STATS{i}'], writes=[f'MV{i}'])
                for (i, j) in tl:
                    op('act', I('activation', out=RS[i][:], in_=MV[i][:, 1:2], func=AF.Sqrt, bias=EPSC[:], scale=1.0), reads=[f'MV{i}', 'EPSC'], writes=[f'RS{i}'])
                for (i, j) in tl:
                    op('dve', I('reciprocal', out=RS[i][:], in_=RS[i][:]), reads=[f'RS{i}'], writes=[f'RS{i}'])
                for (i, j) in tl:
                    op('dve', I('scalar_tensor_tensor', out=NMR[i][:], in0=MV[i][:, 0:1], scalar=-1.0, in1=RS[i][:], op0=ALU.mult, op1=ALU.mult),
                       reads=[f'MV{i}', f'RS{i}'], writes=[f'NMR{i}'])
                for (i, j) in tl:
                    op('act', I('activation', out=X[:, j, :], in_=X[:, j, :], func=AF.Identity, scale=RS[i][:], bias=NMR[i][:]),
                       reads=[f'X{j}', f'RS{i}', f'NMR{i}'], writes=[f'X{j}'])
                for (i, j) in tl:
                    op('dve', I('tensor_tensor', out=X[:, j, :], in0=X[:, j, :], in1=LNG[:], op=ALU.mult), reads=[f'X{j}', 'LNG'], writes=[f'X{j}'])
                for (i, j) in tl:
                    op('dve', I('tensor_tensor', out=X[:, j, :], in0=X[:, j, :], in1=LNB[:], op=ALU.add), reads=[f'X{j}', 'LNB'], writes=[f'X{j}'])

        def halo_exchange(idx, w):
            hi = halo_in[idx].ap()
            ho = halo_out[idx].ap()
            hiv = hi.rearrange("p (s k w) -> p s k w", s=2, k=8)
            dmas('sp', [(hiv[:, 0, :, :], HT[:, :, 0:w]), (hiv[:, 1, :, :], HT[:, :, 1024 - w:1024])],
                 ht_res([0, 7]), [f'halo_in{idx}'], f'hin{idx}', slow=True)
            op('pool', I('collective_compute', "AllGather", ALU.bypass, replica_groups=GROUPS, ins=[hi.opt()], outs=[ho.opt()]),
               reads=[f'halo_in{idx}'], writes=[f'halo_out{idx}'])
            op('pool', None, reads=[f'halo_out{idx}'])
            dma('sp', HALL[:, :, 0:16 * w], ho.rearrange("(r p) c -> p r c", p=128), [f'halo_out{idx}'], ['HALL'], 'HALL')
            hv = HALL[:, :, 0:16 * w].rearrange("p r (s k w) -> p r s k w", s=2, k=8)
            for side in range(2):
                src_s = 1 - side
                for r in range(4):
                    m = MASK[:, side * 4 + r:side * 4 + r + 1]
                    if r == 0:
                        op('dve', I('tensor_scalar', out=HSEL[:, side, :, 0:w], in0=hv[:, r, src_s, :, :], scalar1=m, scalar2=None, op0=ALU.mult),
                           reads=['HALL', 'MASK'], writes=[f'HSEL{side}'])
                    else:
                        op('dve', I('scalar_tensor_tensor', out=HSEL[:, side, :, 0:w], in0=hv[:, r, src_s, :, :], scalar=m,
                                    in1=HSEL[:, side, :, 0:w], op0=ALU.mult, op1=ALU.add),
                           reads=['HALL', 'MASK', f'HSEL{side}'], writes=[f'HSEL{side}'])

        def ffn_up(blk, L):
            sub = 1
            nseq, Ls = (4, 256) if blk == 0 else (1, 1024)
            GT = BIG[:, 0:NFF * 1024].rearrange("p (c n) -> p c n", c=NFF)
            if blk == 1:
                halo_exchange(L, 1)
                op('dve', I('tensor_copy', out=HTH[:, :, 0:1], in_=HSEL[:, 0, :, 0:1]), reads=['HSEL0'], writes=['HTH'])
                op('dve', I('tensor_copy', out=HTH[:, :, 1:2], in_=HSEL[:, 1, :, 0:1]), reads=['HSEL1', 'HTH'], writes=['HTH'])
            pair = 0
            for cg in range(11):
                W, wn = ring()
                Wv = W[:].rearrange("p (k n) -> p k n", k=8)
                c0 = cg * 256
                dmas('pool', [(Wv[:, :, 0:256], wrows(ffn_w_up[L], c0, 256)), (Wv[:, :, 256:512], wrows(ffn_w_up[L], DFF + c0, 256))], [], [wn], wn)
                for ci in range(2):
                    c = cg * 2 + ci
                    pa = (pair % 3) * 2
                    pbk = ((pair + 1) % 3) * 2
                    pair += 2
                    ai = c % 2
                    A = A1[ai]
                    an = f'A1_{ai}'
                    Av = A[:, 0:nseq * (Ls + 2)].rearrange("p (s l) -> p s l", s=nseq)
                    lst = [I('matmul', bank(pa + hf), lhsT=Wv[:, k, ci * 128:(ci + 1) * 128], rhs=HT[:, k, hf * 512:(hf + 1) * 512],
                             start=(k == 0), stop=(k == 7)) for hf in range(2) for k in range(8)]
                    op('pe', G(lst), reads=[wn] + ht_res(range(8)), writes=psr(pa, 2))
                    if blk == 1:
                        lst = [I('matmul', PS[:, 6, 0:2], lhsT=Wv[:, k, ci * 128:(ci + 1) * 128], rhs=HTH[:, k, :], start=(k == 0), stop=(k == 7)) for k in range(8)]
                        op('pe', G(lst), reads=[wn, 'HTH'], writes=['PS6'])
                    lst = [I('matmul', bank(pbk + hf), lhsT=Wv[:, k, 256 + ci * 128:256 + (ci + 1) * 128], rhs=HT[:, k, hf * 512:(hf + 1) * 512],
                             start=(k == 0), stop=(k == 7)) for hf in range(2) for k in range(8)]
                    op('pe', G(lst), reads=[wn] + ht_res(range(8)), writes=psr(pbk, 2))
                    a_ps = bank2(pa).rearrange("p (s l) -> p s l", s=nseq)
                    b_ps = bank2(pbk)
                    w0 = CONV[:, L, 0, c:c + 1]
                    w1 = CONV[:, L, 1, c:c + 1]
                    w2 = CONV[:, L, 2, c:c + 1]
                    cb = CONV[:, L, 3, c:c + 1]
                    Ui = U[:, c % 2, :]
                    un = f'U{c % 2}'
                    Uv = Ui.rearrange("p (s l) -> p s l", s=nseq)
                    if blk == 0:
                        op('pool', G([I('memset', Av[:, :, 0:1], 0.0), I('memset', Av[:, :, Ls + 1:Ls + 2], 0.0)]), writes=[an])
                    else:
                        op('act', G([I('activation', out=Av[:, 0, 0:1], in_=PS[:, 6, 0:1], func=AF.Identity),
                                     I('activation', out=Av[:, 0, Ls + 1:Ls + 2], in_=PS[:, 6, 1:2], func=AF.Identity)]), reads=['PS6'], writes=[an])
                    op('act', I('activation', out=Av[:, :, 1:Ls + 1], in_=a_ps, func=AF.Identity), reads=psr(pa, 2), writes=[an])
                    op('act', I('activation', out=Uv, in_=a_ps, func=AF.Identity, scale=w1, bias=cb), reads=psr(pa, 2) + ['CONV'], writes=[un])
                    op('dve', I('scalar_tensor_tensor', out=Uv, in0=Av[:, :, 0:Ls], scalar=w0, in1=Uv, op0=ALU.mult, op1=ALU.add),
                       reads=[an, un, 'CONV'], writes=[un])
                    op('dve', I('scalar_tensor_tensor', out=Uv, in0=Av[:, :, 2:Ls + 2], scalar=w2, in1=Uv, op0=ALU.mult, op1=ALU.add),
                       reads=[an, un, 'CONV'], writes=[un])
                    op('act', I('activation', out=Ui, in_=Ui, func=AF.Gelu_apprx_tanh), reads=[un], writes=[un])
                    op('dve', I('tensor_tensor', out=GT[:, c, :], in0=Ui, in1=b_ps, op=ALU.mult), reads=[un] + psr(pbk, 2), writes=[f'GT{c}'])

        def ffn_down(blk, L):
            GT = BIG[:, 0:NFF * 1024].rearrange("p (c n) -> p c n", c=NFF)
            proj_out(blk, 1, GT, [f'GT{c}' for c in range(NFF)], NFF, ffn_w_down[L], gsz=3)

        def fnet_a(blk):
            ABs = BIG[:, 0:16384].rearrange("p (j c) -> p j c", j=8)
            CSs = SMALL[:].bitcast(BF16).rearrange("p (k n) -> p k n", k=2)
            dma('sp', CSs, c_cs, [], ['SMALL'], 'SMALL')
            for jj in range(8):
                lst = [I('matmul', bank(g), lhsT=HT[:, 2 * g + kk, jj * 128:(jj + 1) * 128], rhs=CSs[:, kk, :], start=(kk == 0), stop=(kk == 1))
                       for g in range(4) for kk in range(2)]
                op('pe', G(lst), reads=ht_res([jj]) + ['SMALL'], writes=psr(0, 4))
                src = PS[:, 0:4, :].rearrange("p a n -> p (a n)")
                if jj % 2 == 0:
                    op('act', I('activation', out=ABs[:, jj, :], in_=src, func=AF.Identity), reads=psr(0, 4), writes=[f'AB{jj}'])
                else:
                    op('dve', I('tensor_copy', out=ABs[:, jj, :], in_=src), reads=psr(0, 4), writes=[f'AB{jj}'])
                if blk == 1 and jj % 2 == 1:
                    q_ = jj // 2
                    abi = ab_in[q_].ap()
                    abo = ab_out[q_].ap()
                    dma('sp', abi.rearrange("(t p) c -> p t c", p=128), ABs[:, jj - 1:jj + 1, :], [f'AB{jj - 1}', f'AB{jj}'], [f'ab_in{q_}'], f'ab_in{q_}')
                    op('pool', I('collective_compute', "AllGather", ALU.bypass, replica_groups=GROUPS, ins=[abi.opt()], outs=[abo.opt()]),
                       reads=[f'ab_in{q_}'], writes=[f'ab_out{q_}'])
                    op('pool', None, reads=[f'ab_out{q_}'])

        def fnet_b(blk, preloaded=()):
            ABs = BIG[:, 0:16384].rearrange("p (j c) -> p j c", j=8)
            FT = BIG[:, 16384:24576].rearrange("p (k n) -> p k n", k=8)

            def evac(m, mi):
                if m % 2 == 0:
                    op('act', I('activation', out=FT[:, m, :], in_=bank2(mi * 2), func=AF.Identity), reads=psr(mi * 2, 2), writes=[f'FT{m}'])
                else:
                    op('dve', I('tensor_copy', out=FT[:, m, :], in_=bank2(mi * 2)), reads=psr(mi * 2, 2), writes=[f'FT{m}'])
            if blk == 0:
                W, wn = ring()
                CTP = W[:, 0:1024].rearrange("p (k n) -> p k n", k=2)
                dma('sp', CTP, c_ctp, [], [wn], wn)
                for mg in range(2):
                    for mi in range(4):
                        m = mg * 4 + mi
                        g, mm_ = m // 2, m % 2
                        lst = []
                        for s_ in range(4):
                            n = 0
                            for kk in range(2):
                                for ab in range(2):
                                    c0 = g * 512 + ab * 256 + mm_ * 128
                                    lst.append(I('matmul', PS[:, mi * 2 + s_ // 2, (s_ % 2) * 256:(s_ % 2) * 256 + 256],
                                                 lhsT=ABs[:, s_ * 2 + kk, c0:c0 + 128], rhs=CTP[:, kk, ab * 256:(ab + 1) * 256],
                                                 start=(n == 0), stop=(n == 3)))
                                    n += 1
                        op('pe', G(lst), reads=[wn] + [f'AB{j}' for j in range(8)], writes=psr(mi * 2, 2))
                        evac(m, mi)
            else:
                for mg in range(2):
                    for tc in range(32):
                        r_, j_ = tc // 8, tc % 8
                        W, wn = ring()
                        Aw = W[:, 0:1024].rearrange("p (g n) -> p g n", g=2)
                        Cw = W[:, 1024:2048]
                        Sw = W[:, 2048:3072]
                        q_, t_ = j_ // 2, j_ % 2
                        r0 = r_ * 256 + t_ * 128
                        dmas('sp', [(Aw, ab_out[q_].ap()[r0:r0 + 128, mg * 1024:(mg + 1) * 1024].rearrange("p (g n) -> p g n", g=2)),
                                    (Cw, c_cts[tc * 128:(tc + 1) * 128, :]), (Sw, c_nsts[tc * 128:(tc + 1) * 128, :])], [f'ab_out{q_}'], [wn], wn)
                        lst = []
                        for mi in range(4):
                            gl, mm_ = mi // 2, mi % 2
                            for hf in range(2):
                                for ab in range(2):
                                    rhs = (Cw if ab == 0 else Sw)[:, hf * 512:(hf + 1) * 512]
                                    lst.append(I('matmul', bank(mi * 2 + hf), lhsT=Aw[:, gl, ab * 256 + mm_ * 128:ab * 256 + (mm_ + 1) * 128], rhs=rhs,
                                                 start=(tc == 0 and ab == 0), stop=(tc == 31 and ab == 1)))
                        op('pe', G(lst), reads=[wn], writes=psr(0, 8))
                    for mi in range(4):
                        evac(mg * 4 + mi, mi)
            proj_out(blk, 0, FT, [f'FT{m}' for m in range(8)], 8, fnet_w_o[0], preloaded=preloaded)

        def attn(blk):
            ASTG = (DEBUG_STAGE or 99) % 10
            if DEBUG_STAGE is not None and DEBUG_STAGE < 10 and blk == 0:
                return
            QT = BIG[:, 0:8192].rearrange("p (h n) -> p h n", h=8)
            KT = BIG[:, 8192:8192 + 2 * 4608].rearrange("p (h n) -> p h n", h=2)
            VV = BIG[:, 17408:17408 + 36 * 256].rearrange("p (t c) -> p t c", t=36)
            wq = attn_w_qkv[0]
            if blk == 1:
                dma('sp', A1[0][:, 0:1024], c_cos, [], ['COS'], 'COS')
                dma('sp', A1[1][:, 0:1024], c_sin, [], ['SIN'], 'SIN')
            for hg in range(3):
                W, wn = ring()
                nh = 4 if hg < 2 else 2
                Wv = W[:].rearrange("p (k n) -> p k n", k=8)
                dma('pool', Wv[:, :, 0:nh * 128], wrows(wq, hg * 512, nh * 128), [], [wn], wn)
                for hi in range(nh):
                    isq = hg < 2
                    h = hg * 4 + hi if isq else hi
                    gain = QK[:, 0:1] if isq else QK[:, 1:2]
                    for hf in range(2):
                        cols = slice(hf * 512, (hf + 1) * 512)
                        par = (hi * 2 + hf) % 2
                        pb = par * 3
                        lst = [I('matmul', bank(pb), lhsT=Wv[:, k, hi * 128:(hi + 1) * 128], rhs=HT[:, k, cols], start=(k == 0), stop=(k == 7)) for k in range(8)]
                        op('pe', G(lst), reads=[wn] + ht_res(range(hf * 4, hf * 4 + 4)), writes=psr(pb))
                        Q32 = U[:, par, 0:512]
                        SQ = U[:, par, 512:1024]
                        qn, sn = f'U{par}q', f'U{par}s'
                        op('act', I('activation', out=Q32, in_=bank(pb), func=AF.Identity), reads=psr(pb), writes=[qn])
                        op('act', I('activation', out=SQ, in_=bank(pb), func=AF.Square), reads=psr(pb), writes=[sn])
                        op('pe', I('matmul', bank(pb + 1), lhsT=ONESF[:], rhs=SQ, start=True, stop=True), reads=[sn, 'ONESF'], writes=psr(pb + 1))
                        op('act', I('activation', out=SQ, in_=bank(pb + 1), func=AF.Sqrt, bias=EPSC[:], scale=1.0 / 128), reads=psr(pb + 1) + ['EPSC'], writes=[sn])
                        op('dve', I('reciprocal', out=SQ, in_=SQ), reads=[sn], writes=[sn])
                        dstT = QT[:, h, cols] if isq else KT[:, h, cols]
                        dres = f'QT{h}_{hf}' if isq else f'KT{h}_{hf}'
                        if blk == 0 and isq:
                            op('dve', I('scalar_tensor_tensor', out=dstT, in0=Q32, scalar=gain, in1=SQ, op0=ALU.mult, op1=ALU.mult),
                               reads=[qn, sn, 'QK'], writes=[dres])
                            continue
                        op('dve', I('scalar_tensor_tensor', out=Q32, in0=Q32, scalar=gain, in1=SQ, op0=ALU.mult, op1=ALU.mult),
                           reads=[qn, sn, 'QK'], writes=[qn])
                        if blk == 0:
                            op('act', I('activation', out=dstT, in_=Q32, func=AF.Identity), reads=[qn], writes=[dres])
                            NKS = A1[hf][:, 0:1024].rearrange("p (t c) -> p t c", t=4)
                            lst = [I('transpose', out=PS[:, pb + 2, t * 128:(t + 1) * 128], in_=Q32[:, t * 128:(t + 1) * 128], identity=IDENTF[:]) for t in range(4)]
                            op('pe', G(lst), reads=[qn, 'IDENTF'], writes=psr(pb + 2))
                            op('dve', I('tensor_copy', out=NKS[:, :, h * 128:(h + 1) * 128], in_=bank(pb + 2).rearrange("p (t c) -> p t c", t=4)),
                               reads=psr(pb + 2), writes=[f'NKS{hf}_{h}'])
                        else:
                            op('pe', I('matmul', bank(pb + 2), lhsT=ROT[:], rhs=Q32, start=True, stop=True), reads=[qn, 'ROT'], writes=psr(pb + 2))
                            op('dve', I('tensor_tensor', out=SQ, in0=bank(pb + 2), in1=A1[1][:, cols], op=ALU.mult), reads=psr(pb + 2) + ['SIN', sn], writes=[sn])
                            op('dve', I('tensor_tensor', out=Q32, in0=Q32, in1=A1[0][:, cols], op=ALU.mult), reads=[qn, 'COS'], writes=[qn])
                            op('dve', I('tensor_tensor', out=dstT, in0=Q32, in1=SQ, op=ALU.add), reads=[qn, sn], writes=[dres])
            if ASTG == 1:
                return
            if blk == 0:
                for hf in range(2):
                    dma('sp', nk_out[hf * 512:(hf + 1) * 512, :].rearrange("(t p) c -> p t c", p=128), A1[hf][:, 0:1024].rearrange("p (t c) -> p t c", t=4),
                        [f'NKS{hf}_0', f'NKS{hf}_1'], [f'nk_out{hf}'], f'nk_out{hf}')
            W, wn = ring()
            Wv = W[:, 0:2048].rearrange("p (k n) -> p k n", k=8)
            dma('pool', Wv, wrows(wq, 1280, 256), [], [wn], wn)
            VS = U[:].rearrange("p a n -> p (a n)").rearrange("p (t c) -> p t c", t=8)
            for jj in range(8):
                pb = jj % 2
                lst = [I('matmul', PS[:, pb, 0:256], lhsT=HT[:, k, jj * 128:(jj + 1) * 128], rhs=Wv[:, k, :], start=(k == 0), stop=(k == 7)) for k in range(8)]
                op('pe', G(lst), reads=[wn] + ht_res([jj]), writes=psr(pb))
                if blk == 0:
                    op('dve', I('tensor_copy', out=VS[:, jj, :], in_=PS[:, pb, 0:256]), reads=psr(pb), writes=[f'VS{jj}'])
                    op('act', I('activation', out=VV[:, jj, :], in_=VS[:, jj, :], func=AF.Identity), reads=[f'VS{jj}'], writes=[f'V{jj}'])
                else:
                    op('act', I('activation', out=VV[:, jj, :], in_=PS[:, pb, 0:256], func=AF.Identity), reads=psr(pb), writes=[f'V{jj}'])
            if blk == 0:
                dma('sp', nv_out.rearrange("(t p) c -> p t c", p=128), VS, [f'VS{jj}' for jj in range(8)], ['nv_out'], 'nv_out')
            kres = [f'KT{h}_{hf}' for h in range(2) for hf in range(2)]
            vres = [f'V{jj}' for jj in range(8)]
            if blk == 1:
                ki, ko, vi, vo = k_in.ap(), k_out.ap(), v_in.ap(), v_out.ap()
                dma('sp', ki.rearrange("p (h n) -> p h n", h=2), KT[:, :, 0:1024], kres, ['k_in'], 'k_in')
                dma('sp', vi.rearrange("p (t c) -> p t c", t=8), VV[:, 0:8, :], vres, ['v_in'], 'v_in')
                op('pool', I('collective_compute', "AllGather", ALU.bypass, replica_groups=GROUPS, ins=[ki.opt()], outs=[ko.opt()]), reads=['k_in'], writes=['k_out'])
                op('pool', None, reads=['k_out'])
                op('pool', I('collective_compute', "AllGather", ALU.bypass, replica_groups=GROUPS, ins=[vi.opt()], outs=[vo.opt()]), reads=['v_in'], writes=['v_out'])
                op('pool', None, reads=['v_out'])
                kov = ko.rearrange("(r p) (h n) -> h p r n", p=128, h=2)
                for h in range(2):
                    dma('sp', KT[:, h, 0:4096].rearrange("p (r n) -> p r n", r=4), kov[h], ['k_out'], [f'KTall{h}'], f'KTall{h}')
                dma('sp', VV[:, 0:32, :].rearrange("p (r t) c -> p r t c", r=4), vo.rearrange("(r p) (t c) -> p r t c", p=128, t=8), ['v_out'], ['Vall'], 'Vall')
                W, wn = ring()
                CK = W[:, 0:1024].rearrange("p (t c) -> p t c", t=4)
                dma('pool', CK, cache_k.rearrange("(t p) c -> p t c", p=128), [], [wn], wn)
                dma('pool', VV[:, 32:36, :], cache_v.rearrange("(t p) c -> p t c", p=128), [], ['Vc'], 'Vc')
                ptb = PS[:, 7, :].bitcast(BF16).rearrange("p (k n) -> p k n", k=8)
                lst = [I('transpose', out=ptb[:, t * 2 + h, :], in_=CK[:, t, h * 128:(h + 1) * 128], identity=IDENT[:]) for t in range(4) for h in range(2)]
                op('pe', G(lst), reads=[wn, 'IDENT'], writes=psr(7))
                for h in range(2):
                    op('dve', I('tensor_copy', out=KT[:, h, 4096:4608].rearrange("p (t n) -> p t n", t=4), in_=ptb.rearrange("p (t h) n -> p h t n", h=2)[:, h]),
                       reads=psr(7), writes=[f'KTc{h}'])
                kres = ['KTall0', 'KTall1', 'KTc0', 'KTc1']
                vres = ['Vall', 'Vc']
            if ASTG == 2:
                return
            scale = 128 ** -0.5
            a0 = A1[0][:].bitcast(BF16)
            a1 = A1[1][:].bitcast(BF16)
            PT2 = [a0[:, 0:512], a0[:, 1024:1536], a1[:, 0:512], a1[:, 1024:1536]]
            ptn = ['P0', 'P1', 'P2', 'P3']
            cnt = 0
            if blk == 0:
                units = [(h, s_ * 256, 256, [2 * s_, 2 * s_ + 1], h // 4) for h in range(8) for s_ in range(4)]
            else:
                units = [(h, hf * 512, 512, list(range(36)), h // 4) for h in range(8) for hf in range(2)]
            steps = []
            for ui_, (h, q0, nq, ktiles, kv) in enumerate(units):
                for ti, kt in enumerate(ktiles):
                    steps.append((ui_, ti, kt))
            LA = 3

            def emit_S(si):
                ui_, ti, kt = steps[si]
                h, q0, nq, ktiles, kv = units[ui_]
                sb_ = 4 + (si % 4)
                pi = si % 4
                op('pe', I('matmul', PS[:, sb_, 0:nq], lhsT=KT[:, kv, kt * 128:(kt + 1) * 128], rhs=QT[:, h, q0:q0 + nq], start=True, stop=True),
                   reads=kres + [f'QT{h}_{q0 // 512}'], writes=psr(sb_))
                op('act', I('activation', out=PT2[pi][:, 0:nq], in_=PS[:, sb_, 0:nq], func=AF.Exp, scale=scale), reads=psr(sb_), writes=[ptn[pi]])

            def emit_PV(si):
                ui_, ti, kt = steps[si]
                h, q0, nq, ktiles, kv = units[ui_]
                nkt = len(ktiles)
                pi = si % 4
                pacc = (ui_ % 2) * 2
                op('pe', G([I('matmul', PS[:, pacc, 0:nq], lhsT=VV[:, kt, kv * 128:(kv + 1) * 128], rhs=PT2[pi][:, 0:nq], start=(ti == 0), stop=(ti == nkt - 1)),
                            I('matmul', PS[:, pacc + 1, 0:nq], lhsT=ONES[:], rhs=PT2[pi][:, 0:nq], start=(ti == 0), stop=(ti == nkt - 1))]),
                   reads=vres + [ptn[pi], 'ONES'], writes=psr(pacc, 2))
                if ti == nkt - 1:
                    ui2 = ui_ % 2
                    R = U[:, ui2, 0:nq]
                    op('dve', I('reciprocal', out=R, in_=PS[:, pacc + 1, 0:nq]), reads=psr(pacc + 1), writes=[f'U{ui2}q'])
                    op('dve', I('tensor_tensor', out=QT[:, h, q0:q0 + nq], in0=PS[:, pacc, 0:nq], in1=R, op=ALU.mult), reads=psr(pacc) + [f'U{ui2}q'], writes=[f'OT{h}_{q0}'])
            for i_ in range(len(steps) + LA):
                if i_ < len(steps):
                    emit_S(i_)
                if i_ - LA >= 0:
                    emit_PV(i_ - LA)
            if ASTG == 3:
                return
            ores = [f'OT{h}_{q0}' for (h, q0, nq, kt, kv) in units]
            proj_out(blk, 0, QT, ores, 8, attn_w_o[0])

        def pool_mix(blk):
            nseq, Ls = (4, 256) if blk == 0 else (1, 1024)
            PT_ = BIG[:, 0:8192].rearrange("p (k n) -> p k n", k=8)
            Lp = Ls + 16
            if blk == 1:
                halo_exchange(4, 8)
            W, wn = ring()
            PW = W[:, 0:2048].rearrange("p (g k n) -> p g k n", g=4, k=2)
            dma('pool', PW, pool_w[0].rearrange("g (k p) n -> p g k n", p=128), [], [wn], wn)
            RC = SMALL[:, 0:128].rearrange("p (g s w) -> p g s w", g=4, s=2)
            dma('sp', RC, c_rc[:, blk], [], ['SMALL'], 'SMALL')
            dma('sp', LNG[:], pool_scale[0:1, :].partition_broadcast(128), [], ['LNG'], 'LNG')
            op('dve', I('tensor_tensor', out=GATE[blk][:], in0=GATE[blk][:], in1=LNG[:], op=ALU.mult), reads=['LNG', f'GATE{blk}'], writes=[f'GATE{blk}'])
            load_ln(2, 0)
            for k in range(8):
                g = k // 2
                hh = POOL_HALF[g]
                Pa = A1[0][:, 0:nseq * Lp].rearrange("p (s l) -> p s l", s=nseq)
                Pb = A1[1][:, 0:nseq * Lp].rearrange("p (s l) -> p s l", s=nseq)
                hsrc = HT[:, k, :].rearrange("p (s l) -> p s l", s=nseq)
                if blk == 0:
                    op('pool', G([I('memset', Pa[:, :, 0:8], 0.0), I('memset', Pa[:, :, Ls + 8:Ls + 16], 0.0)]), writes=['A1_0'])
                else:
                    op('dve', I('tensor_copy', out=Pa[:, 0, 0:8], in_=HSEL[:, 0, k, :]), reads=['HSEL0'], writes=['A1_0'])
                    op('dve', I('tensor_copy', out=Pa[:, 0, Ls + 8:Ls + 16], in_=HSEL[:, 1, k, :]), reads=['HSEL1'], writes=['A1_0'])
                op('act', I('activation', out=Pa[:, :, 8:Ls + 8], in_=hsrc, func=AF.Identity), reads=ht_res(range(8), [k]), writes=['A1_0'])
                cur, curn, oth, othn = Pa, 'A1_0', Pb, 'A1_1'
                width = 1
                valid = Lp
                while width < 2 * hh:
                    nv_ = valid - width
                    op('dve', I('tensor_tensor', out=oth[:, :, 0:nv_], in0=cur[:, :, 0:nv_], in1=cur[:, :, width:width + nv_], op=ALU.add),
                       reads=[curn], writes=[othn])
                    cur, curn, oth, othn = oth, othn, cur, curn
                    valid = nv_
                    width *= 2
                Uv = U[:, 0, :].rearrange("p (s l) -> p s l", s=nseq)
                op('dve', I('tensor_scalar', out=Uv, in0=cur[:, :, 8 - hh:8 - hh + Ls], scalar1=1.0 / (2 * hh), scalar2=None, op0=ALU.mult),
                   reads=[curn], writes=['U0'])
                op('dve', I('tensor_tensor', out=Uv[:, :, 0:16], in0=cur[:, :, 8 - hh:8 - hh + 16],
                            in1=RC[:, g, 0, :].unsqueeze(1).to_broadcast([128, nseq, 16]), op=ALU.mult), reads=[curn, 'U0', 'SMALL'], writes=['U0'])
                op('dve', I('tensor_tensor', out=Uv[:, :, Ls - 16:Ls], in0=cur[:, :, 8 - hh + Ls - 16:8 - hh + Ls],
                            in1=RC[:, g, 1, :].unsqueeze(1).to_broadcast([128, nseq, 16]), op=ALU.mult), reads=[curn, 'U0', 'SMALL'], writes=['U0'])
                op('dve', I('tensor_tensor', out=PT_[:, k, :].rearrange("p (s l) -> p s l", s=nseq), in0=Uv, in1=hsrc, op=ALU.subtract),
                   reads=['U0'] + ht_res(range(8), [k]), writes=[f'PT{k}'])
            for tg in range(2):
                for t4 in range(4):
                    jj = tg * 4 + t4
                    lst = [I('matmul', PS[:, t4 * 2 + g // 2, (g % 2) * 256:(g % 2) * 256 + 256], lhsT=PT_[:, 2 * g + kk, jj * 128:(jj + 1) * 128],
                             rhs=PW[:, g, kk, :], start=(kk == 0), stop=(kk == 1)) for g in range(4) for kk in range(2)]
                    op('pe', G(lst), reads=[wn] + [f'PT{k}' for k in range(8)], writes=psr(t4 * 2, 2))
                epilogue4(blk, tg, 0)

        def gmlp(blk, first):
            UT = BIG[:, 0:16384].rearrange("p (c n) -> p c n", c=16)
            GBC = BIG[:, 16384:20480].bitcast(F32)
            CT = BIG[:, 20480:24576].bitcast(F32).rearrange("p (c q) -> p c q", c=16)
            VB = BIG[:, 24576:26624]
            win = gmlp_w_in[0]
            GBT = GB[:, 0:16]
            GLB = GB[:, 16:32]
            if first:
                dma('sp', GBC, gmlp_ln_g[0:1, :].partition_broadcast(128), [], ['GBC'], 'GBC')
                dma('sp', SMALL[:], gmlp_b_s[0:1, :].partition_broadcast(128), [], ['SMALL'], 'SMALL')
                dmas('sp', [(GBT, gbinT), (GLB, glnbT)], [], ['GB'], 'GB')
                dma('pool', HALL[:], wsT, [], ['WST'], 'WST')
                lst = [I('matmul', PS[:, 0, g * 128:(g + 1) * 128], lhsT=ONES[:], rhs=HALL[:, g, :], start=True, stop=True) for g in range(4)]
                op('pe', G(lst), reads=['WST', 'ONES'], writes=psr(0))
                for c in range(16):
                    g = c // 4
                    op('dve', I('scalar_tensor_tensor', out=CT[:, c, :], in0=PS[:, 0, g * 128:(g + 1) * 128], scalar=GLB[:, c:c + 1],
                                in1=SMALL[:, g * 128:(g + 1) * 128], op0=ALU.mult, op1=ALU.add), reads=psr(0) + ['GB', 'SMALL'], writes=[f'CT{c}'])
            BROW = A1[0][0:1, 0:1024].bitcast(BF16)
            dma('pool', BROW, gmlp_b_in[0:1, 2048:4096], [], ['BROW'], 'BROW')
            for cg in range(4):
                W, wn = ring()
                Wv = W[:].rearrange("p (k n) -> p k n", k=8)
                dma('pool', Wv, wrows(win, cg * 512, 512), [], [wn], wn)
                for ci in range(4):
                    c = cg * 4 + ci
                    pb = (c % 2) * 2
                    lst = [I('matmul', bank(pb + hf), lhsT=Wv[:, k, ci * 128:(ci + 1) * 128], rhs=HT[:, k, hf * 512:(hf + 1) * 512], start=(k == 0), stop=(k == 7))
                           for hf in range(2) for k in range(8)]
                    op('pe', G(lst), reads=[wn] + ht_res(range(8)), writes=psr(pb, 2))
                    op('act', I('activation', out=UT[:, c, :], in_=bank2(pb), func=AF.Gelu_apprx_tanh, bias=GBT[:, c:c + 1], scale=1.0),
                       reads=psr(pb, 2) + ['GB'], writes=[f'UT{c}'])
            VL = U[:].rearrange("p a n -> p (a n)")
            for tp in range(4):
                for cb in range(4):
                    W, wn = ring()
                    Wv = W[:].rearrange("p (k n) -> p k n", k=8)
                    dma('pool', Wv, wrows(win, 2048 + cb * 512, 512), [], [wn], wn)
                    for t2 in range(2):
                        jj = tp * 2 + t2
                        lst = [I('matmul', bank(t2 * 4 + cb), lhsT=HT[:, k, jj * 128:(jj + 1) * 128], rhs=Wv[:, k, :], start=(k == 0), stop=False) for k in range(8)]
                        lst.append(I('matmul', bank(t2 * 4 + cb), lhsT=ONES[0:1, :], rhs=BROW[:, cb * 512:(cb + 1) * 512], start=False, stop=True))
                        op('pe', G(lst), reads=[wn, 'BROW', 'ONES'] + ht_res([jj]), writes=psr(t2 * 4 + cb))
                for t2 in range(2):
                    jj = tp * 2 + t2
                    src = PS[:, t2 * 4:t2 * 4 + 4, :].rearrange("p a n -> p (a n)")
                    op('act', I('activation', out=VL, in_=src, func=AF.Gelu_apprx_tanh), reads=psr(t2 * 4, 4), writes=['VL'])
                    ln_stats(VL, ['VL'], t2, width=2048)
                    op('dve', I('tensor_scalar', out=VL, in0=VL, scalar1=MV[t2][:, 0:1], scalar2=RS[t2][:], op0=ALU.subtract, op1=ALU.mult),
                       reads=['VL', f'MV{t2}', f'RS{t2}'], writes=['VL'])
                    op('dve', I('tensor_tensor', out=VB, in0=VL, in1=GBC, op=ALU.mult), reads=['VL', 'GBC'], writes=['VB'])
                    lst = [I('matmul', PS[:, t2 * 4 + c // 4, (c % 4) * 128:(c % 4) * 128 + 128], lhsT=VB[:, c * 128:(c + 1) * 128], rhs=HALL[:, c // 4, :], start=True, stop=True)
                           for c in range(16)]
                    op('pe', G(lst), reads=['VB', 'WST'], writes=psr(t2 * 4, 4))
                    Sx = SMALL[:].rearrange("p (c q) -> p c q", c=4)
                    for gg in range(4):
                        op('dve', I('tensor_tensor', out=Sx, in0=PS[:, t2 * 4 + gg, :].rearrange("p (c q) -> p c q", c=4), in1=CT[:, gg * 4:gg * 4 + 4, :], op=ALU.add),
                           reads=psr(t2 * 4 + gg) + [f'CT{c}' for c in range(gg * 4, gg * 4 + 4)], writes=['SMALL'])
                        utv = UT[:, gg * 4:gg * 4 + 4, jj * 128:(jj + 1) * 128]
                        op('dve', I('tensor_tensor', out=utv, in0=Sx, in1=utv, op=ALU.mult),
                           reads=['SMALL'] + [f'UT{c}' for c in range(gg * 4, gg * 4 + 4)], writes=[f'US{jj}_{gg}'])
            ures = [f'US{jj}_{gg}' for jj in range(8) for gg in range(4)]
            proj_out(blk, 0, UT, ures, 16, gmlp_w_o[0])

        pre_done = False
        for L in range(DEPTH):
            if not pre_done:
                mod_phase(L)
            gates_phase(L, 0)
            load_ln(L, 0)
            if L == 0:
                prenorm(1, 0)
                fnet_a(1)
                prenorm(0, 0)
                fnet_a(0)
                fnet_b(0)
                fnet_b(1)
            else:
                for blk in (1, 0):
                    if not (blk == 1 and pre_done):
                        prenorm(blk, 0)
                    if L == 1:
                        attn(blk)
                    elif L == 2:
                        pool_mix(blk)
                    else:
                        gmlp(blk, blk == 1)
            gates_phase(L, 1)
            load_ln(L, 1)
            prenorm(1, 1)
            ffn_up(1, L)
            prenorm(0, 1)
            ffn_down(1, L)
            ffn_up(0, L)
            pre_done = False
            if L + 1 < DEPTH:
                mod_phase(L + 1)
                prenorm(1, 0)
                pre_done = True
            ffn_down(0, L)

        yv = yout.rearrange("(j p) d -> p j d", p=128)
        outs = ['yout']
        dmas('sp', [(yv[:, j, :], X[:, j, :]) for j in range(16)], [f'X{j}' for j in range(16)], ['yout'], 'yout')
        fin = [r for r in outs + ['nk_out0', 'nk_out1', 'nv_out'] if r in S.res_w]
        op('sp', None, reads=fin)

        names = S.sem_names()
        sems = {n: st.enter_context(nc.semaphore(n)) for n in names}
        with nc.Block() as block:
            @block.tensor
            def _(e):
                S.replay('pe', e, sems)

            @block.scalar
            def _(e):
                S.replay('act', e, sems)

            @block.vector
            def _(e):
                S.replay('dve', e, sems)

            @block.gpsimd
            def _(e):
                S.replay('pool', e, sems)

            @block.sync
            def _(e):
                S.replay('sp', e, sems)
    return nc


def _consts():
    bf = ml_dtypes.bfloat16
    c = {}
    c['c_ident'] = np.eye(128, dtype=np.float32).astype(bf)
    c['c_identf'] = np.eye(128, dtype=np.float32)
    c['c_ones'] = np.ones((128, 128), np.float32).astype(bf)
    c['c_onesf'] = np.ones((128, 128), np.float32)
    rot = np.zeros((128, 128), np.float32)
    for d in range(128):
        i = d % 64
        partner = d + 32 if i < 32 else d - 32
        rot[partner, d] = 1.0
    c['c_rot'] = rot
    n = np.arange(256)
    ang = 2 * np.pi * ((n[:, None] * n[None, :]) % 256) / 256.0
    cs = np.concatenate([np.cos(ang), np.sin(ang)], axis=1)
    c['c_cs'] = np.ascontiguousarray(cs.reshape(2, 128, 512).transpose(1, 0, 2)).astype(np.float32).astype(bf)
    ctp = np.concatenate([np.cos(ang) / 256.0, -np.sin(ang) / 256.0], axis=1)
    c['c_ctp'] = np.ascontiguousarray(ctp.reshape(2, 128, 512).transpose(1, 0, 2)).astype(np.float32).astype(bf)
    return c


def _core_consts(core):
    bf = ml_dtypes.bfloat16
    qi = core % 4
    c = {}
    t = np.arange(4096, dtype=np.int64)
    tp = qi * 1024 + np.arange(1024, dtype=np.int64)
    ang = 2 * np.pi * ((t[:, None] * tp[None, :]) % 4096) / 4096.0
    c['c_cts'] = (np.cos(ang) / 1024.0).astype(np.float32).astype(bf)
    c['c_nsts'] = (-np.sin(ang) / 1024.0).astype(np.float32).astype(bf)
    row = (tp // 64).astype(np.float64)
    col = (tp % 64).astype(np.float64)
    inv = 10000.0 ** (-np.arange(32, dtype=np.float64) / 32)
    cos = np.zeros((128, 1024)); sin = np.zeros((128, 1024))
    for d in range(128):
        pos = row if d < 64 else col
        i = d % 64
        a = pos * inv[i % 32]
        cos[d] = np.cos(a)
        sin[d] = np.sin(a) * (-1.0 if i < 32 else 1.0)
    c['c_cos'] = cos.astype(np.float32)
    c['c_sin'] = sin.astype(np.float32)
    mask = np.zeros((128, 8), np.float32)
    if qi > 0:
        mask[:, qi - 1] = 1.0
    if qi < 3:
        mask[:, 4 + qi + 1] = 1.0
    c['c_mask'] = mask
    rc = np.zeros((2, 4, 2, 16), np.float64)
    for g, hh in enumerate(POOL_HALF):
        T = 256
        tt = np.arange(16)
        rc[0, g, 0] = 1.0 / (np.minimum(tt + hh, T) - np.maximum(tt - hh, 0))
        tt2 = T - 16 + np.arange(16)
        rc[0, g, 1] = 1.0 / (np.minimum(tt2 + hh, T) - np.maximum(tt2 - hh, 0))
        T = 4096
        tt = qi * 1024 + np.arange(16)
        rc[1, g, 0] = 1.0 / (np.minimum(tt + hh, T) - np.maximum(tt - hh, 0))
        tt2 = qi * 1024 + 1024 - 16 + np.arange(16)
        rc[1, g, 1] = 1.0 / (np.minimum(tt2 + hh, T) - np.maximum(tt2 - hh, 0))
    c['c_rc'] = np.broadcast_to(rc.astype(np.float32)[None], (128, 2, 4, 2, 16)).copy()
    return c


_NC_CACHE = {}


def kernel(x_prompt, x_sample, cache_k, cache_v, c, c_ctx, w_mod, b_mod, ln_g, ln_b,
           ffn_w_up, ffn_conv_w, ffn_conv_b, ffn_w_down, fnet_w_o, attn_w_qkv,
           attn_q_norm, attn_k_norm, attn_w_o, pool_w, pool_scale, gmlp_w_in, gmlp_b_in,
           gmlp_ln_g, gmlp_ln_b, gmlp_w_s, gmlp_b_s, gmlp_w_o):
    f = lambda a: np.ascontiguousarray(np.asarray(a, dtype=np.float32))
    x_prompt, x_sample, cache_k, cache_v = f(x_prompt), f(x_sample), f(cache_k), f(cache_v)
    c, c_ctx = f(c), f(c_ctx)
    shared = {
        'w_mod': f(w_mod), 'b_mod': f(b_mod), 'ln_g': f(ln_g), 'ln_b': f(ln_b),
        'ffn_w_up': f(ffn_w_up), 'ffn_w_down': f(ffn_w_down), 'fnet_w_o': f(fnet_w_o),
        'attn_w_qkv': f(attn_w_qkv), 'attn_w_o': f(attn_w_o), 'pool_w': f(pool_w), 'pool_scale': f(pool_scale),
        'gmlp_w_in': f(gmlp_w_in), 'gmlp_b_in': f(gmlp_b_in), 'gmlp_ln_g': f(gmlp_ln_g),
        'gmlp_b_s': f(np.asarray(gmlp_b_s).reshape(1, 512)), 'gmlp_w_o': f(gmlp_w_o),
    }
    shared['bmodT'] = f(np.asarray(b_mod).reshape(DEPTH, 48, 128).transpose(2, 0, 1))
    conv = np.concatenate([np.asarray(ffn_conv_w), np.asarray(ffn_conv_b)[:, None, :]], axis=1)
    shared['convT'] = f(conv.reshape(DEPTH, 4, NFF, 128).transpose(3, 0, 1, 2))
    shared['qkT'] = f(np.stack([np.asarray(attn_q_norm)[0], np.asarray(attn_k_norm)[0]], axis=1))
    shared['gbinT'] = f(np.asarray(gmlp_b_in)[0, :2048].reshape(16, 128).T)
    shared['glnbT'] = f(np.asarray(gmlp_ln_b)[0].reshape(16, 128).T)
    shared['wsT'] = f(np.asarray(gmlp_w_s)[0].transpose(2, 0, 1))
    shared.update(_consts())
    in_maps = []
    for core in range(8):
        b = core // 4
        qi = core % 4
        m = dict(shared)
        xp = x_prompt[4 * core:4 * core + 4].reshape(1024, D)
        xs = x_sample[b, qi * 1024:(qi + 1) * 1024]
        m['xin'] = np.ascontiguousarray(np.concatenate([xp, xs], axis=0))
        cond = np.stack([c_ctx, c[b]], axis=1)
        m['condT'] = f(cond.reshape(8, 128, 2).transpose(1, 0, 2))
        m['cache_k'] = f(cache_k[b, 0].reshape(512, 256))
        m['cache_v'] = f(cache_v[b, 0].reshape(512, 256))
        m.update(_core_consts(core))
        in_maps.append(m)
    if 'nc' not in _NC_CACHE:
        _NC_CACHE['nc'] = build_program()
    nc = _NC_CACHE['nc']
    res = run_bass_kernel_spmd(nc, in_maps, core_ids=list(range(8)))
    r = res.results
    y_prompt = np.zeros((32, 256, D), np.float32)
    y_sample = np.zeros((2, 4096, D), np.float32)
    new_k = np.zeros((32, 1, 256, 2, 128), np.float32)
    new_v = np.zeros((32, 1, 256, 2, 128), np.float32)
    for core in range(8):
        b = core // 4
        qi = core % 4
        y = r[core]['yout']
        y_prompt[4 * core:4 * core + 4] = y[:1024].reshape(4, 256, D)
        y_sample[b, qi * 1024:(qi + 1) * 1024] = y[1024:]
        new_k[4 * core:4 * core + 4, 0] = r[core]['nk_out'].reshape(4, 256, 2, 128)
        new_v[4 * core:4 * core + 4, 0] = r[core]['nv_out'].reshape(4, 256, 2, 128)
    return (y_prompt, y_sample, new_k, new_v)
```

```python
import contextlib
import numpy as np
import ml_dtypes
import concourse.bass as bass
import concourse.mybir as mybir
from concourse.bass_utils import run_bass_kernel_spmd

F32 = mybir.dt.float32
BF16 = mybir.dt.bfloat16
AF = mybir.ActivationFunctionType
ALU = mybir.AluOpType

D = 1024
DFF = 2816
NFF = 22
DEPTH = 4
ALPHA = (2 * DEPTH) ** 0.25
EPS = 1e-6
GROUPS = [[0, 1, 2, 3], [4, 5, 6, 7]]
POOL_HALF = (1, 2, 4, 8)
DEBUG_STOP = None
DEBUG_STAGE = None
DEBUG_LAYERS = None
DEBUG_BLKS = None


class Sched:
    ENG = ['pe', 'act', 'dve', 'pool', 'sp']

    def __init__(self):
        self.ops = {e: [] for e in self.ENG}
        self.clock = {e: {} for e in self.ENG}
        self.cnt = {}
        self.res_w = {}
        self.res_r = {}
        self.alias = {}

    def expand(self, names):
        out = []
        for n in names:
            for m in self.alias.get(n, (n,)):
                if m not in out:
                    out.append(m)
        return out

    def op(self, eng, emit, reads=(), writes=(), dma=None, ndma=1):
        reads = self.expand(reads)
        writes = self.expand(writes)
        clk = self.clock[eng]
        own = 'E_' + eng
        need = {}

        def req(ev):
            if ev is None:
                return
            sem, val, eclk = ev
            if sem == own and eng == 'pe':
                return
            if clk.get(sem, 0) >= val:
                return
            if need.get(sem, (0, None))[0] < val:
                need[sem] = (val, eclk)

        for r in reads:
            req(self.res_w.get(r))
        for w in writes:
            req(self.res_w.get(w))
            for ev in self.res_r.get(w, ()):
                req(ev)
        if dma is not None:
            sem = 'D_' + dma + '_' + eng
            prev = self.cnt.get(sem, 0)
            if prev and clk.get(sem, 0) < prev:
                if need.get(sem, (0, None))[0] < prev:
                    need[sem] = (prev, {})
            inc = 16 * ndma
        else:
            sem = own
            inc = 1
        if emit is None:
            sem = None
        waits = []
        for s, (v, eclk) in need.items():
            if clk.get(s, 0) >= v:
                continue
            waits.append((s, v))
            for k2, v2 in eclk.items():
                if clk.get(k2, 0) < v2:
                    clk[k2] = v2
            if clk.get(s, 0) < v:
                clk[s] = v
        if emit is None:
            self.ops[eng].append((waits, None, None, 0))
            return None
        val = self.cnt.get(sem, 0) + inc
        self.cnt[sem] = val
        self.ops[eng].append((waits, emit, sem, inc, dma is not None and ndma > 1))
        eclk = dict(clk)
        eclk[sem] = val
        ev = (sem, val, eclk)
        for w in writes:
            self.res_w[w] = ev
            self.res_r[w] = []
        for r in reads:
            self.res_r.setdefault(r, []).append(ev)
        return ev

    def sem_names(self):
        return sorted(self.cnt.keys())

    def replay(self, eng, handle, sems):
        for rec in self.ops[eng]:
            waits, emit, sem, inc = rec[:4]
            for s, v in waits:
                handle.wait_ge(sems[s], v)
            if emit is None:
                continue
            if len(rec) > 4 and rec[4]:
                emit(handle, sems[sem])
            else:
                ins = emit(handle)
                ins.then_inc(sems[sem], inc)


def I(name, *a, **kw):
    return lambda e: getattr(e, name)(*a, **kw)


def G(lst):
    def f(e):
        ins = None
        for g in lst:
            ins = g(e)
        return ins
    return f


def build_program():
    nc = bass.Bass("TRN2", target_bir_lowering=False)
    S = Sched()
    op = S.op

    def din(name, shape, dt=F32):
        return nc.dram_tensor(name, list(shape), dt, kind="ExternalInput").ap()

    def dout(name, shape, dt=F32):
        return nc.dram_tensor(name, list(shape), dt, kind="ExternalOutput").ap()

    xin = din("xin", [2048, D])
    condT = din("condT", [128, 8, 2])
    w_mod = din("w_mod", [DEPTH, D, 6 * D])
    bmodT = din("bmodT", [128, DEPTH, 48])
    b_mod = din("b_mod", [DEPTH, 6 * D])
    ln_g = din("ln_g", [DEPTH, 2, D])
    ln_b = din("ln_b", [DEPTH, 2, D])
    ffn_w_up = din("ffn_w_up", [DEPTH, D, 2 * DFF])
    convT = din("convT", [128, DEPTH, 4, NFF])
    ffn_w_down = din("ffn_w_down", [DEPTH, DFF, D])
    fnet_w_o = din("fnet_w_o", [1, D, D])
    attn_w_qkv = din("attn_w_qkv", [1, D, 1536])
    qkT = din("qkT", [128, 2])
    attn_w_o = din("attn_w_o", [1, D, D])
    pool_w = din("pool_w", [1, 4, 256, 256])
    pool_scale = din("pool_scale", [1, D])
    gmlp_w_in = din("gmlp_w_in", [1, D, 4096])
    gbinT = din("gbinT", [128, 16])
    gmlp_b_in = din("gmlp_b_in", [1, 4096])
    gmlp_ln_g = din("gmlp_ln_g", [1, 2048])
    glnbT = din("glnbT", [128, 16])
    wsT = din("wsT", [128, 4, 128])
    gmlp_b_s = din("gmlp_b_s", [1, 512])
    gmlp_w_o = din("gmlp_w_o", [1, 2048, D])
    cache_k = din("cache_k", [512, 256])
    cache_v = din("cache_v", [512, 256])
    c_ident = din("c_ident", [128, 128], BF16)
    c_identf = din("c_identf", [128, 128])
    c_ones = din("c_ones", [128, 128], BF16)
    c_onesf = din("c_onesf", [128, 128])
    c_rot = din("c_rot", [128, 128])
    c_cs = din("c_cs", [128, 2, 512], BF16)
    c_ctp = din("c_ctp", [128, 2, 512], BF16)
    c_cts = din("c_cts", [4096, 1024], BF16)
    c_nsts = din("c_nsts", [4096, 1024], BF16)
    c_cos = din("c_cos", [128, 1024])
    c_sin = din("c_sin", [128, 1024])
    c_mask = din("c_mask", [128, 8])
    c_rc = din("c_rc", [128, 2, 4, 2, 16])

    yout = dout("yout", [2048, D])
    nk_out = dout("nk_out", [1024, 256])
    nv_out = dout("nv_out", [1024, 256])

    halo_w = [1, 1, 1, 1, 8]
    halo_in = [nc.dram_tensor(f"halo_in{i}", [128, 16 * w_], BF16) for i, w_ in enumerate(halo_w)]
    halo_out = [nc.dram_tensor(f"halo_out{i}", [512, 16 * w_], BF16) for i, w_ in enumerate(halo_w)]
    ab_in = [nc.dram_tensor(f"ab_in{j}", [256, 2048], BF16) for j in range(4)]
    ab_out = [nc.dram_tensor(f"ab_out{j}", [1024, 2048], BF16) for j in range(4)]
    k_in = nc.dram_tensor("k_in", [128, 2048], BF16)
    k_out = nc.dram_tensor("k_out", [512, 2048], BF16)
    v_in = nc.dram_tensor("v_in", [128, 2048], BF16)
    v_out = nc.dram_tensor("v_out", [512, 2048], BF16)

    st = contextlib.ExitStack()
    with st:
        def sb(name, shape, dt=F32):
            return st.enter_context(nc.sbuf_tensor(name, list(shape), dt))

        X = sb("X", [128, 16, D])
        HT = sb("HT", [128, 8, 1024], BF16)
        BIG = sb("BIG", [128, 26624], BF16)
        WR = [sb(f"WR{i}", [128, 4096], BF16) for i in range(3)]
        GATE = [sb(f"GATE{g}", [128, D]) for g in range(2)]
        LNG = sb("LNG", [128, D])
        LNB = sb("LNB", [128, D])
        A1 = [sb(f"A1_{i}", [128, 1088]) for i in range(2)]
        U = sb("U", [128, 2, 1024])
        XNB = [sb(f"XNB{i}", [128, 1024], BF16) for i in range(2)]
        MODT = sb("MODT", [128, 4, 8, 2])
        BMT = sb("BMT", [128, DEPTH, 48])
        CONV = sb("CONV", [128, DEPTH, 4, NFF])
        CST = sb("CST", [128, 8, 2])
        CSB = sb("CSB", [128, 8, 2], BF16)
        IDENT = sb("IDENT", [128, 128], BF16)
        IDENTF = sb("IDENTF", [128, 128])
        ONES = sb("ONES", [128, 128], BF16)
        ONESF = sb("ONESF", [128, 128])
        ROT = sb("ROT", [128, 128])
        MASK = sb("MASK", [128, 8])
        STATS = [sb(f"STATS{i}", [128, 4, 6]) for i in range(8)]
        MV = [sb(f"MV{i}", [128, 2]) for i in range(8)]
        RS = [sb(f"RS{i}", [128, 1]) for i in range(8)]
        NMR = [sb(f"NMR{i}", [128, 1]) for i in range(8)]
        EPSC = sb("EPSC", [128, 1])
        HALL = sb("HALL", [128, 4, 128], BF16)
        HSEL = sb("HSEL", [128, 2, 8, 8])
        HTH = sb("HTH", [128, 8, 2], BF16)
        QK = sb("QK", [128, 2])
        SMALL = sb("SMALL", [128, 512])
        GB = sb("GB", [128, 48])
        PS = st.enter_context(nc.psum_tensor("PS", [128, 8, 512], F32))

        al = S.alias
        al['U0'] = ['U0a', 'U0b']; al['U1'] = ['U1a', 'U1b']
        al['U0q'] = ['U0a']; al['U0s'] = ['U0b']; al['U1q'] = ['U1a']; al['U1s'] = ['U1b']
        al['VL'] = ['U0a', 'U0b', 'U1a', 'U1b']
        for jj in range(8):
            al[f'VS{jj}'] = [['U0a', 'U0b', 'U1a', 'U1b'][jj // 2]]
        for i in range(2):
            al[f'A1_{i}'] = [f'A1_{i}a', f'A1_{i}b', f'A1_{i}x']
            for h in range(2):
                al[f'NKS{i}_{h}'] = al[f'A1_{i}']
        al['COS'] = ['A1_0a', 'A1_0b']; al['SIN'] = ['A1_1a', 'A1_1b']
        al['P0'] = ['A1_0a']; al['P1'] = ['A1_0b']; al['P2'] = ['A1_1a']; al['P3'] = ['A1_1b']
        al['WST'] = ['HALL']
        al['BROW'] = ['A1_0a', 'A1_0b']

        def bu(off, ln):
            return [f'B{u}' for u in range(off // 1024, (off + ln + 1023) // 1024)]
        for c in range(NFF):
            al[f'GT{c}'] = bu(c * 1024, 1024)
        for jj in range(8):
            al[f'AB{jj}'] = bu(jj * 2048, 2048)
            al[f'FT{jj}'] = bu(16384 + jj * 1024, 1024)
            al[f'PT{jj}'] = bu(jj * 1024, 1024)
            al[f'V{jj}'] = bu(17408 + jj * 256, 256)
        for h in range(8):
            for hf in range(2):
                al[f'QT{h}_{hf}'] = bu(h * 1024 + hf * 512, 512)
            for q0 in range(0, 1024, 256):
                al[f'OT{h}_{q0}'] = bu(h * 1024 + q0, 256)
        for h in range(2):
            for hf in range(2):
                al[f'KT{h}_{hf}'] = bu(8192 + h * 4608 + hf * 512, 512)
            al[f'KTall{h}'] = bu(8192 + h * 4608, 4096)
            al[f'KTc{h}'] = bu(8192 + h * 4608 + 4096, 512)
        al['Vall'] = bu(17408, 32 * 256)
        al['Vc'] = bu(17408 + 32 * 256, 1024)
        for c in range(16):
            al[f'UT{c}'] = bu(c * 1024, 1024)
            al[f'CT{c}'] = bu(20480 + c * 256, 256)
        for jj in range(8):
            for gg in range(4):
                al[f'US{jj}_{gg}'] = bu(gg * 4096, 4096)
        al['GBC'] = bu(16384, 4096)
        al['VB'] = bu(24576, 2048)

        def bank(b):
            return PS[:, b, :]

        def bank2(b):
            return PS[:, b:b + 2, :].rearrange("p a n -> p (a n)")

        def psr(b, n=1):
            return [f'PS{i}' for i in range(b, b + n)]

        ring_i = [0]

        def ring():
            i = ring_i[0] % 3
            ring_i[0] += 1
            return WR[i], f'WR{i}'

        def dma(eng, out, in_, reads, writes, key):
            op(eng, I('dma_start', out=out, in_=in_), reads=reads, writes=writes, dma=key)

        def dmas(eng, pairs, reads, writes, key, slow=False):
            def emit(e, semh, pairs=tuple(pairs)):
                for (o, i_) in pairs:
                    e.dma_start(out=o, in_=i_, allow_slow_non_contiguous=slow).then_inc(semh, 16)
            op(eng, emit, reads=reads, writes=writes, dma=key, ndma=len(pairs))

        def wrows(src2d, c0, ncols):
            return src2d[:, c0:c0 + ncols].rearrange("(k p) n -> p k n", p=128)

        xin_v = xin.rearrange("(j p) d -> p j d", p=128)
        dmas('sp', [(X[:, j, :], xin_v[:, j, :]) for j in range(16)], [], [f'X{j}' for j in range(16)], 'XIN')
        for (t, src, nm) in [(CST, condT, 'CST'), (BMT, bmodT, 'BMT'), (CONV, convT, 'CONV'), (IDENT, c_ident, 'IDENT'),
                             (IDENTF, c_identf, 'IDENTF'), (ONES, c_ones, 'ONES'), (ONESF, c_onesf, 'ONESF'),
                             (ROT, c_rot, 'ROT'), (MASK, c_mask, 'MASK'), (QK, qkT, 'QK')]:
            dma('sp', t[:], src, [], [nm], nm)
        op('dve', I('memset', EPSC[:], EPS), writes=['EPSC'])
        op('act', I('activation', out=CST[:], in_=CST[:], func=AF.Silu), reads=['CST'], writes=['CST'])
        op('dve', I('tensor_copy', out=CSB[:], in_=CST[:]), reads=['CST'], writes=['CSB'])

        def mod_phase(L):
            for vi, v in enumerate((0, 1, 3, 4)):
                for half in range(2):
                    W, wn = ring()
                    Wv = W[:].rearrange("p (k n) -> p k n", k=8)
                    dma('pool', Wv, wrows(w_mod[L], v * 1024 + half * 512, 512), [], [wn], wn)
                    lst = [I('matmul', PS[0:2, 0, :], lhsT=CSB[:, k, :], rhs=Wv[:, k, :], start=(k == 0), stop=(k == 7)) for k in range(8)]
                    op('pe', G(lst), reads=[wn, 'CSB'], writes=['PS0'])
                    op('act', I('activation', out=SMALL[0:2, :], in_=PS[0:2, 0, :], func=AF.Identity), reads=['PS0'], writes=['SMALL'])
                    lst = [I('transpose', out=PS[:, 1, cc * 2:cc * 2 + 2], in_=SMALL[0:2, cc * 128:(cc + 1) * 128], identity=IDENTF[0:2, 0:2]) for cc in range(4)]
                    op('pe', G(lst), reads=['SMALL', 'IDENTF'], writes=['PS1'])
                    bias_ap = BMT[:, L, v * 8 + half * 4:v * 8 + half * 4 + 4].unsqueeze(2).to_broadcast([128, 4, 2])
                    src = PS[:, 1, 0:8].rearrange("p (c g) -> p c g", g=2)
                    dst = MODT[:, vi, half * 4:half * 4 + 4, :]
                    if v in (1, 4):
                        op('dve', I('scalar_tensor_tensor', out=dst, in0=src, scalar=1.0, in1=bias_ap, op0=ALU.add, op1=ALU.add), reads=['PS1', 'BMT'], writes=['MODT'])
                    else:
                        op('dve', I('tensor_tensor', out=dst, in0=src, in1=bias_ap, op=ALU.add), reads=['PS1', 'BMT'], writes=['MODT'])

        def gates_phase(L, s_):
            v = (2, 5)[s_]
            CSREP = [XNB[g][:].rearrange("p (k n) -> p k n", k=8) for g in range(2)]
            for g in range(2):
                op('dve', I('tensor_copy', out=CSREP[g], in_=CST[:, :, g:g + 1].to_broadcast([128, 8, 128])), reads=['CST'], writes=[f'XNB{g}'])
            for half in range(2):
                W, wn = ring()
                Wv = W[:].rearrange("p (k n) -> p k n", k=8)
                c0 = v * 1024 + half * 512
                dma('pool', Wv, wrows(w_mod[L], c0, 512), [], [wn], wn)
                dma('sp', SMALL[:], b_mod[L:L + 1, c0:c0 + 512].partition_broadcast(128), [], ['SMALL'], 'SMALL')
                for g in range(2):
                    lst = [I('matmul', bank(1 + g), lhsT=CSREP[g][:, k, :], rhs=Wv[:, k, :], start=(k == 0), stop=(k == 7)) for k in range(8)]
                    op('pe', G(lst), reads=[wn, f'XNB{g}'], writes=psr(1 + g))
                    op('dve', I('tensor_tensor', out=GATE[g][:, half * 512:(half + 1) * 512], in0=bank(1 + g), in1=SMALL[:], op=ALU.add),
                       reads=psr(1 + g) + ['SMALL'], writes=[f'GATE{g}'])

        def load_ln(L, sub):
            dma('sp', LNG[:], ln_g[L, sub:sub + 1, :].partition_broadcast(128), [], ['LNG'], 'LNG')
            dma('sp', LNB[:], ln_b[L, sub:sub + 1, :].partition_broadcast(128), [], ['LNB'], 'LNB')

        def ln_stats(src_ap, res, i, width=1024):
            nch = width // 512
            lst = [I('bn_stats', out=STATS[i][:, c, :], in_=src_ap[:, c * 512:(c + 1) * 512]) for c in range(nch)]
            op('dve', G(lst), reads=res, writes=[f'STATS{i}'])
            op('dve', I('bn_aggr', out=MV[i][:], in_=STATS[i][:, 0:nch, :]), reads=[f'STATS{i}'], writes=[f'MV{i}'])
            op('act', I('activation', out=RS[i][:], in_=MV[i][:, 1:2], func=AF.Sqrt, bias=EPSC[:], scale=1.0),
               reads=[f'MV{i}', 'EPSC'], writes=[f'RS{i}'])
            op('dve', I('reciprocal', out=RS[i][:], in_=RS[i][:]), reads=[f'RS{i}'], writes=[f'RS{i}'])

        def ht_res(tiles, ks=range(8)):
            return [f'HT{jj}_{k}' for jj in tiles for k in ks]

        def prenorm(blk, sub):
            g = blk
            for j0 in range(0, 8, 2):
                tiles = [(jj, blk * 8 + jj, jj % 2) for jj in (j0, j0 + 1)]
                for (jj, j, i) in tiles:
                    lst = [I('bn_stats', out=STATS[i][:, c, :], in_=X[:, j, c * 512:(c + 1) * 512]) for c in range(2)]
                    op('dve', G(lst), reads=[f'X{j}'], writes=[f'STATS{i}'])
                for (jj, j, i) in tiles:
                    op('dve', I('bn_aggr', out=MV[i][:], in_=STATS[i][:, 0:2, :]), reads=[f'STATS{i}'], writes=[f'MV{i}'])
                for (jj, j, i) in tiles:
                    op('act', I('activation', out=RS[i][:], in_=MV[i][:, 1:2], func=AF.Sqrt, bias=EPSC[:], scale=1.0), reads=[f'MV{i}', 'EPSC'], writes=[f'RS{i}'])
                for (jj, j, i) in tiles:
                    op('dve', I('reciprocal', out=RS[i][:], in_=RS[i][:]), reads=[f'RS{i}'], writes=[f'RS{i}'])
                for (jj, j, i) in tiles:
                    op('dve', I('scalar_tensor_tensor', out=NMR[i][:], in0=MV[i][:, 0:1], scalar=-1.0, in1=RS[i][:], op0=ALU.mult, op1=ALU.mult),
                       reads=[f'MV{i}', f'RS{i}'], writes=[f'NMR{i}'])
                for (jj, j, i) in tiles:
                    op('act', I('activation', out=XNB[i][:], in_=X[:, j, :], func=AF.Identity, scale=RS[i][:], bias=NMR[i][:]),
                       reads=[f'X{j}', f'RS{i}', f'NMR{i}'], writes=[f'XNB{i}'])
                for (jj, j, i) in tiles:
                    pb = 6 + i
                    pt = PS[:, pb, :].bitcast(BF16).rearrange("p (k n) -> p k n", k=8)
                    lst = [I('transpose', out=pt[:, k, :], in_=XNB[i][:, k * 128:(k + 1) * 128], identity=IDENT[:]) for k in range(8)]
                    op('pe', G(lst), reads=[f'XNB{i}', 'IDENT'], writes=psr(pb))
                for (jj, j, i) in tiles:
                    pb = 6 + i
                    pt = PS[:, pb, :].bitcast(BF16).rearrange("p (k n) -> p k n", k=8)
                    for k in range(8):
                        sc = MODT[:, 2 * sub + 1, k, g:g + 1]
                        sh = MODT[:, 2 * sub, k, g:g + 1]
                        dst = HT[:, k, jj * 128:(jj + 1) * 128]
                        op('dve', I('tensor_scalar', out=dst, in0=pt[:, k, :], scalar1=sc, scalar2=sh, op0=ALU.mult, op1=ALU.add),
                           reads=psr(pb) + ['MODT'], writes=[f'HT{jj}_{k}'])

        def epilogue4(blk, tg, sub, tiles=None):
            g = blk
            Tb = [U[:, 0, :], U[:, 1, :], A1[0][:, 0:1024], A1[1][:, 0:1024]]
            Tn = ['U0', 'U1', 'A1_0', 'A1_1']
            if tiles is None:
                tiles = [tg * 4 + t4 for t4 in range(4)]
            tl = [(4 + t4, blk * 8 + jj, Tb[t4], Tn[t4], t4 * 2) for t4, jj in enumerate(tiles)]
            for (i, j, T, tn, pb) in tl:
                op('dve', I('tensor_tensor', out=T, in0=bank2(pb), in1=GATE[g][:], op=ALU.mult), reads=psr(pb, 2) + [f'GATE{g}'], writes=[tn])
            for (i, j, T, tn, pb) in tl:
                op('dve', I('scalar_tensor_tensor', out=T, in0=X[:, j, :], scalar=ALPHA, in1=T, op0=ALU.mult, op1=ALU.add), reads=[tn, f'X{j}'], writes=[tn])
            for (i, j, T, tn, pb) in tl:
                lst = [I('bn_stats', out=STATS[i][:, c, :], in_=T[:, c * 512:(c + 1) * 512]) for c in range(2)]
                op('dve', G(lst), reads=[tn], writes=[f'STATS{i}'])
            for (i, j, T, tn, pb) in tl:
                op('dve', I('bn_aggr', out=MV[i][:], in_=STATS[i][:, 0:2, :]), reads=[f'STATS{i}'], writes=[f'MV{i}'])
            for (i, j, T, tn, pb) in tl:
                op('act', I('activation', out=RS[i][:], in_=MV[i][:, 1:2], func=AF.Sqrt, bias=EPSC[:], scale=1.0), reads=[f'MV{i}', 'EPSC'], writes=[f'RS{i}'])
            for (i, j, T, tn, pb) in tl:
                op('dve', I('reciprocal', out=RS[i][:], in_=RS[i][:]), reads=[f'RS{i}'], writes=[f'RS{i}'])
            for (i, j, T, tn, pb) in tl:
                op('dve', I('scalar_tensor_tensor', out=NMR[i][:], in0=MV[i][:, 0:1], scalar=-1.0, in1=RS[i][:], op0=ALU.mult, op1=ALU.mult),
                   reads=[f'MV{i}', f'RS{i}'], writes=[f'NMR{i}'])
            for (i, j, T, tn, pb) in tl:
                op('act', I('activation', out=T, in_=T, func=AF.Identity, scale=RS[i][:], bias=NMR[i][:]), reads=[tn, f'RS{i}', f'NMR{i}'], writes=[tn])
            for (i, j, T, tn, pb) in tl:
                op('dve', I('tensor_tensor', out=T, in0=T, in1=LNG[:], op=ALU.mult), reads=[tn, 'LNG'], writes=[tn])
            for (i, j, T, tn, pb) in tl:
                op('dve', I('tensor_tensor', out=X[:, j, :], in0=T, in1=LNB[:], op=ALU.add), reads=[tn, 'LNB'], writes=[f'X{j}'])

        def wd_load(Wd, k0, kn):
            W, wn = ring()
            Wv = W[:].rearrange("p (k n) -> p k n", k=4)
            dma('pool', Wv[:, 0:kn, :], Wd[k0 * 128:(k0 + kn) * 128, :].rearrange("(k p) n -> p k n", p=128), [], [wn], wn)
            return Wv, wn

        def proj_out(blk, sub, FT, ft_res, KC, Wd, gsz=4, preloaded=()):
            kgroups = [(k0, min(4, KC - k0)) for k0 in range(0, KC, 4)]
            tgroups = [list(range(t0, min(8, t0 + gsz))) for t0 in range(0, 8, gsz)]
            preloaded = list(preloaded)
            for tiles in tgroups:
                for (k0, kn) in kgroups:
                    if preloaded:
                        Wv, wn = preloaded.pop(0)
                    else:
                        Wv, wn = wd_load(Wd, k0, kn)
                    for t4, jj in enumerate(tiles):
                        lst = []
                        for hf in range(2):
                            for kk in range(kn):
                                lst.append(I('matmul', bank(t4 * 2 + hf), lhsT=FT[:, k0 + kk, jj * 128:(jj + 1) * 128],
                                             rhs=Wv[:, kk, hf * 512:(hf + 1) * 512], start=(k0 + kk == 0), stop=(k0 + kk == KC - 1)))
                        op('pe', G(lst), reads=[wn] + ft_res, writes=psr(t4 * 2, 2))
                epilogue4(blk, 0, sub, tiles)

        def halo_exchange(idx, w):
            hi = halo_in[idx].ap()
            ho = halo_out[idx].ap()
            hiv = hi.rearrange("p (s k w) -> p s k w", s=2, k=8)
            dmas('sp', [(hiv[:, 0, :, :], HT[:, :, 0:w]), (hiv[:, 1, :, :], HT[:, :, 1024 - w:1024])],
                 ht_res([0, 7]), [f'halo_in{idx}'], f'hin{idx}', slow=True)
            op('pool', I('collective_compute', "AllGather", ALU.bypass, replica_groups=GROUPS, ins=[hi.opt()], outs=[ho.opt()]),
               reads=[f'halo_in{idx}'], writes=[f'halo_out{idx}'])
            op('pool', None, reads=[f'halo_out{idx}'])
            dma('sp', HALL[:, :, 0:16 * w], ho.rearrange("(r p) c -> p r c", p=128), [f'halo_out{idx}'], ['HALL'], 'HALL')
            hv = HALL[:, :, 0:16 * w].rearrange("p r (s k w) -> p r s k w", s=2, k=8)
            for side in range(2):
                src_s = 1 - side
                for r in range(4):
                    m = MASK[:, side * 4 + r:side * 4 + r + 1]
                    if r == 0:
                        op('dve', I('tensor_scalar', out=HSEL[:, side, :, 0:w], in0=hv[:, r, src_s, :, :], scalar1=m, scalar2=None, op0=ALU.mult),
                           reads=['HALL', 'MASK'], writes=[f'HSEL{side}'])
                    else:
                        op('dve', I('scalar_tensor_tensor', out=HSEL[:, side, :, 0:w], in0=hv[:, r, src_s, :, :], scalar=m,
                                    in1=HSEL[:, side, :, 0:w], op0=ALU.mult, op1=ALU.add),
                           reads=['HALL', 'MASK', f'HSEL{side}'], writes=[f'HSEL{side}'])

        def ffn_up(blk, L):
            sub = 1
            nseq, Ls = (4, 256) if blk == 0 else (1, 1024)
            GT = BIG[:, 0:NFF * 1024].rearrange("p (c n) -> p c n", c=NFF)
            if blk == 1:
                halo_exchange(L, 1)
                op('dve', I('tensor_copy', out=HTH[:, :, 0:1], in_=HSEL[:, 0, :, 0:1]), reads=['HSEL0'], writes=['HTH'])
                op('dve', I('tensor_copy', out=HTH[:, :, 1:2], in_=HSEL[:, 1, :, 0:1]), reads=['HSEL1', 'HTH'], writes=['HTH'])
            pair = 0
            for cg in range(11):
                W, wn = ring()
                Wv = W[:].rearrange("p (k n) -> p k n", k=8)
                c0 = cg * 256
                dmas('pool', [(Wv[:, :, 0:256], wrows(ffn_w_up[L], c0, 256)), (Wv[:, :, 256:512], wrows(ffn_w_up[L], DFF + c0, 256))], [], [wn], wn)
                for ci in range(2):
                    c = cg * 2 + ci
                    pa = (pair % 3) * 2
                    pbk = ((pair + 1) % 3) * 2
                    pair += 2
                    ai = c % 2
                    A = A1[ai]
                    an = f'A1_{ai}'
                    Av = A[:, 0:nseq * (Ls + 2)].rearrange("p (s l) -> p s l", s=nseq)
                    lst = [I('matmul', bank(pa + hf), lhsT=Wv[:, k, ci * 128:(ci + 1) * 128], rhs=HT[:, k, hf * 512:(hf + 1) * 512],
                             start=(k == 0), stop=(k == 7)) for hf in range(2) for k in range(8)]
                    op('pe', G(lst), reads=[wn] + ht_res(range(8)), writes=psr(pa, 2))
                    if blk == 1:
                        lst = [I('matmul', PS[:, 6, 0:2], lhsT=Wv[:, k, ci * 128:(ci + 1) * 128], rhs=HTH[:, k, :], start=(k == 0), stop=(k == 7)) for k in range(8)]
                        op('pe', G(lst), reads=[wn, 'HTH'], writes=['PS6'])
                    lst = [I('matmul', bank(pbk + hf), lhsT=Wv[:, k, 256 + ci * 128:256 + (ci + 1) * 128], rhs=HT[:, k, hf * 512:(hf + 1) * 512],
                             start=(k == 0), stop=(k == 7)) for hf in range(2) for k in range(8)]
                    op('pe', G(lst), reads=[wn] + ht_res(range(8)), writes=psr(pbk, 2))
                    a_ps = bank2(pa).rearrange("p (s l) -> p s l", s=nseq)
                    b_ps = bank2(pbk)
                    w0 = CONV[:, L, 0, c:c + 1]
                    w1 = CONV[:, L, 1, c:c + 1]
                    w2 = CONV[:, L, 2, c:c + 1]
                    cb = CONV[:, L, 3, c:c + 1]
                    Ui = U[:, c % 2, :]
                    un = f'U{c % 2}'
                    Uv = Ui.rearrange("p (s l) -> p s l", s=nseq)
                    if blk == 0:
                        op('pool', G([I('memset', Av[:, :, 0:1], 0.0), I('memset', Av[:, :, Ls + 1:Ls + 2], 0.0)]), writes=[an])
                    else:
                        op('act', G([I('activation', out=Av[:, 0, 0:1], in_=PS[:, 6, 0:1], func=AF.Identity),
                                     I('activation', out=Av[:, 0, Ls + 1:Ls + 2], in_=PS[:, 6, 1:2], func=AF.Identity)]), reads=['PS6'], writes=[an])
                    op('act', I('activation', out=Av[:, :, 1:Ls + 1], in_=a_ps, func=AF.Identity), reads=psr(pa, 2), writes=[an])
                    op('act', I('activation', out=Uv, in_=a_ps, func=AF.Identity, scale=w1, bias=cb), reads=psr(pa, 2) + ['CONV'], writes=[un])
                    op('dve', I('scalar_tensor_tensor', out=Uv, in0=Av[:, :, 0:Ls], scalar=w0, in1=Uv, op0=ALU.mult, op1=ALU.add),
                       reads=[an, un, 'CONV'], writes=[un])
                    op('dve', I('scalar_tensor_tensor', out=Uv, in0=Av[:, :, 2:Ls + 2], scalar=w2, in1=Uv, op0=ALU.mult, op1=ALU.add),
                       reads=[an, un, 'CONV'], writes=[un])
                    op('act', I('activation', out=Ui, in_=Ui, func=AF.Gelu_apprx_tanh), reads=[un], writes=[un])
                    op('dve', I('tensor_tensor', out=GT[:, c, :], in0=Ui, in1=b_ps, op=ALU.mult), reads=[un] + psr(pbk, 2), writes=[f'GT{c}'])

        def ffn_down(blk, L):
            GT = BIG[:, 0:NFF * 1024].rearrange("p (c n) -> p c n", c=NFF)
            proj_out(blk, 1, GT, [f'GT{c}' for c in range(NFF)], NFF, ffn_w_down[L], gsz=3)

        def fnet_a(blk):
            ABs = BIG[:, 0:16384].rearrange("p (j c) -> p j c", j=8)
            CSs = SMALL[:].bitcast(BF16).rearrange("p (k n) -> p k n", k=2)
            dma('sp', CSs, c_cs, [], ['SMALL'], 'SMALL')
            for jj in range(8):
                lst = [I('matmul', bank(g), lhsT=HT[:, 2 * g + kk, jj * 128:(jj + 1) * 128], rhs=CSs[:, kk, :], start=(kk == 0), stop=(kk == 1))
                       for g in range(4) for kk in range(2)]
                op('pe', G(lst), reads=ht_res([jj]) + ['SMALL'], writes=psr(0, 4))
                src = PS[:, 0:4, :].rearrange("p a n -> p (a n)")
                if jj % 2 == 0:
                    op('act', I('activation', out=ABs[:, jj, :], in_=src, func=AF.Identity), reads=psr(0, 4), writes=[f'AB{jj}'])
                else:
                    op('dve', I('tensor_copy', out=ABs[:, jj, :], in_=src), reads=psr(0, 4), writes=[f'AB{jj}'])
                if blk == 1 and jj % 2 == 1:
                    q_ = jj // 2
                    abi = ab_in[q_].ap()
                    abo = ab_out[q_].ap()
                    dma('sp', abi.rearrange("(t p) c -> p t c", p=128), ABs[:, jj - 1:jj + 1, :], [f'AB{jj - 1}', f'AB{jj}'], [f'ab_in{q_}'], f'ab_in{q_}')
                    op('pool', I('collective_compute', "AllGather", ALU.bypass, replica_groups=GROUPS, ins=[abi.opt()], outs=[abo.opt()]),
                       reads=[f'ab_in{q_}'], writes=[f'ab_out{q_}'])
                    op('pool', None, reads=[f'ab_out{q_}'])

        def fnet_b(blk, preloaded=()):
            ABs = BIG[:, 0:16384].rearrange("p (j c) -> p j c", j=8)
            FT = BIG[:, 16384:24576].rearrange("p (k n) -> p k n", k=8)

            def evac(m, mi):
                if m % 2 == 0:
                    op('act', I('activation', out=FT[:, m, :], in_=bank2(mi * 2), func=AF.Identity), reads=psr(mi * 2, 2), writes=[f'FT{m}'])
                else:
                    op('dve', I('tensor_copy', out=FT[:, m, :], in_=bank2(mi * 2)), reads=psr(mi * 2, 2), writes=[f'FT{m}'])
            if blk == 0:
                W, wn = ring()
                CTP = W[:, 0:1024].rearrange("p (k n) -> p k n", k=2)
                dma('sp', CTP, c_ctp, [], [wn], wn)
                for mg in range(2):
                    for mi in range(4):
                        m = mg * 4 + mi
                        g, mm_ = m // 2, m % 2
                        lst = []
                        for s_ in range(4):
                            n = 0
                            for kk in range(2):
                                for ab in range(2):
                                    c0 = g * 512 + ab * 256 + mm_ * 128
                                    lst.append(I('matmul', PS[:, mi * 2 + s_ // 2, (s_ % 2) * 256:(s_ % 2) * 256 + 256],
                                                 lhsT=ABs[:, s_ * 2 + kk, c0:c0 + 128], rhs=CTP[:, kk, ab * 256:(ab + 1) * 256],
                                                 start=(n == 0), stop=(n == 3)))
                                    n += 1
                        op('pe', G(lst), reads=[wn] + [f'AB{j}' for j in range(8)], writes=psr(mi * 2, 2))
                        evac(m, mi)
            else:
                for mg in range(2):
                    for tc in range(32):
                        r_, j_ = tc // 8, tc % 8
                        W, wn = ring()
                        Aw = W[:, 0:1024].rearrange("p (g n) -> p g n", g=2)
                        Cw = W[:, 1024:2048]
                        Sw = W[:, 2048:3072]
                        q_, t_ = j_ // 2, j_ % 2
                        r0 = r_ * 256 + t_ * 128
                        dmas('sp', [(Aw, ab_out[q_].ap()[r0:r0 + 128, mg * 1024:(mg + 1) * 1024].rearrange("p (g n) -> p g n", g=2)),
                                    (Cw, c_cts[tc * 128:(tc + 1) * 128, :]), (Sw, c_nsts[tc * 128:(tc + 1) * 128, :])], [f'ab_out{q_}'], [wn], wn)
                        lst = []
                        for mi in range(4):
                            gl, mm_ = mi // 2, mi % 2
                            for hf in range(2):
                                for ab in range(2):
                                    rhs = (Cw if ab == 0 else Sw)[:, hf * 512:(hf + 1) * 512]
                                    lst.append(I('matmul', bank(mi * 2 + hf), lhsT=Aw[:, gl, ab * 256 + mm_ * 128:ab * 256 + (mm_ + 1) * 128], rhs=rhs,
                                                 start=(tc == 0 and ab == 0), stop=(tc == 31 and ab == 1)))
                        op('pe', G(lst), reads=[wn], writes=psr(0, 8))
                    for mi in range(4):
                        evac(mg * 4 + mi, mi)
            proj_out(blk, 0, FT, [f'FT{m}' for m in range(8)], 8, fnet_w_o[0], preloaded=preloaded)

        def attn(blk):
            ASTG = (DEBUG_STAGE or 99) % 10
            if DEBUG_STAGE is not None and DEBUG_STAGE < 10 and blk == 0:
                return
            QT = BIG[:, 0:8192].rearrange("p (h n) -> p h n", h=8)
            KT = BIG[:, 8192:8192 + 2 * 4608].rearrange("p (h n) -> p h n", h=2)
            VV = BIG[:, 17408:17408 + 36 * 256].rearrange("p (t c) -> p t c", t=36)
            wq = attn_w_qkv[0]
            if blk == 1:
                dma('sp', A1[0][:, 0:1024], c_cos, [], ['COS'], 'COS')
                dma('sp', A1[1][:, 0:1024], c_sin, [], ['SIN'], 'SIN')
            for hg in range(3):
                W, wn = ring()
                nh = 4 if hg < 2 else 2
                Wv = W[:].rearrange("p (k n) -> p k n", k=8)
                dma('pool', Wv[:, :, 0:nh * 128], wrows(wq, hg * 512, nh * 128), [], [wn], wn)
                for hi in range(nh):
                    isq = hg < 2
                    h = hg * 4 + hi if isq else hi
                    gain = QK[:, 0:1] if isq else QK[:, 1:2]
                    for hf in range(2):
                        cols = slice(hf * 512, (hf + 1) * 512)
                        par = (hi * 2 + hf) % 2
                        pb = par * 3
                        lst = [I('matmul', bank(pb), lhsT=Wv[:, k, hi * 128:(hi + 1) * 128], rhs=HT[:, k, cols], start=(k == 0), stop=(k == 7)) for k in range(8)]
                        op('pe', G(lst), reads=[wn] + ht_res(range(hf * 4, hf * 4 + 4)), writes=psr(pb))
                        Q32 = U[:, par, 0:512]
                        SQ = U[:, par, 512:1024]
                        qn, sn = f'U{par}q', f'U{par}s'
                        op('act', I('activation', out=Q32, in_=bank(pb), func=AF.Identity), reads=psr(pb), writes=[qn])
                        op('act', I('activation', out=SQ, in_=bank(pb), func=AF.Square), reads=psr(pb), writes=[sn])
                        op('pe', I('matmul', bank(pb + 1), lhsT=ONESF[:], rhs=SQ, start=True, stop=True), reads=[sn, 'ONESF'], writes=psr(pb + 1))
                        op('act', I('activation', out=SQ, in_=bank(pb + 1), func=AF.Sqrt, bias=EPSC[:], scale=1.0 / 128), reads=psr(pb + 1) + ['EPSC'], writes=[sn])
                        op('dve', I('reciprocal', out=SQ, in_=SQ), reads=[sn], writes=[sn])
                        dstT = QT[:, h, cols] if isq else KT[:, h, cols]
                        dres = f'QT{h}_{hf}' if isq else f'KT{h}_{hf}'
                        if blk == 0 and isq:
                            op('dve', I('scalar_tensor_tensor', out=dstT, in0=Q32, scalar=gain, in1=SQ, op0=ALU.mult, op1=ALU.mult),
                               reads=[qn, sn, 'QK'], writes=[dres])
                            continue
                        op('dve', I('scalar_tensor_tensor', out=Q32, in0=Q32, scalar=gain, in1=SQ, op0=ALU.mult, op1=ALU.mult),
                           reads=[qn, sn, 'QK'], writes=[qn])
                        if blk == 0:
                            op('act', I('activation', out=dstT, in_=Q32, func=AF.Identity), reads=[qn], writes=[dres])
                            NKS = A1[hf][:, 0:1024].rearrange("p (t c) -> p t c", t=4)
                            lst = [I('transpose', out=PS[:, pb + 2, t * 128:(t + 1) * 128], in_=Q32[:, t * 128:(t + 1) * 128], identity=IDENTF[:]) for t in range(4)]
                            op('pe', G(lst), reads=[qn, 'IDENTF'], writes=psr(pb + 2))
                            op('dve', I('tensor_copy', out=NKS[:, :, h * 128:(h + 1) * 128], in_=bank(pb + 2).rearrange("p (t c) -> p t c", t=4)),
                               reads=psr(pb + 2), writes=[f'NKS{hf}_{h}'])
                        else:
                            op('pe', I('matmul', bank(pb + 2), lhsT=ROT[:], rhs=Q32, start=True, stop=True), reads=[qn, 'ROT'], writes=psr(pb + 2))
                            op('dve', I('tensor_tensor', out=SQ, in0=bank(pb + 2), in1=A1[1][:, cols], op=ALU.mult), reads=psr(pb + 2) + ['SIN', sn], writes=[sn])
                            op('dve', I('tensor_tensor', out=Q32, in0=Q32, in1=A1[0][:, cols], op=ALU.mult), reads=[qn, 'COS'], writes=[qn])
                            op('dve', I('tensor_tensor', out=dstT, in0=Q32, in1=SQ, op=ALU.add), reads=[qn, sn], writes=[dres])
            if ASTG == 1:
                return
            if blk == 0:
                for hf in range(2):
                    dma('sp', nk_out[hf * 512:(hf + 1) * 512, :].rearrange("(t p) c -> p t c", p=128), A1[hf][:, 0:1024].rearrange("p (t c) -> p t c", t=4),
                        [f'NKS{hf}_0', f'NKS{hf}_1'], [f'nk_out{hf}'], f'nk_out{hf}')
            W, wn = ring()
            Wv = W[:, 0:2048].rearrange("p (k n) -> p k n", k=8)
            dma('pool', Wv, wrows(wq, 1280, 256), [], [wn], wn)
            VS = U[:].rearrange("p a n -> p (a n)").rearrange("p (t c) -> p t c", t=8)
            for jj in range(8):
                pb = jj % 2
                lst = [I('matmul', PS[:, pb, 0:256], lhsT=HT[:, k, jj * 128:(jj + 1) * 128], rhs=Wv[:, k, :], start=(k == 0), stop=(k == 7)) for k in range(8)]
                op('pe', G(lst), reads=[wn] + ht_res([jj]), writes=psr(pb))
                if blk == 0:
                    op('dve', I('tensor_copy', out=VS[:, jj, :], in_=PS[:, pb, 0:256]), reads=psr(pb), writes=[f'VS{jj}'])
                    op('act', I('activation', out=VV[:, jj, :], in_=VS[:, jj, :], func=AF.Identity), reads=[f'VS{jj}'], writes=[f'V{jj}'])
                else:
                    op('act', I('activation', out=VV[:, jj, :], in_=PS[:, pb, 0:256], func=AF.Identity), reads=psr(pb), writes=[f'V{jj}'])
            if blk == 0:
                dma('sp', nv_out.rearrange("(t p) c -> p t c", p=128), VS, [f'VS{jj}' for jj in range(8)], ['nv_out'], 'nv_out')
            kres = [f'KT{h}_{hf}' for h in range(2) for hf in range(2)]
            vres = [f'V{jj}' for jj in range(8)]
            if blk == 1:
                ki, ko, vi, vo = k_in.ap(), k_out.ap(), v_in.ap(), v_out.ap()
                dma('sp', ki.rearrange("p (h n) -> p h n", h=2), KT[:, :, 0:1024], kres, ['k_in'], 'k_in')
                dma('sp', vi.rearrange("p (t c) -> p t c", t=8), VV[:, 0:8, :], vres, ['v_in'], 'v_in')
                op('pool', I('collective_compute', "AllGather", ALU.bypass, replica_groups=GROUPS, ins=[ki.opt()], outs=[ko.opt()]), reads=['k_in'], writes=['k_out'])
                op('pool', None, reads=['k_out'])
                op('pool', I('collective_compute', "AllGather", ALU.bypass, replica_groups=GROUPS, ins=[vi.opt()], outs=[vo.opt()]), reads=['v_in'], writes=['v_out'])
                op('pool', None, reads=['v_out'])
                kov = ko.rearrange("(r p) (h n) -> h p r n", p=128, h=2)
                for h in range(2):
                    dma('sp', KT[:, h, 0:4096].rearrange("p (r n) -> p r n", r=4), kov[h], ['k_out'], [f'KTall{h}'], f'KTall{h}')
                dma('sp', VV[:, 0:32, :].rearrange("p (r t) c -> p r t c", r=4), vo.rearrange("(r p) (t c) -> p r t c", p=128, t=8), ['v_out'], ['Vall'], 'Vall')
                W, wn = ring()
                CK = W[:, 0:1024].rearrange("p (t c) -> p t c", t=4)
                dma('pool', CK, cache_k.rearrange("(t p) c -> p t c", p=128), [], [wn], wn)
                dma('pool', VV[:, 32:36, :], cache_v.rearrange("(t p) c -> p t c", p=128), [], ['Vc'], 'Vc')
                ptb = PS[:, 7, :].bitcast(BF16).rearrange("p (k n) -> p k n", k=8)
                lst = [I('transpose', out=ptb[:, t * 2 + h, :], in_=CK[:, t, h * 128:(h + 1) * 128], identity=IDENT[:]) for t in range(4) for h in range(2)]
                op('pe', G(lst), reads=[wn, 'IDENT'], writes=psr(7))
                for h in range(2):
                    op('dve', I('tensor_copy', out=KT[:, h, 4096:4608].rearrange("p (t n) -> p t n", t=4), in_=ptb.rearrange("p (t h) n -> p h t n", h=2)[:, h]),
                       reads=psr(7), writes=[f'KTc{h}'])
                kres = ['KTall0', 'KTall1', 'KTc0', 'KTc1']
                vres = ['Vall', 'Vc']
            if ASTG == 2:
                return
            scale = 128 ** -0.5
            a0 = A1[0][:].bitcast(BF16)
            a1 = A1[1][:].bitcast(BF16)
            PT2 = [a0[:, 0:512], a0[:, 1024:1536], a1[:, 0:512], a1[:, 1024:1536]]
            ptn = ['P0', 'P1', 'P2', 'P3']
            cnt = 0
            if blk == 0:
                units = [(h, s_ * 256, 256, [2 * s_, 2 * s_ + 1], h // 4) for h in range(8) for s_ in range(4)]
            else:
                units = [(h, hf * 512, 512, list(range(36)), h // 4) for h in range(8) for hf in range(2)]
            steps = []
            for ui_, (h, q0, nq, ktiles, kv) in enumerate(units):
                for ti, kt in enumerate(ktiles):
                    steps.append((ui_, ti, kt))
            LA = 3

            def emit_S(si):
                ui_, ti, kt = steps[si]
                h, q0, nq, ktiles, kv = units[ui_]
                sb_ = 4 + (si % 4)
                pi = si % 4
                op('pe', I('matmul', PS[:, sb_, 0:nq], lhsT=KT[:, kv, kt * 128:(kt + 1) * 128], rhs=QT[:, h, q0:q0 + nq], start=True, stop=True),
                   reads=kres + [f'QT{h}_{q0 // 512}'], writes=psr(sb_))
                op('act', I('activation', out=PT2[pi][:, 0:nq], in_=PS[:, sb_, 0:nq], func=AF.Exp, scale=scale), reads=psr(sb_), writes=[ptn[pi]])

            def emit_PV(si):
                ui_, ti, kt = steps[si]
                h, q0, nq, ktiles, kv = units[ui_]
                nkt = len(ktiles)
                pi = si % 4
                pacc = (ui_ % 2) * 2
                op('pe', G([I('matmul', PS[:, pacc, 0:nq], lhsT=VV[:, kt, kv * 128:(kv + 1) * 128], rhs=PT2[pi][:, 0:nq], start=(ti == 0), stop=(ti == nkt - 1)),
                            I('matmul', PS[:, pacc + 1, 0:nq], lhsT=ONES[:], rhs=PT2[pi][:, 0:nq], start=(ti == 0), stop=(ti == nkt - 1))]),
                   reads=vres + [ptn[pi], 'ONES'], writes=psr(pacc, 2))
                if ti == nkt - 1:
                    ui2 = ui_ % 2
                    R = U[:, ui2, 0:nq]
                    op('dve', I('reciprocal', out=R, in_=PS[:, pacc + 1, 0:nq]), reads=psr(pacc + 1), writes=[f'U{ui2}q'])
                    op('dve', I('tensor_tensor', out=QT[:, h, q0:q0 + nq], in0=PS[:, pacc, 0:nq], in1=R, op=ALU.mult), reads=psr(pacc) + [f'U{ui2}q'], writes=[f'OT{h}_{q0}'])
            for i_ in range(len(steps) + LA):
                if i_ < len(steps):
                    emit_S(i_)
                if i_ - LA >= 0:
                    emit_PV(i_ - LA)
            if ASTG == 3:
                return
            ores = [f'OT{h}_{q0}' for (h, q0, nq, kt, kv) in units]
            proj_out(blk, 0, QT, ores, 8, attn_w_o[0])

        def pool_mix(blk):
            nseq, Ls = (4, 256) if blk == 0 else (1, 1024)
            PT_ = BIG[:, 0:8192].rearrange("p (k n) -> p k n", k=8)
            Lp = Ls + 16
            if blk == 1:
                halo_exchange(4, 8)
            W, wn = ring()
            PW = W[:, 0:2048].rearrange("p (g k n) -> p g k n", g=4, k=2)
            dma('pool', PW, pool_w[0].rearrange("g (k p) n -> p g k n", p=128), [], [wn], wn)
            RC = SMALL[:, 0:128].rearrange("p (g s w) -> p g s w", g=4, s=2)
            dma('sp', RC, c_rc[:, blk], [], ['SMALL'], 'SMALL')
            dma('sp', LNG[:], pool_scale[0:1, :].partition_broadcast(128), [], ['LNG'], 'LNG')
            op('dve', I('tensor_tensor', out=GATE[blk][:], in0=GATE[blk][:], in1=LNG[:], op=ALU.mult), reads=['LNG', f'GATE{blk}'], writes=[f'GATE{blk}'])
            load_ln(2, 0)
            for k in range(8):
                g = k // 2
                hh = POOL_HALF[g]
                Pa = A1[0][:, 0:nseq * Lp].rearrange("p (s l) -> p s l", s=nseq)
                Pb = A1[1][:, 0:nseq * Lp].rearrange("p (s l) -> p s l", s=nseq)
                hsrc = HT[:, k, :].rearrange("p (s l) -> p s l", s=nseq)
                if blk == 0:
                    op('pool', G([I('memset', Pa[:, :, 0:8], 0.0), I('memset', Pa[:, :, Ls + 8:Ls + 16], 0.0)]), writes=['A1_0'])
                else:
                    op('dve', I('tensor_copy', out=Pa[:, 0, 0:8], in_=HSEL[:, 0, k, :]), reads=['HSEL0'], writes=['A1_0'])
                    op('dve', I('tensor_copy', out=Pa[:, 0, Ls + 8:Ls + 16], in_=HSEL[:, 1, k, :]), reads=['HSEL1'], writes=['A1_0'])
                op('act', I('activation', out=Pa[:, :, 8:Ls + 8], in_=hsrc, func=AF.Identity), reads=ht_res(range(8), [k]), writes=['A1_0'])
                cur, curn, oth, othn = Pa, 'A1_0', Pb, 'A1_1'
                width = 1
                valid = Lp
                while width < 2 * hh:
                    nv_ = valid - width
                    op('dve', I('tensor_tensor', out=oth[:, :, 0:nv_], in0=cur[:, :, 0:nv_], in1=cur[:, :, width:width + nv_], op=ALU.add),
                       reads=[curn], writes=[othn])
                    cur, curn, oth, othn = oth, othn, cur, curn
                    valid = nv_
                    width *= 2
                Uv = U[:, 0, :].rearrange("p (s l) -> p s l", s=nseq)
                op('dve', I('tensor_scalar', out=Uv, in0=cur[:, :, 8 - hh:8 - hh + Ls], scalar1=1.0 / (2 * hh), scalar2=None, op0=ALU.mult),
                   reads=[curn], writes=['U0'])
                op('dve', I('tensor_tensor', out=Uv[:, :, 0:16], in0=cur[:, :, 8 - hh:8 - hh + 16],
                            in1=RC[:, g, 0, :].unsqueeze(1).to_broadcast([128, nseq, 16]), op=ALU.mult), reads=[curn, 'U0', 'SMALL'], writes=['U0'])
                op('dve', I('tensor_tensor', out=Uv[:, :, Ls - 16:Ls], in0=cur[:, :, 8 - hh + Ls - 16:8 - hh + Ls],
                            in1=RC[:, g, 1, :].unsqueeze(1).to_broadcast([128, nseq, 16]), op=ALU.mult), reads=[curn, 'U0', 'SMALL'], writes=['U0'])
                op('dve', I('tensor_tensor', out=PT_[:, k, :].rearrange("p (s l) -> p s l", s=nseq), in0=Uv, in1=hsrc, op=ALU.subtract),
                   reads=['U0'] + ht_res(range(8), [k]), writes=[f'PT{k}'])
            for tg in range(2):
                for t4 in range(4):
                    jj = tg * 4 + t4
                    lst = [I('matmul', PS[:, t4 * 2 + g // 2, (g % 2) * 256:(g % 2) * 256 + 256], lhsT=PT_[:, 2 * g + kk, jj * 128:(jj + 1) * 128],
                             rhs=PW[:, g, kk, :], start=(kk == 0), stop=(kk == 1)) for g in range(4) for kk in range(2)]
                    op('pe', G(lst), reads=[wn] + [f'PT{k}' for k in range(8)], writes=psr(t4 * 2, 2))
                epilogue4(blk, tg, 0)

        def gmlp(blk, first):
            UT = BIG[:, 0:16384].rearrange("p (c n) -> p c n", c=16)
            GBC = BIG[:, 16384:20480].bitcast(F32)
            CT = BIG[:, 20480:24576].bitcast(F32).rearrange("p (c q) -> p c q", c=16)
            VB = BIG[:, 24576:26624]
            win = gmlp_w_in[0]
            GBT = GB[:, 0:16]
            GLB = GB[:, 16:32]
            if first:
                dma('sp', GBC, gmlp_ln_g[0:1, :].partition_broadcast(128), [], ['GBC'], 'GBC')
                dma('sp', SMALL[:], gmlp_b_s[0:1, :].partition_broadcast(128), [], ['SMALL'], 'SMALL')
                dmas('sp', [(GBT, gbinT), (GLB, glnbT)], [], ['GB'], 'GB')
                dma('pool', HALL[:], wsT, [], ['WST'], 'WST')
                lst = [I('matmul', PS[:, 0, g * 128:(g + 1) * 128], lhsT=ONES[:], rhs=HALL[:, g, :], start=True, stop=True) for g in range(4)]
                op('pe', G(lst), reads=['WST', 'ONES'], writes=psr(0))
                for c in range(16):
                    g = c // 4
                    op('dve', I('scalar_tensor_tensor', out=CT[:, c, :], in0=PS[:, 0, g * 128:(g + 1) * 128], scalar=GLB[:, c:c + 1],
                                in1=SMALL[:, g * 128:(g + 1) * 128], op0=ALU.mult, op1=ALU.add), reads=psr(0) + ['GB', 'SMALL'], writes=[f'CT{c}'])
            BROW = A1[0][0:1, 0:1024].bitcast(BF16)
            dma('pool', BROW, gmlp_b_in[0:1, 2048:4096], [], ['BROW'], 'BROW')
            for cg in range(4):
                W, wn = ring()
                Wv = W[:].rearrange("p (k n) -> p k n", k=8)
                dma('pool', Wv, wrows(win, cg * 512, 512), [], [wn], wn)
                for ci in range(4):
                    c = cg * 4 + ci
                    pb = (c % 2) * 2
                    lst = [I('matmul', bank(pb + hf), lhsT=Wv[:, k, ci * 128:(ci + 1) * 128], rhs=HT[:, k, hf * 512:(hf + 1) * 512], start=(k == 0), stop=(k == 7))
                           for hf in range(2) for k in range(8)]
                    op('pe', G(lst), reads=[wn] + ht_res(range(8)), writes=psr(pb, 2))
                    op('act', I('activation', out=UT[:, c, :], in_=bank2(pb), func=AF.Gelu_apprx_tanh, bias=GBT[:, c:c + 1], scale=1.0),
                       reads=psr(pb, 2) + ['GB'], writes=[f'UT{c}'])
            VL = U[:].rearrange("p a n -> p (a n)")
            for tp in range(4):
                for cb in range(4):
                    W, wn = ring()
                    Wv = W[:].rearrange("p (k n) -> p k n", k=8)
                    dma('pool', Wv, wrows(win, 2048 + cb * 512, 512), [], [wn], wn)
                    for t2 in range(2):
                        jj = tp * 2 + t2
                        lst = [I('matmul', bank(t2 * 4 + cb), lhsT=HT[:, k, jj * 128:(jj + 1) * 128], rhs=Wv[:, k, :], start=(k == 0), stop=False) for k in range(8)]
                        lst.append(I('matmul', bank(t2 * 4 + cb), lhsT=ONES[0:1, :], rhs=BROW[:, cb * 512:(cb + 1) * 512], start=False, stop=True))
                        op('pe', G(lst), reads=[wn, 'BROW', 'ONES'] + ht_res([jj]), writes=psr(t2 * 4 + cb))
                for t2 in range(2):
                    jj = tp * 2 + t2
                    src = PS[:, t2 * 4:t2 * 4 + 4, :].rearrange("p a n -> p (a n)")
                    op('act', I('activation', out=VL, in_=src, func=AF.Gelu_apprx_tanh), reads=psr(t2 * 4, 4), writes=['VL'])
                    ln_stats(VL, ['VL'], t2, width=2048)
                    op('dve', I('tensor_scalar', out=VL, in0=VL, scalar1=MV[t2][:, 0:1], scalar2=RS[t2][:], op0=ALU.subtract, op1=ALU.mult),
                       reads=['VL', f'MV{t2}', f'RS{t2}'], writes=['VL'])
                    op('dve', I('tensor_tensor', out=VB, in0=VL, in1=GBC, op=ALU.mult), reads=['VL', 'GBC'], writes=['VB'])
                    lst = [I('matmul', PS[:, t2 * 4 + c // 4, (c % 4) * 128:(c % 4) * 128 + 128], lhsT=VB[:, c * 128:(c + 1) * 128], rhs=HALL[:, c // 4, :], start=True, stop=True)
                           for c in range(16)]
                    op('pe', G(lst), reads=['VB', 'WST'], writes=psr(t2 * 4, 4))
                    Sx = SMALL[:].rearrange("p (c q) -> p c q", c=4)
                    for gg in range(4):
                        op('dve', I('tensor_tensor', out=Sx, in0=PS[:, t2 * 4 + gg, :].rearrange("p (c q) -> p c q", c=4), in1=CT[:, gg * 4:gg * 4 + 4, :], op=ALU.add),
                           reads=psr(t2 * 4 + gg) + [f'CT{c}' for c in range(gg * 4, gg * 4 + 4)], writes=['SMALL'])
                        utv = UT[:, gg * 4:gg * 4 + 4, jj * 128:(jj + 1) * 128]
                        op('dve', I('tensor_tensor', out=utv, in0=Sx, in1=utv, op=ALU.mult),
                           reads=['SMALL'] + [f'UT{c}' for c in range(gg * 4, gg * 4 + 4)], writes=[f'US{jj}_{gg}'])
            ures = [f'US{jj}_{gg}' for jj in range(8) for gg in range(4)]
            proj_out(blk, 0, UT, ures, 16, gmlp_w_o[0])

        pre_done = False
        for L in range(DEPTH):
            if not pre_done:
                mod_phase(L)
            gates_phase(L, 0)
            load_ln(L, 0)
            if L == 0:
                prenorm(1, 0)
                pre = [wd_load(fnet_w_o[0], 0, 4), wd_load(fnet_w_o[0], 512 // 128, 4)]
                fnet_a(1)
                prenorm(0, 0)
                fnet_a(0)
                fnet_b(0, preloaded=pre)
                fnet_b(1)
            else:
                for blk in (1, 0):
                    if not (blk == 1 and pre_done):
                        prenorm(blk, 0)
                    if L == 1:
                        attn(blk)
                    elif L == 2:
                        pool_mix(blk)
                    else:
                        gmlp(blk, blk == 1)
            gates_phase(L, 1)
            load_ln(L, 1)
            prenorm(1, 1)
            ffn_up(1, L)
            prenorm(0, 1)
            ffn_down(1, L)
            ffn_up(0, L)
            pre_done = False
            if L + 1 < DEPTH:
                mod_phase(L + 1)
                prenorm(1, 0)
                pre_done = True
            ffn_down(0, L)

        yv = yout.rearrange("(j p) d -> p j d", p=128)
        outs = ['yout']
        dmas('sp', [(yv[:, j, :], X[:, j, :]) for j in range(16)], [f'X{j}' for j in range(16)], ['yout'], 'yout')
        fin = [r for r in outs + ['nk_out0', 'nk_out1', 'nv_out'] if r in S.res_w]
        op('sp', None, reads=fin)

        names = S.sem_names()
        sems = {n: st.enter_context(nc.semaphore(n)) for n in names}
        with nc.Block() as block:
            @block.tensor
            def _(e):
                S.replay('pe', e, sems)

            @block.scalar
            def _(e):
                S.replay('act', e, sems)

            @block.vector
            def _(e):
                S.replay('dve', e, sems)

            @block.gpsimd
            def _(e):
                S.replay('pool', e, sems)

            @block.sync
            def _(e):
                S.replay('sp', e, sems)
    return nc


def _consts():
    bf = ml_dtypes.bfloat16
    c = {}
    c['c_ident'] = np.eye(128, dtype=np.float32).astype(bf)
    c['c_identf'] = np.eye(128, dtype=np.float32)
    c['c_ones'] = np.ones((128, 128), np.float32).astype(bf)
    c['c_onesf'] = np.ones((128, 128), np.float32)
    rot = np.zeros((128, 128), np.float32)
    for d in range(128):
        i = d % 64
        partner = d + 32 if i < 32 else d - 32
        rot[partner, d] = 1.0
    c['c_rot'] = rot
    n = np.arange(256)
    ang = 2 * np.pi * ((n[:, None] * n[None, :]) % 256) / 256.0
    cs = np.concatenate([np.cos(ang), np.sin(ang)], axis=1)
    c['c_cs'] = np.ascontiguousarray(cs.reshape(2, 128, 512).transpose(1, 0, 2)).astype(np.float32).astype(bf)
    ctp = np.concatenate([np.cos(ang) / 256.0, -np.sin(ang) / 256.0], axis=1)
    c['c_ctp'] = np.ascontiguousarray(ctp.reshape(2, 128, 512).transpose(1, 0, 2)).astype(np.float32).astype(bf)
    return c


def _core_consts(core):
    bf = ml_dtypes.bfloat16
    qi = core % 4
    c = {}
    t = np.arange(4096, dtype=np.int64)
    tp = qi * 1024 + np.arange(1024, dtype=np.int64)
    ang = 2 * np.pi * ((t[:, None] * tp[None, :]) % 4096) / 4096.0
    c['c_cts'] = (np.cos(ang) / 1024.0).astype(np.float32).astype(bf)
    c['c_nsts'] = (-np.sin(ang) / 1024.0).astype(np.float32).astype(bf)
    row = (tp // 64).astype(np.float64)
    col = (tp % 64).astype(np.float64)
    inv = 10000.0 ** (-np.arange(32, dtype=np.float64) / 32)
    cos = np.zeros((128, 1024)); sin = np.zeros((128, 1024))
    for d in range(128):
        pos = row if d < 64 else col
        i = d % 64
        a = pos * inv[i % 32]
        cos[d] = np.cos(a)
        sin[d] = np.sin(a) * (-1.0 if i < 32 else 1.0)
    c['c_cos'] = cos.astype(np.float32)
    c['c_sin'] = sin.astype(np.float32)
    mask = np.zeros((128, 8), np.float32)
    if qi > 0:
        mask[:, qi - 1] = 1.0
    if qi < 3:
        mask[:, 4 + qi + 1] = 1.0
    c['c_mask'] = mask
    rc = np.zeros((2, 4, 2, 16), np.float64)
    for g, hh in enumerate(POOL_HALF):
        T = 256
        tt = np.arange(16)
        rc[0, g, 0] = 1.0 / (np.minimum(tt + hh, T) - np.maximum(tt - hh, 0))
        tt2 = T - 16 + np.arange(16)
        rc[0, g, 1] = 1.0 / (np.minimum(tt2 + hh, T) - np.maximum(tt2 - hh, 0))
        T = 4096
        tt = qi * 1024 + np.arange(16)
        rc[1, g, 0] = 1.0 / (np.minimum(tt + hh, T) - np.maximum(tt - hh, 0))
        tt2 = qi * 1024 + 1024 - 16 + np.arange(16)
        rc[1, g, 1] = 1.0 / (np.minimum(tt2 + hh, T) - np.maximum(tt2 - hh, 0))
    c['c_rc'] = np.broadcast_to(rc.astype(np.float32)[None], (128, 2, 4, 2, 16)).copy()
    return c


_NC_CACHE = {}


def kernel(x_prompt, x_sample, cache_k, cache_v, c, c_ctx, w_mod, b_mod, ln_g, ln_b,
           ffn_w_up, ffn_conv_w, ffn_conv_b, ffn_w_down, fnet_w_o, attn_w_qkv,
           attn_q_norm, attn_k_norm, attn_w_o, pool_w, pool_scale, gmlp_w_in, gmlp_b_in,
           gmlp_ln_g, gmlp_ln_b, gmlp_w_s, gmlp_b_s, gmlp_w_o):
    f = lambda a: np.ascontiguousarray(np.asarray(a, dtype=np.float32))
    x_prompt, x_sample, cache_k, cache_v = f(x_prompt), f(x_sample), f(cache_k), f(cache_v)
    c, c_ctx = f(c), f(c_ctx)
    shared = {
        'w_mod': f(w_mod), 'b_mod': f(b_mod), 'ln_g': f(ln_g), 'ln_b': f(ln_b),
        'ffn_w_up': f(ffn_w_up), 'ffn_w_down': f(ffn_w_down), 'fnet_w_o': f(fnet_w_o),
        'attn_w_qkv': f(attn_w_qkv), 'attn_w_o': f(attn_w_o), 'pool_w': f(pool_w), 'pool_scale': f(pool_scale),
        'gmlp_w_in': f(gmlp_w_in), 'gmlp_b_in': f(gmlp_b_in), 'gmlp_ln_g': f(gmlp_ln_g),
        'gmlp_b_s': f(np.asarray(gmlp_b_s).reshape(1, 512)), 'gmlp_w_o': f(gmlp_w_o),
    }
    shared['bmodT'] = f(np.asarray(b_mod).reshape(DEPTH, 48, 128).transpose(2, 0, 1))
    conv = np.concatenate([np.asarray(ffn_conv_w), np.asarray(ffn_conv_b)[:, None, :]], axis=1)
    shared['convT'] = f(conv.reshape(DEPTH, 4, NFF, 128).transpose(3, 0, 1, 2))
    shared['qkT'] = f(np.stack([np.asarray(attn_q_norm)[0], np.asarray(attn_k_norm)[0]], axis=1))
    shared['gbinT'] = f(np.asarray(gmlp_b_in)[0, :2048].reshape(16, 128).T)
    shared['glnbT'] = f(np.asarray(gmlp_ln_b)[0].reshape(16, 128).T)
    shared['wsT'] = f(np.asarray(gmlp_w_s)[0].transpose(2, 0, 1))
    shared.update(_consts())
    in_maps = []
    for core in range(8):
        b = core // 4
        qi = core % 4
        m = dict(shared)
        xp = x_prompt[4 * core:4 * core + 4].reshape(1024, D)
        xs = x_sample[b, qi * 1024:(qi + 1) * 1024]
        m['xin'] = np.ascontiguousarray(np.concatenate([xp, xs], axis=0))
        cond = np.stack([c_ctx, c[b]], axis=1)
        m['condT'] = f(cond.reshape(8, 128, 2).transpose(1, 0, 2))
        m['cache_k'] = f(cache_k[b, 0].reshape(512, 256))
        m['cache_v'] = f(cache_v[b, 0].reshape(512, 256))
        m.update(_core_consts(core))
        in_maps.append(m)
    if 'nc' not in _NC_CACHE:
        _NC_CACHE['nc'] = build_program()
    nc = _NC_CACHE['nc']
    res = run_bass_kernel_spmd(nc, in_maps, core_ids=list(range(8)))
    r = res.results
    y_prompt = np.zeros((32, 256, D), np.float32)
    y_sample = np.zeros((2, 4096, D), np.float32)
    new_k = np.zeros((32, 1, 256, 2, 128), np.float32)
    new_v = np.zeros((32, 1, 256, 2, 128), np.float32)
    for core in range(8):
        b = core // 4
        qi = core % 4
        y = r[core]['yout']
        y_prompt[4 * core:4 * core + 4] = y[:1024].reshape(4, 256, D)
        y_sample[b, qi * 1024:(qi + 1) * 1024] = y[1024:]
        new_k[4 * core:4 * core + 4, 0] = r[core]['nk_out'].reshape(4, 256, 2, 128)
        new_v[4 * core:4 * core + 4, 0] = r[core]['nv_out'].reshape(4, 256, 2, 128)
    return (y_prompt, y_sample, new_k, new_v)
```

```python
import contextlib
import numpy as np
import ml_dtypes
import concourse.bass as bass
import concourse.mybir as mybir
from concourse.bass_utils import run_bass_kernel_spmd

F32 = mybir.dt.float32
BF16 = mybir.dt.bfloat16
AF = mybir.ActivationFunctionType
ALU = mybir.AluOpType

D = 1024
DFF = 2816
NFF = 22
DEPTH = 4
ALPHA = (2 * DEPTH) ** 0.25
EPS = 1e-6
GROUPS = [[0, 1, 2, 3], [4, 5, 6, 7]]
POOL_HALF = (1, 2, 4, 8)
DEBUG_STOP = None
DEBUG_STAGE = None
DEBUG_LAYERS = None
DEBUG_BLKS = None


class Sched:
    ENG = ['pe', 'act', 'dve', 'pool', 'sp']

    def __init__(self):
        self.ops = {e: [] for e in self.ENG}
        self.clock = {e: {} for e in self.ENG}
        self.cnt = {}
        self.res_w = {}
        self.res_r = {}
        self.alias = {}

    def expand(self, names):
        out = []
        for n in names:
            for m in self.alias.get(n, (n,)):
                if m not in out:
                    out.append(m)
        return out

    def op(self, eng, emit, reads=(), writes=(), dma=None, ndma=1):
        reads = self.expand(reads)
        writes = self.expand(writes)
        clk = self.clock[eng]
        own = 'E_' + eng
        need = {}

        def req(ev):
            if ev is None:
                return
            sem, val, eclk = ev
            if sem == own and eng == 'pe':
                return
            if clk.get(sem, 0) >= val:
                return
            if need.get(sem, (0, None))[0] < val:
                need[sem] = (val, eclk)

        for r in reads:
            req(self.res_w.get(r))
        for w in writes:
            req(self.res_w.get(w))
            for ev in self.res_r.get(w, ()):
                req(ev)
        if dma is not None:
            sem = 'D_' + dma + '_' + eng
            prev = self.cnt.get(sem, 0)
            if prev and clk.get(sem, 0) < prev:
                if need.get(sem, (0, None))[0] < prev:
                    need[sem] = (prev, {})
            inc = 16 * ndma
        else:
            sem = own
            inc = 1
        if emit is None:
            sem = None
        waits = []
        for s, (v, eclk) in need.items():
            if clk.get(s, 0) >= v:
                continue
            waits.append((s, v))
            for k2, v2 in eclk.items():
                if clk.get(k2, 0) < v2:
                    clk[k2] = v2
            if clk.get(s, 0) < v:
                clk[s] = v
        if emit is None:
            self.ops[eng].append((waits, None, None, 0))
            return None
        val = self.cnt.get(sem, 0) + inc
        self.cnt[sem] = val
        self.ops[eng].append((waits, emit, sem, inc, dma is not None and ndma > 1))
        eclk = dict(clk)
        eclk[sem] = val
        ev = (sem, val, eclk)
        for w in writes:
            self.res_w[w] = ev
            self.res_r[w] = []
        for r in reads:
            self.res_r.setdefault(r, []).append(ev)
        return ev

    def sem_names(self):
        return sorted(self.cnt.keys())

    def replay(self, eng, handle, sems):
        for rec in self.ops[eng]:
            waits, emit, sem, inc = rec[:4]
            for s, v in waits:
                handle.wait_ge(sems[s], v)
            if emit is None:
                continue
            if len(rec) > 4 and rec[4]:
                emit(handle, sems[sem])
            else:
                ins = emit(handle)
                ins.then_inc(sems[sem], inc)


def I(name, *a, **kw):
    return lambda e: getattr(e, name)(*a, **kw)


def G(lst):
    def f(e):
        ins = None
        for g in lst:
            ins = g(e)
        return ins
    return f


def build_program():
    nc = bass.Bass("TRN2", target_bir_lowering=False)
    S = Sched()
    op = S.op

    def din(name, shape, dt=F32):
        return nc.dram_tensor(name, list(shape), dt, kind="ExternalInput").ap()

    def dout(name, shape, dt=F32):
        return nc.dram_tensor(name, list(shape), dt, kind="ExternalOutput").ap()

    xin = din("xin", [2048, D])
    condT = din("condT", [128, 8, 2])
    w_mod = din("w_mod", [DEPTH, D, 6 * D])
    bmodT = din("bmodT", [128, DEPTH, 48])
    b_mod = din("b_mod", [DEPTH, 6 * D])
    ln_g = din("ln_g", [DEPTH, 2, D])
    ln_b = din("ln_b", [DEPTH, 2, D])
    ffn_w_up = din("ffn_w_up", [DEPTH, D, 2 * DFF])
    convT = din("convT", [128, DEPTH, 4, NFF])
    ffn_w_down = din("ffn_w_down", [DEPTH, DFF, D])
    fnet_w_o = din("fnet_w_o", [1, D, D])
    attn_w_qkv = din("attn_w_qkv", [1, D, 1536])
    qkT = din("qkT", [128, 2])
    attn_w_o = din("attn_w_o", [1, D, D])
    pool_w = din("pool_w", [1, 4, 256, 256])
    pool_scale = din("pool_scale", [1, D])
    gmlp_w_in = din("gmlp_w_in", [1, D, 4096])
    gbinT = din("gbinT", [128, 16])
    gmlp_b_in = din("gmlp_b_in", [1, 4096])
    gmlp_ln_g = din("gmlp_ln_g", [1, 2048])
    glnbT = din("glnbT", [128, 16])
    wsT = din("wsT", [128, 4, 128])
    gmlp_b_s = din("gmlp_b_s", [1, 512])
    gmlp_w_o = din("gmlp_w_o", [1, 2048, D])
    cache_k = din("cache_k", [512, 256])
    cache_v = din("cache_v", [512, 256])
    c_ident = din("c_ident", [128, 128], BF16)
    c_identf = din("c_identf", [128, 128])
    c_ones = din("c_ones", [128, 128], BF16)
    c_onesf = din("c_onesf", [128, 128])
    c_rot = din("c_rot", [128, 128])
    c_cs = din("c_cs", [128, 2, 512], BF16)
    c_ctp = din("c_ctp", [128, 2, 512], BF16)
    c_cts = din("c_cts", [4096, 1024], BF16)
    c_nsts = din("c_nsts", [4096, 1024], BF16)
    c_cos = din("c_cos", [128, 1024])
    c_sin = din("c_sin", [128, 1024])
    c_mask = din("c_mask", [128, 8])
    c_rc = din("c_rc", [128, 2, 4, 2, 16])

    yout = dout("yout", [2048, D])
    nk_out = dout("nk_out", [1024, 256])
    nv_out = dout("nv_out", [1024, 256])

    halo_w = [1, 1, 1, 1, 8]
    halo_in = [nc.dram_tensor(f"halo_in{i}", [128, 16 * w_], BF16) for i, w_ in enumerate(halo_w)]
    halo_out = [nc.dram_tensor(f"halo_out{i}", [512, 16 * w_], BF16) for i, w_ in enumerate(halo_w)]
    ab_in = [nc.dram_tensor(f"ab_in{j}", [256, 2048], BF16) for j in range(4)]
    ab_out = [nc.dram_tensor(f"ab_out{j}", [1024, 2048], BF16) for j in range(4)]
    k_in = nc.dram_tensor("k_in", [128, 2048], BF16)
    k_out = nc.dram_tensor("k_out", [512, 2048], BF16)
    v_in = nc.dram_tensor("v_in", [128, 2048], BF16)
    v_out = nc.dram_tensor("v_out", [512, 2048], BF16)

    st = contextlib.ExitStack()
    with st:
        def sb(name, shape, dt=F32):
            return st.enter_context(nc.sbuf_tensor(name, list(shape), dt))

        X = sb("X", [128, 16, D])
        HT = sb("HT", [128, 8, 1024], BF16)
        BIG = sb("BIG", [128, 26624], BF16)
        WR = [sb(f"WR{i}", [128, 4096], BF16) for i in range(3)]
        GATE = [sb(f"GATE{g}", [128, D]) for g in range(2)]
        LNG = sb("LNG", [128, D])
        LNB = sb("LNB", [128, D])
        A1 = [sb(f"A1_{i}", [128, 1088]) for i in range(2)]
        U = sb("U", [128, 2, 1024])
        XNB = [sb(f"XNB{i}", [128, 1024], BF16) for i in range(2)]
        MODT = sb("MODT", [128, 4, 8, 2])
        BMT = sb("BMT", [128, DEPTH, 48])
        CONV = sb("CONV", [128, DEPTH, 4, NFF])
        CST = sb("CST", [128, 8, 2])
        CSB = sb("CSB", [128, 8, 2], BF16)
        IDENT = sb("IDENT", [128, 128], BF16)
        IDENTF = sb("IDENTF", [128, 128])
        ONES = sb("ONES", [128, 128], BF16)
        ONESF = sb("ONESF", [128, 128])
        ROT = sb("ROT", [128, 128])
        MASK = sb("MASK", [128, 8])
        STATS = [sb(f"STATS{i}", [128, 4, 6]) for i in range(8)]
        MV = [sb(f"MV{i}", [128, 2]) for i in range(8)]
        RS = [sb(f"RS{i}", [128, 1]) for i in range(8)]
        NMR = [sb(f"NMR{i}", [128, 1]) for i in range(8)]
        EPSC = sb("EPSC", [128, 1])
        HALL = sb("HALL", [128, 4, 128], BF16)
        HSEL = sb("HSEL", [128, 2, 8, 8])
        HTH = sb("HTH", [128, 8, 2], BF16)
        QK = sb("QK", [128, 2])
        SMALL = sb("SMALL", [128, 512])
        GB = sb("GB", [128, 48])
        PS = st.enter_context(nc.psum_tensor("PS", [128, 8, 512], F32))

        al = S.alias
        al['U0'] = ['U0a', 'U0b']; al['U1'] = ['U1a', 'U1b']
        al['U0q'] = ['U0a']; al['U0s'] = ['U0b']; al['U1q'] = ['U1a']; al['U1s'] = ['U1b']
        al['VL'] = ['U0a', 'U0b', 'U1a', 'U1b']
        for jj in range(8):
            al[f'VS{jj}'] = [['U0a', 'U0b', 'U1a', 'U1b'][jj // 2]]
        for i in range(2):
            al[f'A1_{i}'] = [f'A1_{i}a', f'A1_{i}b', f'A1_{i}x']
            for h in range(2):
                al[f'NKS{i}_{h}'] = al[f'A1_{i}']
        al['COS'] = ['A1_0a', 'A1_0b']; al['SIN'] = ['A1_1a', 'A1_1b']
        al['P0'] = ['A1_0a']; al['P1'] = ['A1_0b']; al['P2'] = ['A1_1a']; al['P3'] = ['A1_1b']
        al['WST'] = ['HALL']
        al['BROW'] = ['A1_0a', 'A1_0b']

        def bu(off, ln):
            return [f'B{u}' for u in range(off // 1024, (off + ln + 1023) // 1024)]
        for c in range(NFF):
            al[f'GT{c}'] = bu(c * 1024, 1024)
        for jj in range(8):
            al[f'AB{jj}'] = bu(jj * 2048, 2048)
            al[f'FT{jj}'] = bu(16384 + jj * 1024, 1024)
            al[f'PT{jj}'] = bu(jj * 1024, 1024)
            al[f'V{jj}'] = bu(17408 + jj * 256, 256)
        for h in range(8):
            for hf in range(2):
                al[f'QT{h}_{hf}'] = bu(h * 1024 + hf * 512, 512)
            for q0 in range(0, 1024, 256):
                al[f'OT{h}_{q0}'] = bu(h * 1024 + q0, 256)
        for h in range(2):
            for hf in range(2):
                al[f'KT{h}_{hf}'] = bu(8192 + h * 4608 + hf * 512, 512)
            al[f'KTall{h}'] = bu(8192 + h * 4608, 4096)
            al[f'KTc{h}'] = bu(8192 + h * 4608 + 4096, 512)
        al['Vall'] = bu(17408, 32 * 256)
        al['Vc'] = bu(17408 + 32 * 256, 1024)
        for c in range(16):
            al[f'UT{c}'] = bu(c * 1024, 1024)
            al[f'CT{c}'] = bu(20480 + c * 256, 256)
        for jj in range(8):
            for gg in range(4):
                al[f'US{jj}_{gg}'] = bu(gg * 4096, 4096)
        al['GBC'] = bu(16384, 4096)
        al['VB'] = bu(24576, 2048)

        def bank(b):
            return PS[:, b, :]

        def bank2(b):
            return PS[:, b:b + 2, :].rearrange("p a n -> p (a n)")

        def psr(b, n=1):
            return [f'PS{i}' for i in range(b, b + n)]

        ring_i = [0]

        def ring():
            i = ring_i[0] % 3
            ring_i[0] += 1
            return WR[i], f'WR{i}'

        def dma(eng, out, in_, reads, writes, key):
            op(eng, I('dma_start', out=out, in_=in_), reads=reads, writes=writes, dma=key)

        def dmas(eng, pairs, reads, writes, key, slow=False):
            def emit(e, semh, pairs=tuple(pairs)):
                for (o, i_) in pairs:
                    e.dma_start(out=o, in_=i_, allow_slow_non_contiguous=slow).then_inc(semh, 16)
            op(eng, emit, reads=reads, writes=writes, dma=key, ndma=len(pairs))

        def wrows(src2d, c0, ncols):
            return src2d[:, c0:c0 + ncols].rearrange("(k p) n -> p k n", p=128)

        xin_v = xin.rearrange("(j p) d -> p j d", p=128)
        dmas('sp', [(X[:, j, :], xin_v[:, j, :]) for j in range(16)], [], [f'X{j}' for j in range(16)], 'XIN')
        for (t, src, nm) in [(CST, condT, 'CST'), (BMT, bmodT, 'BMT'), (CONV, convT, 'CONV'), (IDENT, c_ident, 'IDENT'),
                             (IDENTF, c_identf, 'IDENTF'), (ONES, c_ones, 'ONES'), (ONESF, c_onesf, 'ONESF'),
                             (ROT, c_rot, 'ROT'), (MASK, c_mask, 'MASK'), (QK, qkT, 'QK')]:
            dma('sp', t[:], src, [], [nm], nm)
        op('dve', I('memset', EPSC[:], EPS), writes=['EPSC'])
        op('act', I('activation', out=CST[:], in_=CST[:], func=AF.Silu), reads=['CST'], writes=['CST'])
        op('dve', I('tensor_copy', out=CSB[:], in_=CST[:]), reads=['CST'], writes=['CSB'])

        def mod_phase(L):
            for vi, v in enumerate((0, 1, 3, 4)):
                for half in range(2):
                    W, wn = ring()
                    Wv = W[:].rearrange("p (k n) -> p k n", k=8)
                    dma('pool', Wv, wrows(w_mod[L], v * 1024 + half * 512, 512), [], [wn], wn)
                    lst = [I('matmul', PS[0:2, 0, :], lhsT=CSB[:, k, :], rhs=Wv[:, k, :], start=(k == 0), stop=(k == 7)) for k in range(8)]
                    op('pe', G(lst), reads=[wn, 'CSB'], writes=['PS0'])
                    op('act', I('activation', out=SMALL[0:2, :], in_=PS[0:2, 0, :], func=AF.Identity), reads=['PS0'], writes=['SMALL'])
                    lst = [I('transpose', out=PS[:, 1, cc * 2:cc * 2 + 2], in_=SMALL[0:2, cc * 128:(cc + 1) * 128], identity=IDENTF[0:2, 0:2]) for cc in range(4)]
                    op('pe', G(lst), reads=['SMALL', 'IDENTF'], writes=['PS1'])
                    bias_ap = BMT[:, L, v * 8 + half * 4:v * 8 + half * 4 + 4].unsqueeze(2).to_broadcast([128, 4, 2])
                    src = PS[:, 1, 0:8].rearrange("p (c g) -> p c g", g=2)
                    dst = MODT[:, vi, half * 4:half * 4 + 4, :]
                    if v in (1, 4):
                        op('dve', I('scalar_tensor_tensor', out=dst, in0=src, scalar=1.0, in1=bias_ap, op0=ALU.add, op1=ALU.add), reads=['PS1', 'BMT'], writes=['MODT'])
                    else:
                        op('dve', I('tensor_tensor', out=dst, in0=src, in1=bias_ap, op=ALU.add), reads=['PS1', 'BMT'], writes=['MODT'])

        def gates_phase(L, s_):
            v = (2, 5)[s_]
            CSREP = [XNB[g][:].rearrange("p (k n) -> p k n", k=8) for g in range(2)]
            for g in range(2):
                op('dve', I('tensor_copy', out=CSREP[g], in_=CST[:, :, g:g + 1].to_broadcast([128, 8, 128])), reads=['CST'], writes=[f'XNB{g}'])
            for half in range(2):
                W, wn = ring()
                Wv = W[:].rearrange("p (k n) -> p k n", k=8)
                c0 = v * 1024 + half * 512
                dma('pool', Wv, wrows(w_mod[L], c0, 512), [], [wn], wn)
                dma('sp', SMALL[:], b_mod[L:L + 1, c0:c0 + 512].partition_broadcast(128), [], ['SMALL'], 'SMALL')
                for g in range(2):
                    lst = [I('matmul', bank(1 + g), lhsT=CSREP[g][:, k, :], rhs=Wv[:, k, :], start=(k == 0), stop=(k == 7)) for k in range(8)]
                    op('pe', G(lst), reads=[wn, f'XNB{g}'], writes=psr(1 + g))
                    op('dve', I('tensor_tensor', out=GATE[g][:, half * 512:(half + 1) * 512], in0=bank(1 + g), in1=SMALL[:], op=ALU.add),
                       reads=psr(1 + g) + ['SMALL'], writes=[f'GATE{g}'])

        def load_ln(L, sub):
            dma('sp', LNG[:], ln_g[L, sub:sub + 1, :].partition_broadcast(128), [], ['LNG'], 'LNG')
            dma('sp', LNB[:], ln_b[L, sub:sub + 1, :].partition_broadcast(128), [], ['LNB'], 'LNB')

        def ln_stats(src_ap, res, i, width=1024):
            nch = width // 512
            lst = [I('bn_stats', out=STATS[i][:, c, :], in_=src_ap[:, c * 512:(c + 1) * 512]) for c in range(nch)]
            op('dve', G(lst), reads=res, writes=[f'STATS{i}'])
            op('dve', I('bn_aggr', out=MV[i][:], in_=STATS[i][:, 0:nch, :]), reads=[f'STATS{i}'], writes=[f'MV{i}'])
            op('act', I('activation', out=RS[i][:], in_=MV[i][:, 1:2], func=AF.Sqrt, bias=EPSC[:], scale=1.0),
               reads=[f'MV{i}', 'EPSC'], writes=[f'RS{i}'])
            op('dve', I('reciprocal', out=RS[i][:], in_=RS[i][:]), reads=[f'RS{i}'], writes=[f'RS{i}'])

        def ht_res(tiles, ks=range(8)):
            return [f'HT{jj}_{k}' for jj in tiles for k in ks]

        def prenorm(blk, sub):
            for _ in prenorm_pairs(blk, sub):
                pass

        def prenorm_pairs(blk, sub):
            g = blk
            for j0 in range(0, 8, 2):
                tiles = [(jj, blk * 8 + jj, jj % 2) for jj in (j0, j0 + 1)]
                for (jj, j, i) in tiles:
                    lst = [I('bn_stats', out=STATS[i][:, c, :], in_=X[:, j, c * 512:(c + 1) * 512]) for c in range(2)]
                    op('dve', G(lst), reads=[f'X{j}'], writes=[f'STATS{i}'])
                for (jj, j, i) in tiles:
                    op('dve', I('bn_aggr', out=MV[i][:], in_=STATS[i][:, 0:2, :]), reads=[f'STATS{i}'], writes=[f'MV{i}'])
                for (jj, j, i) in tiles:
                    op('act', I('activation', out=RS[i][:], in_=MV[i][:, 1:2], func=AF.Sqrt, bias=EPSC[:], scale=1.0), reads=[f'MV{i}', 'EPSC'], writes=[f'RS{i}'])
                for (jj, j, i) in tiles:
                    op('dve', I('reciprocal', out=RS[i][:], in_=RS[i][:]), reads=[f'RS{i}'], writes=[f'RS{i}'])
                for (jj, j, i) in tiles:
                    op('dve', I('tensor_scalar', out=XNB[i][:], in0=X[:, j, :], scalar1=MV[i][:, 0:1], scalar2=RS[i][:], op0=ALU.subtract, op1=ALU.mult),
                       reads=[f'X{j}', f'MV{i}', f'RS{i}'], writes=[f'XNB{i}'])
                for (jj, j, i) in tiles:
                    pb = 6 + i
                    pt = PS[:, pb, :].bitcast(BF16).rearrange("p (k n) -> p k n", k=8)
                    lst = [I('transpose', out=pt[:, k, :], in_=XNB[i][:, k * 128:(k + 1) * 128], identity=IDENT[:]) for k in range(8)]
                    op('pe', G(lst), reads=[f'XNB{i}', 'IDENT'], writes=psr(pb))
                for (jj, j, i) in tiles:
                    pb = 6 + i
                    pt = PS[:, pb, :].bitcast(BF16).rearrange("p (k n) -> p k n", k=8)
                    for k in range(8):
                        sc = MODT[:, 2 * sub + 1, k, g:g + 1]
                        sh = MODT[:, 2 * sub, k, g:g + 1]
                        dst = HT[:, k, jj * 128:(jj + 1) * 128]
                        op('dve', I('tensor_scalar', out=dst, in0=pt[:, k, :], scalar1=sc, scalar2=sh, op0=ALU.mult, op1=ALU.add),
                           reads=psr(pb) + ['MODT'], writes=[f'HT{jj}_{k}'])
                yield j0

        def epilogue4(blk, tg, sub, tiles=None):
            g = blk
            Tb = [U[:, 0, :], U[:, 1, :], A1[0][:, 0:1024], A1[1][:, 0:1024]]
            Tn = ['U0', 'U1', 'A1_0', 'A1_1']
            if tiles is None:
                tiles = [tg * 4 + t4 for t4 in range(4)]
            tl = [(4 + t4, blk * 8 + jj, Tb[t4], Tn[t4], t4 * 2) for t4, jj in enumerate(tiles)]
            for (i, j, T, tn, pb) in tl:
                op('dve', I('tensor_tensor', out=T, in0=bank2(pb), in1=GATE[g][:], op=ALU.mult), reads=psr(pb, 2) + [f'GATE{g}'], writes=[tn])
            for (i, j, T, tn, pb) in tl:
                op('dve', I('scalar_tensor_tensor', out=T, in0=X[:, j, :], scalar=ALPHA, in1=T, op0=ALU.mult, op1=ALU.add), reads=[tn, f'X{j}'], writes=[tn])
            for (i, j, T, tn, pb) in tl:
                lst = [I('bn_stats', out=STATS[i][:, c, :], in_=T[:, c * 512:(c + 1) * 512]) for c in range(2)]
                op('dve', G(lst), reads=[tn], writes=[f'STATS{i}'])
            for (i, j, T, tn, pb) in tl:
                op('dve', I('bn_aggr', out=MV[i][:], in_=STATS[i][:, 0:2, :]), reads=[f'STATS{i}'], writes=[f'MV{i}'])
            for (i, j, T, tn, pb) in tl:
                op('act', I('activation', out=RS[i][:], in_=MV[i][:, 1:2], func=AF.Sqrt, bias=EPSC[:], scale=1.0), reads=[f'MV{i}', 'EPSC'], writes=[f'RS{i}'])
            for (i, j, T, tn, pb) in tl:
                op('dve', I('reciprocal', out=RS[i][:], in_=RS[i][:]), reads=[f'RS{i}'], writes=[f'RS{i}'])
            for (i, j, T, tn, pb) in tl:
                op('dve', I('scalar_tensor_tensor', out=NMR[i][:], in0=MV[i][:, 0:1], scalar=-1.0, in1=RS[i][:], op0=ALU.mult, op1=ALU.mult),
                   reads=[f'MV{i}', f'RS{i}'], writes=[f'NMR{i}'])
            for (i, j, T, tn, pb) in tl:
                op('act', I('activation', out=T, in_=T, func=AF.Identity, scale=RS[i][:], bias=NMR[i][:]), reads=[tn, f'RS{i}', f'NMR{i}'], writes=[tn])
            for (i, j, T, tn, pb) in tl:
                op('dve', I('tensor_tensor', out=T, in0=T, in1=LNG[:], op=ALU.mult), reads=[tn, 'LNG'], writes=[tn])
            for (i, j, T, tn, pb) in tl:
                op('dve', I('tensor_tensor', out=X[:, j, :], in0=T, in1=LNB[:], op=ALU.add), reads=[tn, 'LNB'], writes=[f'X{j}'])

        def wd_load(Wd, k0, kn):
            W, wn = ring()
            Wv = W[:].rearrange("p (k n) -> p k n", k=4)
            dma('pool', Wv[:, 0:kn, :], Wd[k0 * 128:(k0 + kn) * 128, :].rearrange("(k p) n -> p k n", p=128), [], [wn], wn)
            return Wv, wn

        def proj_out(blk, sub, FT, ft_res, KC, Wd, gsz=4, preloaded=(), interleave=None):
            step = 0
            kgroups = [(k0, min(4, KC - k0)) for k0 in range(0, KC, 4)]
            tgroups = [list(range(t0, min(8, t0 + gsz))) for t0 in range(0, 8, gsz)]
            preloaded = list(preloaded)
            for tiles in tgroups:
                for (k0, kn) in kgroups:
                    if preloaded:
                        Wv, wn = preloaded.pop(0)
                    else:
                        Wv, wn = wd_load(Wd, k0, kn)
                    for t4, jj in enumerate(tiles):
                        lst = []
                        for hf in range(2):
                            for kk in range(kn):
                                lst.append(I('matmul', bank(t4 * 2 + hf), lhsT=FT[:, k0 + kk, jj * 128:(jj + 1) * 128],
                                             rhs=Wv[:, kk, hf * 512:(hf + 1) * 512], start=(k0 + kk == 0), stop=(k0 + kk == KC - 1)))
                        op('pe', G(lst), reads=[wn] + ft_res, writes=psr(t4 * 2, 2))
                    step += 1
                    if interleave is not None and step % 4 == 2:
                        next(interleave, None)
                epilogue4(blk, 0, sub, tiles)
            if interleave is not None:
                for _ in interleave:
                    pass

        def halo_exchange(idx, w):
            hi = halo_in[idx].ap()
            ho = halo_out[idx].ap()
            hiv = hi.rearrange("p (s k w) -> p s k w", s=2, k=8)
            dmas('sp', [(hiv[:, 0, :, :], HT[:, :, 0:w]), (hiv[:, 1, :, :], HT[:, :, 1024 - w:1024])],
                 ht_res([0, 7]), [f'halo_in{idx}'], f'hin{idx}', slow=True)
            op('pool', I('collective_compute', "AllGather", ALU.bypass, replica_groups=GROUPS, ins=[hi.opt()], outs=[ho.opt()]),
               reads=[f'halo_in{idx}'], writes=[f'halo_out{idx}'])
            op('pool', None, reads=[f'halo_out{idx}'])
            dma('sp', HALL[:, :, 0:16 * w], ho.rearrange("(r p) c -> p r c", p=128), [f'halo_out{idx}'], ['HALL'], 'HALL')
            hv = HALL[:, :, 0:16 * w].rearrange("p r (s k w) -> p r s k w", s=2, k=8)
            for side in range(2):
                src_s = 1 - side
                for r in range(4):
                    m = MASK[:, side * 4 + r:side * 4 + r + 1]
                    if r == 0:
                        op('dve', I('tensor_scalar', out=HSEL[:, side, :, 0:w], in0=hv[:, r, src_s, :, :], scalar1=m, scalar2=None, op0=ALU.mult),
                           reads=['HALL', 'MASK'], writes=[f'HSEL{side}'])
                    else:
                        op('dve', I('scalar_tensor_tensor', out=HSEL[:, side, :, 0:w], in0=hv[:, r, src_s, :, :], scalar=m,
                                    in1=HSEL[:, side, :, 0:w], op0=ALU.mult, op1=ALU.add),
                           reads=['HALL', 'MASK', f'HSEL{side}'], writes=[f'HSEL{side}'])

        def ffn_up(blk, L):
            sub = 1
            nseq, Ls = (4, 256) if blk == 0 else (1, 1024)
            GT = BIG[:, 0:NFF * 1024].rearrange("p (c n) -> p c n", c=NFF)
            if blk == 1:
                halo_exchange(L, 1)
                op('dve', I('tensor_copy', out=HTH[:, :, 0:1], in_=HSEL[:, 0, :, 0:1]), reads=['HSEL0'], writes=['HTH'])
                op('dve', I('tensor_copy', out=HTH[:, :, 1:2], in_=HSEL[:, 1, :, 0:1]), reads=['HSEL1', 'HTH'], writes=['HTH'])
            pair = 0
            for cg in range(11):
                W, wn = ring()
                Wv = W[:].rearrange("p (k n) -> p k n", k=8)
                c0 = cg * 256
                dmas('pool', [(Wv[:, :, 0:256], wrows(ffn_w_up[L], c0, 256)), (Wv[:, :, 256:512], wrows(ffn_w_up[L], DFF + c0, 256))], [], [wn], wn)
                for ci in range(2):
                    c = cg * 2 + ci
                    pa = (pair % 3) * 2
                    pbk = ((pair + 1) % 3) * 2
                    pair += 2
                    ai = c % 2
                    A = A1[ai]
                    an = f'A1_{ai}'
                    Av = A[:, 0:nseq * (Ls + 2)].rearrange("p (s l) -> p s l", s=nseq)
                    lst = [I('matmul', bank(pa + hf), lhsT=Wv[:, k, ci * 128:(ci + 1) * 128], rhs=HT[:, k, hf * 512:(hf + 1) * 512],
                             start=(k == 0), stop=(k == 7)) for hf in range(2) for k in range(8)]
                    op('pe', G(lst), reads=[wn] + ht_res(range(8)), writes=psr(pa, 2))
                    if blk == 1:
                        lst = [I('matmul', PS[:, 6, 0:2], lhsT=Wv[:, k, ci * 128:(ci + 1) * 128], rhs=HTH[:, k, :], start=(k == 0), stop=(k == 7)) for k in range(8)]
                        op('pe', G(lst), reads=[wn, 'HTH'], writes=['PS6'])
                    lst = [I('matmul', bank(pbk + hf), lhsT=Wv[:, k, 256 + ci * 128:256 + (ci + 1) * 128], rhs=HT[:, k, hf * 512:(hf + 1) * 512],
                             start=(k == 0), stop=(k == 7)) for hf in range(2) for k in range(8)]
                    op('pe', G(lst), reads=[wn] + ht_res(range(8)), writes=psr(pbk, 2))
                    a_ps = bank2(pa).rearrange("p (s l) -> p s l", s=nseq)
                    b_ps = bank2(pbk)
                    w0 = CONV[:, L, 0, c:c + 1]
                    w1 = CONV[:, L, 1, c:c + 1]
                    w2 = CONV[:, L, 2, c:c + 1]
                    cb = CONV[:, L, 3, c:c + 1]
                    Ui = U[:, c % 2, :]
                    un = f'U{c % 2}'
                    Uv = Ui.rearrange("p (s l) -> p s l", s=nseq)
                    if blk == 0:
                        op('pool', G([I('memset', Av[:, :, 0:1], 0.0), I('memset', Av[:, :, Ls + 1:Ls + 2], 0.0)]), writes=[an])
                    else:
                        op('act', G([I('activation', out=Av[:, 0, 0:1], in_=PS[:, 6, 0:1], func=AF.Identity),
                                     I('activation', out=Av[:, 0, Ls + 1:Ls + 2], in_=PS[:, 6, 1:2], func=AF.Identity)]), reads=['PS6'], writes=[an])
                    op('act', I('activation', out=Av[:, :, 1:Ls + 1], in_=a_ps, func=AF.Identity), reads=psr(pa, 2), writes=[an])
                    op('act', I('activation', out=Uv, in_=a_ps, func=AF.Identity, scale=w1, bias=cb), reads=psr(pa, 2) + ['CONV'], writes=[un])
                    op('dve', I('scalar_tensor_tensor', out=Uv, in0=Av[:, :, 0:Ls], scalar=w0, in1=Uv, op0=ALU.mult, op1=ALU.add),
                       reads=[an, un, 'CONV'], writes=[un])
                    op('dve', I('scalar_tensor_tensor', out=Uv, in0=Av[:, :, 2:Ls + 2], scalar=w2, in1=Uv, op0=ALU.mult, op1=ALU.add),
                       reads=[an, un, 'CONV'], writes=[un])
                    op('act', I('activation', out=Ui, in_=Ui, func=AF.Gelu_apprx_tanh), reads=[un], writes=[un])
                    op('dve', I('tensor_tensor', out=GT[:, c, :], in0=Ui, in1=b_ps, op=ALU.mult), reads=[un] + psr(pbk, 2), writes=[f'GT{c}'])

        def ffn_down(blk, L, interleave=None):
            GT = BIG[:, 0:NFF * 1024].rearrange("p (c n) -> p c n", c=NFF)
            proj_out(blk, 1, GT, [f'GT{c}' for c in range(NFF)], NFF, ffn_w_down[L], gsz=3, interleave=interleave)

        def fnet_a(blk):
            ABs = BIG[:, 0:16384].rearrange("p (j c) -> p j c", j=8)
            CSs = SMALL[:].bitcast(BF16).rearrange("p (k n) -> p k n", k=2)
            dma('sp', CSs, c_cs, [], ['SMALL'], 'SMALL')
            for jj in range(8):
                lst = [I('matmul', bank(g), lhsT=HT[:, 2 * g + kk, jj * 128:(jj + 1) * 128], rhs=CSs[:, kk, :], start=(kk == 0), stop=(kk == 1))
                       for g in range(4) for kk in range(2)]
                op('pe', G(lst), reads=ht_res([jj]) + ['SMALL'], writes=psr(0, 4))
                src = PS[:, 0:4, :].rearrange("p a n -> p (a n)")
                if jj % 2 == 0:
                    op('act', I('activation', out=ABs[:, jj, :], in_=src, func=AF.Identity), reads=psr(0, 4), writes=[f'AB{jj}'])
                else:
                    op('dve', I('tensor_copy', out=ABs[:, jj, :], in_=src), reads=psr(0, 4), writes=[f'AB{jj}'])
                if blk == 1 and jj % 2 == 1:
                    q_ = jj // 2
                    abi = ab_in[q_].ap()
                    abo = ab_out[q_].ap()
                    dma('sp', abi.rearrange("(t p) c -> p t c", p=128), ABs[:, jj - 1:jj + 1, :], [f'AB{jj - 1}', f'AB{jj}'], [f'ab_in{q_}'], f'ab_in{q_}')
                    op('pool', I('collective_compute', "AllGather", ALU.bypass, replica_groups=GROUPS, ins=[abi.opt()], outs=[abo.opt()]),
                       reads=[f'ab_in{q_}'], writes=[f'ab_out{q_}'])
                    op('pool', None, reads=[f'ab_out{q_}'])

        def fnet_b(blk, preloaded=()):
            ABs = BIG[:, 0:16384].rearrange("p (j c) -> p j c", j=8)
            FT = BIG[:, 16384:24576].rearrange("p (k n) -> p k n", k=8)

            def evac(m, mi):
                if m % 2 == 0:
                    op('act', I('activation', out=FT[:, m, :], in_=bank2(mi * 2), func=AF.Identity), reads=psr(mi * 2, 2), writes=[f'FT{m}'])
                else:
                    op('dve', I('tensor_copy', out=FT[:, m, :], in_=bank2(mi * 2)), reads=psr(mi * 2, 2), writes=[f'FT{m}'])
            if blk == 0:
                W, wn = ring()
                CTP = W[:, 0:1024].rearrange("p (k n) -> p k n", k=2)
                dma('sp', CTP, c_ctp, [], [wn], wn)
                for mg in range(2):
                    for mi in range(4):
                        m = mg * 4 + mi
                        g, mm_ = m // 2, m % 2
                        lst = []
                        for s_ in range(4):
                            n = 0
                            for kk in range(2):
                                for ab in range(2):
                                    c0 = g * 512 + ab * 256 + mm_ * 128
                                    lst.append(I('matmul', PS[:, mi * 2 + s_ // 2, (s_ % 2) * 256:(s_ % 2) * 256 + 256],
                                                 lhsT=ABs[:, s_ * 2 + kk, c0:c0 + 128], rhs=CTP[:, kk, ab * 256:(ab + 1) * 256],
                                                 start=(n == 0), stop=(n == 3)))
                                    n += 1
                        op('pe', G(lst), reads=[wn] + [f'AB{j}' for j in range(8)], writes=psr(mi * 2, 2))
                        evac(m, mi)
            else:
                for mg in range(2):
                    for tc in range(32):
                        r_, j_ = tc // 8, tc % 8
                        W, wn = ring()
                        Aw = W[:, 0:1024].rearrange("p (g n) -> p g n", g=2)
                        Cw = W[:, 1024:2048]
                        Sw = W[:, 2048:3072]
                        q_, t_ = j_ // 2, j_ % 2
                        r0 = r_ * 256 + t_ * 128
                        dmas('sp', [(Aw, ab_out[q_].ap()[r0:r0 + 128, mg * 1024:(mg + 1) * 1024].rearrange("p (g n) -> p g n", g=2)),
                                    (Cw, c_cts[tc * 128:(tc + 1) * 128, :]), (Sw, c_nsts[tc * 128:(tc + 1) * 128, :])], [f'ab_out{q_}'], [wn], wn)
                        lst = []
                        for mi in range(4):
                            gl, mm_ = mi // 2, mi % 2
                            for hf in range(2):
                                for ab in range(2):
                                    rhs = (Cw if ab == 0 else Sw)[:, hf * 512:(hf + 1) * 512]
                                    lst.append(I('matmul', bank(mi * 2 + hf), lhsT=Aw[:, gl, ab * 256 + mm_ * 128:ab * 256 + (mm_ + 1) * 128], rhs=rhs,
                                                 start=(tc == 0 and ab == 0), stop=(tc == 31 and ab == 1)))
                        op('pe', G(lst), reads=[wn], writes=psr(0, 8))
                    for mi in range(4):
                        evac(mg * 4 + mi, mi)
            proj_out(blk, 0, FT, [f'FT{m}' for m in range(8)], 8, fnet_w_o[0], preloaded=preloaded)

        def attn(blk):
            ASTG = (DEBUG_STAGE or 99) % 10
            if DEBUG_STAGE is not None and DEBUG_STAGE < 10 and blk == 0:
                return
            QT = BIG[:, 0:8192].rearrange("p (h n) -> p h n", h=8)
            KT = BIG[:, 8192:8192 + 2 * 4608].rearrange("p (h n) -> p h n", h=2)
            VV = BIG[:, 17408:17408 + 36 * 256].rearrange("p (t c) -> p t c", t=36)
            wq = attn_w_qkv[0]
            if blk == 1:
                dma('sp', A1[0][:, 0:1024], c_cos, [], ['COS'], 'COS')
                dma('sp', A1[1][:, 0:1024], c_sin, [], ['SIN'], 'SIN')
            for hg in range(3):
                W, wn = ring()
                nh = 4 if hg < 2 else 2
                Wv = W[:].rearrange("p (k n) -> p k n", k=8)
                dma('pool', Wv[:, :, 0:nh * 128], wrows(wq, hg * 512, nh * 128), [], [wn], wn)
                for hi in range(nh):
                    isq = hg < 2
                    h = hg * 4 + hi if isq else hi
                    gain = QK[:, 0:1] if isq else QK[:, 1:2]
                    for hf in range(2):
                        cols = slice(hf * 512, (hf + 1) * 512)
                        par = (hi * 2 + hf) % 2
                        pb = par * 3
                        lst = [I('matmul', bank(pb), lhsT=Wv[:, k, hi * 128:(hi + 1) * 128], rhs=HT[:, k, cols], start=(k == 0), stop=(k == 7)) for k in range(8)]
                        op('pe', G(lst), reads=[wn] + ht_res(range(hf * 4, hf * 4 + 4)), writes=psr(pb))
                        Q32 = U[:, par, 0:512]
                        SQ = U[:, par, 512:1024]
                        qn, sn = f'U{par}q', f'U{par}s'
                        op('act', I('activation', out=Q32, in_=bank(pb), func=AF.Identity), reads=psr(pb), writes=[qn])
                        op('act', I('activation', out=SQ, in_=bank(pb), func=AF.Square), reads=psr(pb), writes=[sn])
                        op('pe', I('matmul', bank(pb + 1), lhsT=ONESF[:], rhs=SQ, start=True, stop=True), reads=[sn, 'ONESF'], writes=psr(pb + 1))
                        op('act', I('activation', out=SQ, in_=bank(pb + 1), func=AF.Sqrt, bias=EPSC[:], scale=1.0 / 128), reads=psr(pb + 1) + ['EPSC'], writes=[sn])
                        op('dve', I('reciprocal', out=SQ, in_=SQ), reads=[sn], writes=[sn])
                        dstT = QT[:, h, cols] if isq else KT[:, h, cols]
                        dres = f'QT{h}_{hf}' if isq else f'KT{h}_{hf}'
                        if blk == 0 and isq:
                            op('dve', I('scalar_tensor_tensor', out=dstT, in0=Q32, scalar=gain, in1=SQ, op0=ALU.mult, op1=ALU.mult),
                               reads=[qn, sn, 'QK'], writes=[dres])
                            continue
                        op('dve', I('scalar_tensor_tensor', out=Q32, in0=Q32, scalar=gain, in1=SQ, op0=ALU.mult, op1=ALU.mult),
                           reads=[qn, sn, 'QK'], writes=[qn])
                        if blk == 0:
                            op('act', I('activation', out=dstT, in_=Q32, func=AF.Identity), reads=[qn], writes=[dres])
                            NKS = A1[hf][:, 0:1024].rearrange("p (t c) -> p t c", t=4)
                            lst = [I('transpose', out=PS[:, pb + 2, t * 128:(t + 1) * 128], in_=Q32[:, t * 128:(t + 1) * 128], identity=IDENTF[:]) for t in range(4)]
                            op('pe', G(lst), reads=[qn, 'IDENTF'], writes=psr(pb + 2))
                            op('dve', I('tensor_copy', out=NKS[:, :, h * 128:(h + 1) * 128], in_=bank(pb + 2).rearrange("p (t c) -> p t c", t=4)),
                               reads=psr(pb + 2), writes=[f'NKS{hf}_{h}'])
                        else:
                            op('pe', I('matmul', bank(pb + 2), lhsT=ROT[:], rhs=Q32, start=True, stop=True), reads=[qn, 'ROT'], writes=psr(pb + 2))
                            op('dve', I('tensor_tensor', out=SQ, in0=bank(pb + 2), in1=A1[1][:, cols], op=ALU.mult), reads=psr(pb + 2) + ['SIN', sn], writes=[sn])
                            op('dve', I('tensor_tensor', out=Q32, in0=Q32, in1=A1[0][:, cols], op=ALU.mult), reads=[qn, 'COS'], writes=[qn])
                            op('dve', I('tensor_tensor', out=dstT, in0=Q32, in1=SQ, op=ALU.add), reads=[qn, sn], writes=[dres])
            if ASTG == 1:
                return
            if blk == 0:
                for hf in range(2):
                    dma('sp', nk_out[hf * 512:(hf + 1) * 512, :].rearrange("(t p) c -> p t c", p=128), A1[hf][:, 0:1024].rearrange("p (t c) -> p t c", t=4),
                        [f'NKS{hf}_0', f'NKS{hf}_1'], [f'nk_out{hf}'], f'nk_out{hf}')
            W, wn = ring()
            Wv = W[:, 0:2048].rearrange("p (k n) -> p k n", k=8)
            dma('pool', Wv, wrows(wq, 1280, 256), [], [wn], wn)
            VS = U[:].rearrange("p a n -> p (a n)").rearrange("p (t c) -> p t c", t=8)
            for jj in range(8):
                pb = jj % 2
                lst = [I('matmul', PS[:, pb, 0:256], lhsT=HT[:, k, jj * 128:(jj + 1) * 128], rhs=Wv[:, k, :], start=(k == 0), stop=(k == 7)) for k in range(8)]
                op('pe', G(lst), reads=[wn] + ht_res([jj]), writes=psr(pb))
                if blk == 0:
                    op('dve', I('tensor_copy', out=VS[:, jj, :], in_=PS[:, pb, 0:256]), reads=psr(pb), writes=[f'VS{jj}'])
                    op('act', I('activation', out=VV[:, jj, :], in_=VS[:, jj, :], func=AF.Identity), reads=[f'VS{jj}'], writes=[f'V{jj}'])
                else:
                    op('act', I('activation', out=VV[:, jj, :], in_=PS[:, pb, 0:256], func=AF.Identity), reads=psr(pb), writes=[f'V{jj}'])
            if blk == 0:
                dma('sp', nv_out.rearrange("(t p) c -> p t c", p=128), VS, [f'VS{jj}' for jj in range(8)], ['nv_out'], 'nv_out')
            kres = [f'KT{h}_{hf}' for h in range(2) for hf in range(2)]
            vres = [f'V{jj}' for jj in range(8)]
            if blk == 1:
                ki, ko, vi, vo = k_in.ap(), k_out.ap(), v_in.ap(), v_out.ap()
                dma('sp', ki.rearrange("p (h n) -> p h n", h=2), KT[:, :, 0:1024], kres, ['k_in'], 'k_in')
                dma('sp', vi.rearrange("p (t c) -> p t c", t=8), VV[:, 0:8, :], vres, ['v_in'], 'v_in')
                op('pool', I('collective_compute', "AllGather", ALU.bypass, replica_groups=GROUPS, ins=[ki.opt()], outs=[ko.opt()]), reads=['k_in'], writes=['k_out'])
                op('pool', None, reads=['k_out'])
                op('pool', I('collective_compute', "AllGather", ALU.bypass, replica_groups=GROUPS, ins=[vi.opt()], outs=[vo.opt()]), reads=['v_in'], writes=['v_out'])
                op('pool', None, reads=['v_out'])
                kov = ko.rearrange("(r p) (h n) -> h p r n", p=128, h=2)
                for h in range(2):
                    dma('sp', KT[:, h, 0:4096].rearrange("p (r n) -> p r n", r=4), kov[h], ['k_out'], [f'KTall{h}'], f'KTall{h}')
                dma('sp', VV[:, 0:32, :].rearrange("p (r t) c -> p r t c", r=4), vo.rearrange("(r p) (t c) -> p r t c", p=128, t=8), ['v_out'], ['Vall'], 'Vall')
                W, wn = ring()
                CK = W[:, 0:1024].rearrange("p (t c) -> p t c", t=4)
                dma('pool', CK, cache_k.rearrange("(t p) c -> p t c", p=128), [], [wn], wn)
                dma('pool', VV[:, 32:36, :], cache_v.rearrange("(t p) c -> p t c", p=128), [], ['Vc'], 'Vc')
                ptb = PS[:, 7, :].bitcast(BF16).rearrange("p (k n) -> p k n", k=8)
                lst = [I('transpose', out=ptb[:, t * 2 + h, :], in_=CK[:, t, h * 128:(h + 1) * 128], identity=IDENT[:]) for t in range(4) for h in range(2)]
                op('pe', G(lst), reads=[wn, 'IDENT'], writes=psr(7))
                for h in range(2):
                    op('dve', I('tensor_copy', out=KT[:, h, 4096:4608].rearrange("p (t n) -> p t n", t=4), in_=ptb.rearrange("p (t h) n -> p h t n", h=2)[:, h]),
                       reads=psr(7), writes=[f'KTc{h}'])
                kres = ['KTall0', 'KTall1', 'KTc0', 'KTc1']
                vres = ['Vall', 'Vc']
            if ASTG == 2:
                return
            scale = 128 ** -0.5
            a0 = A1[0][:].bitcast(BF16)
            a1 = A1[1][:].bitcast(BF16)
            PT2 = [a0[:, 0:512], a0[:, 1024:1536], a1[:, 0:512], a1[:, 1024:1536]]
            ptn = ['P0', 'P1', 'P2', 'P3']
            cnt = 0
            if blk == 0:
                units = [(h, s_ * 256, 256, [2 * s_, 2 * s_ + 1], h // 4) for h in range(8) for s_ in range(4)]
            else:
                units = [(h, hf * 512, 512, list(range(36)), h // 4) for h in range(8) for hf in range(2)]
            steps = []
            for ui_, (h, q0, nq, ktiles, kv) in enumerate(units):
                for ti, kt in enumerate(ktiles):
                    steps.append((ui_, ti, kt))
            LA = 3

            def emit_S(si):
                ui_, ti, kt = steps[si]
                h, q0, nq, ktiles, kv = units[ui_]
                sb_ = 4 + (si % 4)
                pi = si % 4
                op('pe', I('matmul', PS[:, sb_, 0:nq], lhsT=KT[:, kv, kt * 128:(kt + 1) * 128], rhs=QT[:, h, q0:q0 + nq], start=True, stop=True),
                   reads=kres + [f'QT{h}_{q0 // 512}'], writes=psr(sb_))
                op('act', I('activation', out=PT2[pi][:, 0:nq], in_=PS[:, sb_, 0:nq], func=AF.Exp, scale=scale), reads=psr(sb_), writes=[ptn[pi]])

            def emit_PV(si):
                ui_, ti, kt = steps[si]
                h, q0, nq, ktiles, kv = units[ui_]
                nkt = len(ktiles)
                pi = si % 4
                pacc = (ui_ % 2) * 2
                op('pe', G([I('matmul', PS[:, pacc, 0:nq], lhsT=VV[:, kt, kv * 128:(kv + 1) * 128], rhs=PT2[pi][:, 0:nq], start=(ti == 0), stop=(ti == nkt - 1)),
                            I('matmul', PS[:, pacc + 1, 0:nq], lhsT=ONES[:], rhs=PT2[pi][:, 0:nq], start=(ti == 0), stop=(ti == nkt - 1))]),
                   reads=vres + [ptn[pi], 'ONES'], writes=psr(pacc, 2))
                if ti == nkt - 1:
                    ui2 = ui_ % 2
                    R = U[:, ui2, 0:nq]
                    op('dve', I('reciprocal', out=R, in_=PS[:, pacc + 1, 0:nq]), reads=psr(pacc + 1), writes=[f'U{ui2}q'])
                    op('dve', I('tensor_tensor', out=QT[:, h, q0:q0 + nq], in0=PS[:, pacc, 0:nq], in1=R, op=ALU.mult), reads=psr(pacc) + [f'U{ui2}q'], writes=[f'OT{h}_{q0}'])
            for i_ in range(len(steps) + LA):
                if i_ < len(steps):
                    emit_S(i_)
                if i_ - LA >= 0:
                    emit_PV(i_ - LA)
            if ASTG == 3:
                return
            ores = [f'OT{h}_{q0}' for (h, q0, nq, kt, kv) in units]
            proj_out(blk, 0, QT, ores, 8, attn_w_o[0])

        def pool_mix(blk):
            nseq, Ls = (4, 256) if blk == 0 else (1, 1024)
            PT_ = BIG[:, 0:8192].rearrange("p (k n) -> p k n", k=8)
            Lp = Ls + 16
            if blk == 1:
                halo_exchange(4, 8)
            W, wn = ring()
            PW = W[:, 0:2048].rearrange("p (g k n) -> p g k n", g=4, k=2)
            dma('pool', PW, pool_w[0].rearrange("g (k p) n -> p g k n", p=128), [], [wn], wn)
            RC = SMALL[:, 0:128].rearrange("p (g s w) -> p g s w", g=4, s=2)
            dma('sp', RC, c_rc[:, blk], [], ['SMALL'], 'SMALL')
            dma('sp', LNG[:], pool_scale[0:1, :].partition_broadcast(128), [], ['LNG'], 'LNG')
            op('dve', I('tensor_tensor', out=GATE[blk][:], in0=GATE[blk][:], in1=LNG[:], op=ALU.mult), reads=['LNG', f'GATE{blk}'], writes=[f'GATE{blk}'])
            load_ln(2, 0)
            for k in range(8):
                g = k // 2
                hh = POOL_HALF[g]
                Pa = A1[0][:, 0:nseq * Lp].rearrange("p (s l) -> p s l", s=nseq)
                Pb = A1[1][:, 0:nseq * Lp].rearrange("p (s l) -> p s l", s=nseq)
                hsrc = HT[:, k, :].rearrange("p (s l) -> p s l", s=nseq)
                if blk == 0:
                    op('pool', G([I('memset', Pa[:, :, 0:8], 0.0), I('memset', Pa[:, :, Ls + 8:Ls + 16], 0.0)]), writes=['A1_0'])
                else:
                    op('dve', I('tensor_copy', out=Pa[:, 0, 0:8], in_=HSEL[:, 0, k, :]), reads=['HSEL0'], writes=['A1_0'])
                    op('dve', I('tensor_copy', out=Pa[:, 0, Ls + 8:Ls + 16], in_=HSEL[:, 1, k, :]), reads=['HSEL1'], writes=['A1_0'])
                op('act', I('activation', out=Pa[:, :, 8:Ls + 8], in_=hsrc, func=AF.Identity), reads=ht_res(range(8), [k]), writes=['A1_0'])
                cur, curn, oth, othn = Pa, 'A1_0', Pb, 'A1_1'
                width = 1
                valid = Lp
                while width < 2 * hh:
                    nv_ = valid - width
                    op('dve', I('tensor_tensor', out=oth[:, :, 0:nv_], in0=cur[:, :, 0:nv_], in1=cur[:, :, width:width + nv_], op=ALU.add),
                       reads=[curn], writes=[othn])
                    cur, curn, oth, othn = oth, othn, cur, curn
                    valid = nv_
                    width *= 2
                Uv = U[:, 0, :].rearrange("p (s l) -> p s l", s=nseq)
                op('dve', I('tensor_scalar', out=Uv, in0=cur[:, :, 8 - hh:8 - hh + Ls], scalar1=1.0 / (2 * hh), scalar2=None, op0=ALU.mult),
                   reads=[curn], writes=['U0'])
                op('dve', I('tensor_tensor', out=Uv[:, :, 0:16], in0=cur[:, :, 8 - hh:8 - hh + 16],
                            in1=RC[:, g, 0, :].unsqueeze(1).to_broadcast([128, nseq, 16]), op=ALU.mult), reads=[curn, 'U0', 'SMALL'], writes=['U0'])
                op('dve', I('tensor_tensor', out=Uv[:, :, Ls - 16:Ls], in0=cur[:, :, 8 - hh + Ls - 16:8 - hh + Ls],
                            in1=RC[:, g, 1, :].unsqueeze(1).to_broadcast([128, nseq, 16]), op=ALU.mult), reads=[curn, 'U0', 'SMALL'], writes=['U0'])
                op('dve', I('tensor_tensor', out=PT_[:, k, :].rearrange("p (s l) -> p s l", s=nseq), in0=Uv, in1=hsrc, op=ALU.subtract),
                   reads=['U0'] + ht_res(range(8), [k]), writes=[f'PT{k}'])
            for tg in range(2):
                for t4 in range(4):
                    jj = tg * 4 + t4
                    lst = [I('matmul', PS[:, t4 * 2 + g // 2, (g % 2) * 256:(g % 2) * 256 + 256], lhsT=PT_[:, 2 * g + kk, jj * 128:(jj + 1) * 128],
                             rhs=PW[:, g, kk, :], start=(kk == 0), stop=(kk == 1)) for g in range(4) for kk in range(2)]
                    op('pe', G(lst), reads=[wn] + [f'PT{k}' for k in range(8)], writes=psr(t4 * 2, 2))
                epilogue4(blk, tg, 0)

        def gmlp(blk, first):
            UT = BIG[:, 0:16384].rearrange("p (c n) -> p c n", c=16)
            GBC = BIG[:, 16384:20480].bitcast(F32)
            CT = BIG[:, 20480:24576].bitcast(F32).rearrange("p (c q) -> p c q", c=16)
            VB = BIG[:, 24576:26624]
            win = gmlp_w_in[0]
            GBT = GB[:, 0:16]
            GLB = GB[:, 16:32]
            if first:
                dma('sp', GBC, gmlp_ln_g[0:1, :].partition_broadcast(128), [], ['GBC'], 'GBC')
                dma('sp', SMALL[:], gmlp_b_s[0:1, :].partition_broadcast(128), [], ['SMALL'], 'SMALL')
                dmas('sp', [(GBT, gbinT), (GLB, glnbT)], [], ['GB'], 'GB')
                dma('pool', HALL[:], wsT, [], ['WST'], 'WST')
                lst = [I('matmul', PS[:, 0, g * 128:(g + 1) * 128], lhsT=ONES[:], rhs=HALL[:, g, :], start=True, stop=True) for g in range(4)]
                op('pe', G(lst), reads=['WST', 'ONES'], writes=psr(0))
                for c in range(16):
                    g = c // 4
                    op('dve', I('scalar_tensor_tensor', out=CT[:, c, :], in0=PS[:, 0, g * 128:(g + 1) * 128], scalar=GLB[:, c:c + 1],
                                in1=SMALL[:, g * 128:(g + 1) * 128], op0=ALU.mult, op1=ALU.add), reads=psr(0) + ['GB', 'SMALL'], writes=[f'CT{c}'])
            BROW = A1[0][0:1, 0:1024].bitcast(BF16)
            dma('pool', BROW, gmlp_b_in[0:1, 2048:4096], [], ['BROW'], 'BROW')
            for cg in range(4):
                W, wn = ring()
                Wv = W[:].rearrange("p (k n) -> p k n", k=8)
                dma('pool', Wv, wrows(win, cg * 512, 512), [], [wn], wn)
                for ci in range(4):
                    c = cg * 4 + ci
                    pb = (c % 2) * 2
                    lst = [I('matmul', bank(pb + hf), lhsT=Wv[:, k, ci * 128:(ci + 1) * 128], rhs=HT[:, k, hf * 512:(hf + 1) * 512], start=(k == 0), stop=(k == 7))
                           for hf in range(2) for k in range(8)]
                    op('pe', G(lst), reads=[wn] + ht_res(range(8)), writes=psr(pb, 2))
                    op('act', I('activation', out=UT[:, c, :], in_=bank2(pb), func=AF.Gelu_apprx_tanh, bias=GBT[:, c:c + 1], scale=1.0),
                       reads=psr(pb, 2) + ['GB'], writes=[f'UT{c}'])
            VL = U[:].rearrange("p a n -> p (a n)")
            for tp in range(4):
                for cb in range(4):
                    W, wn = ring()
                    Wv = W[:].rearrange("p (k n) -> p k n", k=8)
                    dma('pool', Wv, wrows(win, 2048 + cb * 512, 512), [], [wn], wn)
                    for t2 in range(2):
                        jj = tp * 2 + t2
                        lst = [I('matmul', bank(t2 * 4 + cb), lhsT=HT[:, k, jj * 128:(jj + 1) * 128], rhs=Wv[:, k, :], start=(k == 0), stop=False) for k in range(8)]
                        lst.append(I('matmul', bank(t2 * 4 + cb), lhsT=ONES[0:1, :], rhs=BROW[:, cb * 512:(cb + 1) * 512], start=False, stop=True))
                        op('pe', G(lst), reads=[wn, 'BROW', 'ONES'] + ht_res([jj]), writes=psr(t2 * 4 + cb))
                for t2 in range(2):
                    jj = tp * 2 + t2
                    src = PS[:, t2 * 4:t2 * 4 + 4, :].rearrange("p a n -> p (a n)")
                    op('act', I('activation', out=VL, in_=src, func=AF.Gelu_apprx_tanh), reads=psr(t2 * 4, 4), writes=['VL'])
                    ln_stats(VL, ['VL'], t2, width=2048)
                    op('dve', I('tensor_scalar', out=VL, in0=VL, scalar1=MV[t2][:, 0:1], scalar2=RS[t2][:], op0=ALU.subtract, op1=ALU.mult),
                       reads=['VL', f'MV{t2}', f'RS{t2}'], writes=['VL'])
                    op('dve', I('tensor_tensor', out=VB, in0=VL, in1=GBC, op=ALU.mult), reads=['VL', 'GBC'], writes=['VB'])
                    lst = [I('matmul', PS[:, t2 * 4 + c // 4, (c % 4) * 128:(c % 4) * 128 + 128], lhsT=VB[:, c * 128:(c + 1) * 128], rhs=HALL[:, c // 4, :], start=True, stop=True)
                           for c in range(16)]
                    op('pe', G(lst), reads=['VB', 'WST'], writes=psr(t2 * 4, 4))
                    Sx = SMALL[:].rearrange("p (c q) -> p c q", c=4)
                    for gg in range(4):
                        op('dve', I('tensor_tensor', out=Sx, in0=PS[:, t2 * 4 + gg, :].rearrange("p (c q) -> p c q", c=4), in1=CT[:, gg * 4:gg * 4 + 4, :], op=ALU.add),
                           reads=psr(t2 * 4 + gg) + [f'CT{c}' for c in range(gg * 4, gg * 4 + 4)], writes=['SMALL'])
                        utv = UT[:, gg * 4:gg * 4 + 4, jj * 128:(jj + 1) * 128]
                        op('dve', I('tensor_tensor', out=utv, in0=Sx, in1=utv, op=ALU.mult),
                           reads=['SMALL'] + [f'UT{c}' for c in range(gg * 4, gg * 4 + 4)], writes=[f'US{jj}_{gg}'])
            ures = [f'US{jj}_{gg}' for jj in range(8) for gg in range(4)]
            proj_out(blk, 0, UT, ures, 16, gmlp_w_o[0])

        pre_done = False
        for L in range(DEPTH):
            if not pre_done:
                mod_phase(L)
            gates_phase(L, 0)
            load_ln(L, 0)
            if L == 0:
                prenorm(1, 0)
                pre = [wd_load(fnet_w_o[0], 0, 4), wd_load(fnet_w_o[0], 512 // 128, 4)]
                fnet_a(1)
                prenorm(0, 0)
                fnet_a(0)
                fnet_b(0, preloaded=pre)
                fnet_b(1)
            else:
                for blk in (1, 0):
                    if not (blk == 1 and pre_done):
                        prenorm(blk, 0)
                    if L == 1:
                        attn(blk)
                    elif L == 2:
                        pool_mix(blk)
                    else:
                        gmlp(blk, blk == 1)
            gates_phase(L, 1)
            load_ln(L, 1)
            prenorm(1, 1)
            ffn_up(1, L)
            ffn_down(1, L, interleave=prenorm_pairs(0, 1))
            ffn_up(0, L)
            pre_done = False
            if L + 1 < DEPTH:
                mod_phase(L + 1)
                ffn_down(0, L, interleave=prenorm_pairs(1, 0))
                pre_done = True
            else:
                ffn_down(0, L)

        yv = yout.rearrange("(j p) d -> p j d", p=128)
        outs = ['yout']
        dmas('sp', [(yv[:, j, :], X[:, j, :]) for j in range(16)], [f'X{j}' for j in range(16)], ['yout'], 'yout')
        fin = [r for r in outs + ['nk_out0', 'nk_out1', 'nv_out'] if r in S.res_w]
        op('sp', None, reads=fin)

        names = S.sem_names()
        sems = {n: st.enter_context(nc.semaphore(n)) for n in names}
        with nc.Block() as block:
            @block.tensor
            def _(e):
                S.replay('pe', e, sems)

            @block.scalar
            def _(e):
                S.replay('act', e, sems)

            @block.vector
            def _(e):
                S.replay('dve', e, sems)

            @block.gpsimd
            def _(e):
                S.replay('pool', e, sems)

            @block.sync
            def _(e):
                S.replay('sp', e, sems)
    return nc


def _consts():
    bf = ml_dtypes.bfloat16
    c = {}
    c['c_ident'] = np.eye(128, dtype=np.float32).astype(bf)
    c['c_identf'] = np.eye(128, dtype=np.float32)
    c['c_ones'] = np.ones((128, 128), np.float32).astype(bf)
    c['c_onesf'] = np.ones((128, 128), np.float32)
    rot = np.zeros((128, 128), np.float32)
    for d in range(128):
        i = d % 64
        partner = d + 32 if i < 32 else d - 32
        rot[partner, d] = 1.0
    c['c_rot'] = rot
    n = np.arange(256)
    ang = 2 * np.pi * ((n[:, None] * n[None, :]) % 256) / 256.0
    cs = np.concatenate([np.cos(ang), np.sin(ang)], axis=1)
    c['c_cs'] = np.ascontiguousarray(cs.reshape(2, 128, 512).transpose(1, 0, 2)).astype(np.float32).astype(bf)
    ctp = np.concatenate([np.cos(ang) / 256.0, -np.sin(ang) / 256.0], axis=1)
    c['c_ctp'] = np.ascontiguousarray(ctp.reshape(2, 128, 512).transpose(1, 0, 2)).astype(np.float32).astype(bf)
    return c


def _core_consts(core):
    bf = ml_dtypes.bfloat16
    qi = core % 4
    c = {}
    t = np.arange(4096, dtype=np.int64)
    tp = qi * 1024 + np.arange(1024, dtype=np.int64)
    ang = 2 * np.pi * ((t[:, None] * tp[None, :]) % 4096) / 4096.0
    c['c_cts'] = (np.cos(ang) / 1024.0).astype(np.float32).astype(bf)
    c['c_nsts'] = (-np.sin(ang) / 1024.0).astype(np.float32).astype(bf)
    row = (tp // 64).astype(np.float64)
    col = (tp % 64).astype(np.float64)
    inv = 10000.0 ** (-np.arange(32, dtype=np.float64) / 32)
    cos = np.zeros((128, 1024)); sin = np.zeros((128, 1024))
    for d in range(128):
        pos = row if d < 64 else col
        i = d % 64
        a = pos * inv[i % 32]
        cos[d] = np.cos(a)
        sin[d] = np.sin(a) * (-1.0 if i < 32 else 1.0)
    c['c_cos'] = cos.astype(np.float32)
    c['c_sin'] = sin.astype(np.float32)
    mask = np.zeros((128, 8), np.float32)
    if qi > 0:
        mask[:, qi - 1] = 1.0
    if qi < 3:
        mask[:, 4 + qi + 1] = 1.0
    c['c_mask'] = mask
    rc = np.zeros((2, 4, 2, 16), np.float64)
    for g, hh in enumerate(POOL_HALF):
        T = 256
        tt = np.arange(16)
        rc[0, g, 0] = 1.0 / (np.minimum(tt + hh, T) - np.maximum(tt - hh, 0))
        tt2 = T - 16 + np.arange(16)
        rc[0, g, 1] = 1.0 / (np.minimum(tt2 + hh, T) - np.maximum(tt2 - hh, 0))
        T = 4096
        tt = qi * 1024 + np.arange(16)
        rc[1, g, 0] = 1.0 / (np.minimum(tt + hh, T) - np.maximum(tt - hh, 0))
        tt2 = qi * 1024 + 1024 - 16 + np.arange(16)
        rc[1, g, 1] = 1.0 / (np.minimum(tt2 + hh, T) - np.maximum(tt2 - hh, 0))
    c['c_rc'] = np.broadcast_to(rc.astype(np.float32)[None], (128, 2, 4, 2, 16)).copy()
    return c


_NC_CACHE = {}


def kernel(x_prompt, x_sample, cache_k, cache_v, c, c_ctx, w_mod, b_mod, ln_g, ln_b,
           ffn_w_up, ffn_conv_w, ffn_conv_b, ffn_w_down, fnet_w_o, attn_w_qkv,
           attn_q_norm, attn_k_norm, attn_w_o, pool_w, pool_scale, gmlp_w_in, gmlp_b_in,
           gmlp_ln_g, gmlp_ln_b, gmlp_w_s, gmlp_b_s, gmlp_w_o):
    f = lambda a: np.ascontiguousarray(np.asarray(a, dtype=np.float32))
    x_prompt, x_sample, cache_k, cache_v = f(x_prompt), f(x_sample), f(cache_k), f(cache_v)
    c, c_ctx = f(c), f(c_ctx)
    shared = {
        'w_mod': f(w_mod), 'b_mod': f(b_mod), 'ln_g': f(ln_g), 'ln_b': f(ln_b),
        'ffn_w_up': f(ffn_w_up), 'ffn_w_down': f(ffn_w_down), 'fnet_w_o': f(fnet_w_o),
        'attn_w_qkv': f(attn_w_qkv), 'attn_w_o': f(attn_w_o), 'pool_w': f(pool_w), 'pool_scale': f(pool_scale),
        'gmlp_w_in': f(gmlp_w_in), 'gmlp_b_in': f(gmlp_b_in), 'gmlp_ln_g': f(gmlp_ln_g),
        'gmlp_b_s': f(np.asarray(gmlp_b_s).reshape(1, 512)), 'gmlp_w_o': f(gmlp_w_o),
    }
    shared['bmodT'] = f(np.asarray(b_mod).reshape(DEPTH, 48, 128).transpose(2, 0, 1))
    conv = np.concatenate([np.asarray(ffn_conv_w), np.asarray(ffn_conv_b)[:, None, :]], axis=1)
    shared['convT'] = f(conv.reshape(DEPTH, 4, NFF, 128).transpose(3, 0, 1, 2))
    shared['qkT'] = f(np.stack([np.asarray(attn_q_norm)[0], np.asarray(attn_k_norm)[0]], axis=1))
    shared['gbinT'] = f(np.asarray(gmlp_b_in)[0, :2048].reshape(16, 128).T)
    shared['glnbT'] = f(np.asarray(gmlp_ln_b)[0].reshape(16, 128).T)
    shared['wsT'] = f(np.asarray(gmlp_w_s)[0].transpose(2, 0, 1))
    shared.update(_consts())
    in_maps = []
    for core in range(8):
        b = core // 4
        qi = core % 4
        m = dict(shared)
        xp = x_prompt[4 * core:4 * core + 4].reshape(1024, D)
        xs = x_sample[b, qi * 1024:(qi + 1) * 1024]
        m['xin'] = np.ascontiguousarray(np.concatenate([xp, xs], axis=0))
        cond = np.stack([c_ctx, c[b]], axis=1)
        m['condT'] = f(cond.reshape(8, 128, 2).transpose(1, 0, 2))
        m['cache_k'] = f(cache_k[b, 0].reshape(512, 256))
        m['cache_v'] = f(cache_v[b, 0].reshape(512, 256))
        m.update(_core_consts(core))
        in_maps.append(m)
    if 'nc' not in _NC_CACHE:
        _NC_CACHE['nc'] = build_program()
    nc = _NC_CACHE['nc']
    res = run_bass_kernel_spmd(nc, in_maps, core_ids=list(range(8)))
    r = res.results
    y_prompt = np.zeros((32, 256, D), np.float32)
    y_sample = np.zeros((2, 4096, D), np.float32)
    new_k = np.zeros((32, 1, 256, 2, 128), np.float32)
    new_v = np.zeros((32, 1, 256, 2, 128), np.float32)
    for core in range(8):
        b = core // 4
        qi = core % 4
        y = r[core]['yout']
        y_prompt[4 * core:4 * core + 4] = y[:1024].reshape(4, 256, D)
        y_sample[b, qi * 1024:(qi + 1) * 1024] = y[1024:]
        new_k[4 * core:4 * core + 4, 0] = r[core]['nk_out'].reshape(4, 256, 2, 128)
        new_v[4 * core:4 * core + 4, 0] = r[core]['nv_out'].reshape(4, 256, 2, 128)
    return (y_prompt, y_sample, new_k, new_v)
```
